# Optimizing a Trainium2 kernel written in Bass

```python
import math
import jax
import jax.numpy as jnp
from jax import lax
import numpy as np

D_MODEL = 1024
BATCH = 8
SEQ = 2048
DEPTH = 4
DEC_BATCH = 4
DEC_SEQ = 4096
PAST_LEN = 128

D_MIX = D_MODEL
HEAD_DIM = 64
RWKV_WIDTH = D_MIX // 2
RWKV_HEADS = RWKV_WIDTH // HEAD_DIM
DECAY_LORA = 64
AAA_LORA = 64
GATE_LORA = 128
SSM_WIDTH = D_MIX - RWKV_WIDTH
SSM_HEADS = SSM_WIDTH // HEAD_DIM
SSM_GROUPS = 2
D_STATE = 128
CONV_WIDTH = 5
CHUNK = 128
D_FF = 4 * D_MODEL
N_DIR = 2
N_MOD = 6
NORM_EPS = 1e-6
GN_EPS = 64e-5

RWKV_COLS = 3 * RWKV_WIDTH + N_DIR * DECAY_LORA + N_DIR * AAA_LORA + GATE_LORA
CONV_COLS = SSM_WIDTH + 2 * SSM_GROUPS * D_STATE
SSM_COLS = SSM_WIDTH + CONV_COLS + N_DIR * SSM_HEADS
D_IN_PROJ = RWKV_COLS + SSM_COLS

kernel_name = "hymba_rwkv7_mamba2_biencoder"


def rmsnorm(x, gain):
    xf = x.astype(jnp.float32)
    y = xf * lax.rsqrt(jnp.mean(xf * xf, axis=-1, keepdims=True) + NORM_EPS)
    return (y * gain.astype(jnp.float32)).astype(x.dtype)


def split_heads(z):
    return z.reshape(z.shape[:-1] + (-1, HEAD_DIM))


def centred_token_shift(u, mu_prev, mu_next):
    prev = jnp.pad(u[:, :-1], ((0, 0), (1, 0), (0, 0)))
    nxt = jnp.pad(u[:, 1:], ((0, 0), (0, 1), (0, 0)))
    return u + mu_prev * (prev - u) + mu_next * (nxt - u)


def centred_depthwise_conv(u, w, b):
    half = CONV_WIDTH // 2
    out = lax.conv_general_dilated(
        u, w[:, None, :].astype(u.dtype), window_strides=(1,), padding=[(half, half)],
        dimension_numbers=("NWC", "WIO", "NWC"), feature_group_count=u.shape[-1])
    return out + b


def rwkv7_scan(r, w, k, v, kk, a, reverse):
    b, t, h, n = r.shape
    xs = tuple(jnp.moveaxis(z, 1, 0) for z in (r, w, k, v, kk, a))

    def step(S, inp):
        r_t, w_t, k_t, v_t, kk_t, a_t = inp
        sa = jnp.einsum("bhvk,bhk->bhv", S, kk_t)
        S = (S * w_t[:, :, None, :]
             - sa[..., None] * (kk_t * a_t)[:, :, None, :]
             + v_t[..., None] * k_t[:, :, None, :])
        return S, jnp.einsum("bhvk,bhk->bhv", S, r_t)

    S0 = jnp.zeros((b, h, n, n), jnp.float32)
    _, y = lax.scan(step, S0, xs, reverse=reverse)
    return jnp.moveaxis(y, 0, 1)


def rwkv7_mixer(u, mu, w0, w_up, a0, a_up, g_up, k_k, k_a, r_k, gn_w, gn_b):
    b, t, _ = u.shape
    u = centred_token_shift(u, mu[0], mu[1]).astype(jnp.float32)
    offs = [RWKV_WIDTH, 2 * RWKV_WIDTH, 3 * RWKV_WIDTH,
            3 * RWKV_WIDTH + N_DIR * DECAY_LORA,
            3 * RWKV_WIDTH + N_DIR * DECAY_LORA + N_DIR * AAA_LORA]
    r, k, v, w_lo, a_lo, g_lo = jnp.split(u, offs, axis=-1)
    w_lo = w_lo.reshape(b, t, N_DIR, DECAY_LORA)
    a_lo = a_lo.reshape(b, t, N_DIR, AAA_LORA)
    w_raw = w0 + jnp.einsum("btdr,drc->btdc", jnp.tanh(w_lo), w_up)
    decay = jnp.exp(-jnp.exp(-jax.nn.softplus(-w_raw) - 0.5))
    a = jax.nn.sigmoid(a0 + jnp.einsum("btdr,drc->btdc", a_lo, a_up))
    g = jax.nn.sigmoid(g_lo) @ g_up
    rh, vh = split_heads(r), split_heads(v)
    kk = split_heads(k * k_k)
    kk = kk / jnp.maximum(jnp.sqrt(jnp.sum(kk * kk, axis=-1, keepdims=True)), 1e-12)
    y = jnp.zeros_like(rh)
    for d in range(N_DIR):
        k_d = k * (1.0 + (a[:, :, d] - 1.0) * k_a)
        y = y + rwkv7_scan(rh, split_heads(decay[:, :, d]), split_heads(k_d), vh, kk,
                           split_heads(a[:, :, d]), reverse=(d == 1))
    mean = jnp.mean(y, axis=-1, keepdims=True)
    var = jnp.mean(jnp.square(y - mean), axis=-1, keepdims=True)
    y = ((y - mean) * lax.rsqrt(var + GN_EPS)).reshape(b, t, RWKV_WIDTH) * gn_w + gn_b
    bonus = jnp.sum(rh * split_heads(k) * r_k, axis=-1, keepdims=True) * vh
    y = y + bonus.reshape(b, t, RWKV_WIDTH)
    return y * g


def ssd_chunked(x, dt, A, B, C):
    x, dt, B, C = (z.astype(jnp.float32) for z in (x, dt, B, C))
    A = A.astype(jnp.float32)
    b, t, h, p = x.shape
    g, n = B.shape[-2:]
    e = h // g
    nc = t // CHUNK
    xc = (x * dt[..., None]).reshape(b, nc, CHUNK, g, e, p)
    Ac = jnp.transpose((dt * A).reshape(b, nc, CHUNK, g, e), (0, 3, 4, 1, 2))
    Bc = B.reshape(b, nc, CHUNK, g, n)
    Cc = C.reshape(b, nc, CHUNK, g, n)
    A_cum = jnp.cumsum(Ac, axis=-1)
    lower = jnp.tril(jnp.ones((CHUNK, CHUNK), dtype=bool))
    seg = A_cum[..., :, None] - A_cum[..., None, :]
    Lmat = jnp.exp(jnp.where(lower, seg, -jnp.inf))
    CB = jnp.einsum("bclgn,bcsgn->bgcls", Cc, Bc)
    y_diag = jnp.einsum("bgcls,bgecls,bcsgep->bclgep", CB, Lmat, xc)
    decay_states = jnp.exp(A_cum[..., -1:] - A_cum)
    states = jnp.einsum("bclgn,bgecl,bclgep->bcgepn", Bc, decay_states, xc)
    chunk_decay = jnp.exp(A_cum[..., -1])

    def step(S, inp):
        st, dec = inp
        return S * dec[..., None, None] + st, S

    S0 = jnp.zeros((b, g, e, p, n), jnp.float32)
    _, prev = lax.scan(step, S0, (jnp.moveaxis(states, 1, 0), jnp.moveaxis(chunk_decay, 3, 0)))
    prev = jnp.moveaxis(prev, 0, 1)
    y_off = jnp.einsum("bclgn,bcgepn,bgecl->bclgep", Cc, prev, jnp.exp(A_cum))
    return (y_diag + y_off).reshape(b, t, h, p)


def mamba2_mixer(u, conv_w, conv_b, dt_bias, A_log, d_skip, norm_w):
    b, t, _ = u.shape
    z, xBC, dt_raw = jnp.split(u, [SSM_WIDTH, SSM_WIDTH + CONV_COLS], axis=-1)
    xBC = jax.nn.silu(centred_depthwise_conv(xBC, conv_w, conv_b))
    xs, Bm, Cm = jnp.split(xBC, [SSM_WIDTH, SSM_WIDTH + SSM_GROUPS * D_STATE], axis=-1)
    xs = xs.reshape(b, t, SSM_HEADS, HEAD_DIM)
    Bm = Bm.reshape(b, t, SSM_GROUPS, D_STATE)
    Cm = Cm.reshape(b, t, SSM_GROUPS, D_STATE)
    dt_raw = dt_raw.astype(jnp.float32).reshape(b, t, N_DIR, SSM_HEADS)
    A = -jnp.exp(A_log.astype(jnp.float32))
    flip = lambda q: jnp.flip(q, axis=1)
    dt_f = jax.nn.softplus(dt_raw[:, :, 0] + dt_bias[0])
    dt_b = jax.nn.softplus(dt_raw[:, :, 1] + dt_bias[1])
    y = ssd_chunked(xs, dt_f, A[0], Bm, Cm)
    y = y + flip(ssd_chunked(flip(xs), flip(dt_b), A[1], flip(Bm), flip(Cm)))
    y = y + d_skip[:, None] * xs
    y = y.reshape(b, t, SSM_WIDTH)
    return rmsnorm(y * jax.nn.silu(z.astype(jnp.float32)), norm_w)


def encoder_trunk(x, c, weights):
    (w_mod, b_mod, norm_g, w_in, shift_mu, w0, w_up, a0, a_up, g_up, k_k, k_a, r_k,
     gn_w, gn_b, conv_w, conv_b, dt_bias, A_log, d_skip, ssm_norm_w, w_out, w_ff1, w_ff2) = weights
    nb = c.shape[0]
    for l in range(DEPTH):
        mod = (jax.nn.silu(c) @ w_mod[l] + b_mod[l]).reshape(nb, N_MOD, D_MODEL)[:, :, None, :]
        sh_m, sc_m, gt_m, sh_f, sc_f, gt_f = (mod[:, i] for i in range(N_MOD))
        h = rmsnorm(x, norm_g[l, 0]) * (1.0 + sc_m) + sh_m
        u = h @ w_in[l]
        y_r = rwkv7_mixer(u[..., :RWKV_COLS], shift_mu[l], w0[l], w_up[l], a0[l], a_up[l],
                          g_up[l], k_k[l], k_a[l], r_k[l], gn_w[l], gn_b[l])
        y_s = mamba2_mixer(u[..., RWKV_COLS:], conv_w[l], conv_b[l], dt_bias[l], A_log[l],
                           d_skip[l], ssm_norm_w[l])
        o = jnp.concatenate([y_r, y_s], axis=-1).astype(h.dtype) @ w_out[l]
        x = x + gt_m * rmsnorm(o, norm_g[l, 1])
        h = rmsnorm(x, norm_g[l, 2]) * (1.0 + sc_f) + sh_f
        f = jnp.square(jax.nn.relu(h @ w_ff1[l])) @ w_ff2[l]
        x = x + gt_f * rmsnorm(f, norm_g[l, 3])
    return x


def setup_inputs(seed: int = 0) -> dict:
    key = jax.random.key(seed)
    ks = jax.random.split(key, 28)
    nrm = lambda k, shape, s: s * jax.random.normal(k, shape, jnp.float32)
    L = DEPTH
    x_prompt = nrm(ks[0], (BATCH, SEQ, D_MODEL), 1.0)
    x_sample = nrm(ks[1], (DEC_BATCH, DEC_SEQ, D_MODEL), 1.0)
    c_prompt = nrm(ks[2], (BATCH, D_MODEL), 1.0)
    c_sample = nrm(ks[3], (DEC_BATCH, D_MODEL), 1.0)
    w_mod = nrm(ks[4], (L, D_MODEL, N_MOD * D_MODEL), 0.5 * D_MODEL ** -0.5)
    b_mod = nrm(ks[5], (L, N_MOD * D_MODEL), 0.02)
    norm_g = 1.0 + nrm(ks[6], (L, 4, D_MODEL), 0.05)
    w_in = nrm(ks[7], (L, D_MODEL, D_IN_PROJ), D_MODEL ** -0.5)
    shift_mu = jax.random.uniform(ks[8], (L, 2, RWKV_COLS), jnp.float32, 0.05, 0.45)
    decay_base = jnp.linspace(-6.0, -1.0, RWKV_WIDTH, dtype=jnp.float32)
    w0 = decay_base + nrm(ks[9], (L, N_DIR, RWKV_WIDTH), 0.1)
    w_up = nrm(ks[10], (L, N_DIR, DECAY_LORA, RWKV_WIDTH), 0.5 * DECAY_LORA ** -0.5)
    a0 = nrm(ks[11], (L, N_DIR, RWKV_WIDTH), 0.1)
    a_up = nrm(ks[12], (L, N_DIR, AAA_LORA, RWKV_WIDTH), 0.5 * AAA_LORA ** -0.5)
    g_up = nrm(ks[13], (L, GATE_LORA, RWKV_WIDTH), GATE_LORA ** -0.5)
    k_k = 0.85 + nrm(ks[14], (L, RWKV_WIDTH), 0.02)
    k_a = 1.0 + nrm(ks[15], (L, RWKV_WIDTH), 0.02)
    r_k = nrm(ks[16], (L, RWKV_HEADS, HEAD_DIM), 0.1)
    gn_w = 1.0 + nrm(ks[17], (L, RWKV_WIDTH), 0.05)
    gn_b = nrm(ks[18], (L, RWKV_WIDTH), 0.02)
    conv_w = nrm(ks[19], (L, CONV_WIDTH, CONV_COLS), CONV_WIDTH ** -0.5)
    conv_b = nrm(ks[20], (L, CONV_COLS), 0.02)
    dt0 = jnp.exp(jax.random.uniform(ks[21], (L, N_DIR, SSM_HEADS), jnp.float32,
                                     math.log(1e-3), math.log(1e-1)))
    dt_bias = dt0 + jnp.log(-jnp.expm1(-dt0))
    A_log = jnp.log(jax.random.uniform(ks[22], (L, N_DIR, SSM_HEADS), jnp.float32, 1.0, 16.0))
    d_skip = 1.0 + nrm(ks[23], (L, SSM_HEADS), 0.1)
    ssm_norm_w = 1.0 + nrm(ks[24], (L, SSM_WIDTH), 0.05)
    w_out = nrm(ks[25], (L, D_MIX, D_MODEL), D_MIX ** -0.5)
    w_ff1 = nrm(ks[26], (L, D_MODEL, D_FF), D_MODEL ** -0.5)
    w_ff2 = nrm(ks[27], (L, D_FF, D_MODEL), D_FF ** -0.5)
    return {"x_prompt": x_prompt, "x_sample": x_sample, "c_prompt": c_prompt, "c_sample": c_sample,
            "w_mod": w_mod, "b_mod": b_mod, "norm_g": norm_g, "w_in": w_in, "shift_mu": shift_mu,
            "w0": w0, "w_up": w_up, "a0": a0, "a_up": a_up, "g_up": g_up, "k_k": k_k, "k_a": k_a,
            "r_k": r_k, "gn_w": gn_w, "gn_b": gn_b, "conv_w": conv_w, "conv_b": conv_b,
            "dt_bias": dt_bias, "A_log": A_log, "d_skip": d_skip, "ssm_norm_w": ssm_norm_w,
            "w_out": w_out, "w_ff1": w_ff1, "w_ff2": w_ff2}


def reference(x_prompt, x_sample, c_prompt, c_sample, w_mod, b_mod, norm_g, w_in, shift_mu,
              w0, w_up, a0, a_up, g_up, k_k, k_a, r_k, gn_w, gn_b, conv_w, conv_b,
              dt_bias, A_log, d_skip, ssm_norm_w, w_out, w_ff1, w_ff2):
    weights = (w_mod, b_mod, norm_g, w_in, shift_mu, w0, w_up, a0, a_up, g_up, k_k, k_a, r_k,
               gn_w, gn_b, conv_w, conv_b, dt_bias, A_log, d_skip, ssm_norm_w, w_out, w_ff1, w_ff2)
    y_prompt = encoder_trunk(x_prompt, c_prompt, weights)
    y_sample = encoder_trunk(x_sample, c_sample, weights)
    return (y_prompt, y_sample)
```

```python
import contextlib
import numpy as np
import concourse.bass as bass
import concourse.mybir as mybir
from concourse.bass_utils import run_bass_kernel_spmd

F32 = mybir.dt.float32
BF16 = mybir.dt.bfloat16
AF = mybir.ActivationFunctionType
ALU = mybir.AluOpType

D = 1024
KD = D // 128
DFF = 4096
NMOD = 6
NORM_EPS = 1e-6


class Tk:
    __slots__ = ("w", "r", "dsem", "dcnt", "dkey", "name", "psum")

    def __init__(self, name=""):
        self.w = None
        self.r = {}
        self.dsem = None
        self.dcnt = 0
        self.name = name
        self.psum = False


class Ctx:
    def __init__(self, nc, es):
        self.nc = nc
        self.es = es
        self.E = {"pe": nc.tensor, "act": nc.scalar, "dve": nc.vector, "pool": nc.gpsimd, "sp": nc.sync}
        self.sem = {}
        self.cnt = {}
        for k in self.E:
            self.sem[k] = es.enter_context(nc.semaphore("c_" + k))
            self.cnt[k] = 0
        self.seen = {k: {} for k in self.E}
        self.nsem = 5
        self.ninst = 0
        self.mem_es = es
        self.sem_free = []
        self.owners = []
        self.gsems = []

    def _wait(self, eng, evs):
        need = {}
        for ev in evs:
            if ev is None:
                continue
            key, h, v = ev
            if eng == "pe" and key == "pe":
                continue
            if v > need.get(key, (None, 0))[1]:
                need[key] = (h, v)
        for key, (h, v) in need.items():
            if self.seen[eng].get(key, 0) >= v:
                continue
            self.E[eng].wait_ge(h, v)
            self.seen[eng][key] = v

    def _deps(self, reads, writes):
        evs = []
        for t in reads:
            evs.append(t.w)
            if t.psum:
                evs.extend(t.r.values())
        for t in writes:
            evs.append(t.w)
            evs.extend(t.r.values())
        return evs

    def _record(self, ev, reads, writes):
        key = ev[0]
        for t in reads:
            old = t.r.get(key)
            if old is None or old[2] < ev[2]:
                t.r[key] = ev
        for t in writes:
            t.w = ev
            t.r = {}

    def op(self, eng, fn, reads=(), writes=()):
        self._wait(eng, self._deps(reads, writes))
        ins = fn(self.E[eng])
        self.cnt[eng] += 1
        ins.then_inc(self.sem[eng], 1)
        ev = (eng, self.sem[eng], self.cnt[eng])
        self._record(ev, reads, writes)
        self.ninst += 1
        return ev

    def dma(self, q, pairs, reads, writes, owner, **kw):
        if q == "pool":
            assert len(pairs) == 1
            sem = self.es.enter_context(self.nc.semaphore("g%d" % self.nsem))
            key = "g%d" % self.nsem
            self.nsem += 1
            self._wait(q, self._deps(reads, writes))
            o, i = pairs[0]
            self.E[q].dma_start(out=o, in_=i, **kw).then_inc(sem, 16)
            self.ninst += 1
            ev = (key, sem, 16)
            self._record(ev, reads, writes)
            self.gsems.append(ev)
            return ev
        if owner.dsem is None:
            if self.sem_free:
                owner.dsem, owner.dcnt, owner.dkey = self.sem_free.pop()
            else:
                owner.dsem = self.es.enter_context(self.nc.semaphore("d%d" % self.nsem))
                owner.dkey = "d%d" % self.nsem
                self.nsem += 1
            self.owners.append(owner)
        key = owner.dkey
        evs = self._deps(reads, writes)
        if owner.dcnt:
            evs.append((key, owner.dsem, owner.dcnt))
        self._wait(q, evs)
        for (o, i) in pairs:
            self.E[q].dma_start(out=o, in_=i, **kw).then_inc(owner.dsem, 16)
            owner.dcnt += 16
            self.ninst += 1
        ev = (key, owner.dsem, owner.dcnt)
        self._record(ev, reads, writes)
        return ev

    def barrier(self):
        for eng in self.E:
            evs = [(k, self.sem[k], self.cnt[k]) for k in self.E if self.cnt[k] > 0]
            evs += [(o.dkey, o.dsem, o.dcnt) for o in self.owners if o.dcnt > 0]
            evs += self.gsems
            self._wait(eng, evs)

    def end_phase(self, first_owner):
        self.barrier()
        rel = self.owners[first_owner:]
        self.owners = self.owners[:first_owner]
        for o in rel:
            self.sem_free.append((o.dsem, o.dcnt, o.dkey))
            o.dsem = None

    def wait_all(self, eng, tks):
        evs = []
        for t in tks:
            evs.append(t.w)
            evs.extend(t.r.values())
        self._wait(eng, evs)


class Buf:
    _n = [0]

    def __init__(self, cx, name, shape, dtype, psum=False):
        Buf._n[0] += 1
        name = "%s_%d" % (name, Buf._n[0])
        if psum:
            self.t = cx.mem_es.enter_context(cx.nc.psum_tensor(name, shape, dtype))
        else:
            self.t = cx.mem_es.enter_context(cx.nc.sbuf_tensor(name, shape, dtype))
        self.k = Tk(name)
        self.k.psum = psum

    def __getitem__(self, idx):
        return self.t[idx]


def build(NT=4096, depth=4, do_mix=True, do_ffn=True, debug=False):
    SEG = NT // 2
    nc = bass.Bass("TRN2", target_bir_lowering=False)
    dram_in = {}

    def din(name, shape, dt=F32):
        dram_in[name] = nc.dram_tensor(name, list(shape), dt, kind="ExternalInput").ap()
        return dram_in[name]

    xT = din("xT", [KD, 128, NT])
    cT = din("cT", [128, KD, 2])
    flag = din("flag", [128, 1])
    w_mod = din("w_mod", [depth, 128, KD, NMOD * D])
    b_mod = din("b_mod", [depth, 128, NMOD * KD])
    norm_g = din("norm_g", [depth, 128, 4, KD])
    w_ff1 = din("w_ff1", [depth, 128, KD * DFF])
    w_ff2 = din("w_ff2", [depth, 128, (DFF // 128) * D])
    c_ones = din("c_ones", [128, 128])
    c_ident = din("c_ident", [128, 128])
    c_masks = din("c_masks", [128, 4, 128])
    c_blk1 = din("c_blk1", [128, 128])
    c_bmask = din("c_bmask", [128, 4, 128])
    c_rmask = din("c_rmask", [128, 512])
    NPP, NBB, WIN = 114, 1056, 3584
    w_in = din("w_in", [depth, 128, KD * WIN])
    w_out = din("w_out", [depth, 128, KD * D])
    pp_in = din("pp", [depth, 128, NPP])
    bb_in = din("bb", [depth, NBB])
    wup_in = din("wup", [depth, 128, 512])
    aup_in = din("aup", [depth, 128, 512])
    gup_in = din("gup", [depth, 128, 512])
    NFC = 23
    NC = NT // 128
    uT = nc.dram_tensor("uT", [NFC, 128, NT], F32, kind="Internal").ap()
    z_tok = nc.dram_tensor("z_tok", [NT, 512], F32, kind="Internal").ap()
    dt_tok = nc.dram_tensor("dt_tok", [NT, 16], F32, kind="Internal").ap()
    ysacc = nc.dram_tensor("ysacc", [NT, 512], F32, kind="Internal").ap()
    yracc = nc.dram_tensor("yracc", [4, 128, NT], F32, kind="Internal").ap()
    ymix_d = nc.dram_tensor("ymix_d", [KD, 128, NT], BF16, kind="Internal").ap()
    dbg_out = nc.dram_tensor("dbg_out", [KD, 128, NT], F32, kind="ExternalOutput").ap() if debug else None
    yT = nc.dram_tensor("yT", [KD, 128, NT], F32, kind="ExternalOutput").ap()
    xres = nc.dram_tensor("xres", [KD, 128, NT], F32, kind="Internal").ap()

    with contextlib.ExitStack() as es:
        cx = Ctx(nc, es)
        ones_f = Buf(cx, "ones_f", [128, 128], F32)
        ones_b = Buf(cx, "ones_b", [128, 128], BF16)
        eps_t = Buf(cx, "eps_t", [128, 1], F32)
        sc_t = Buf(cx, "sc_t", [128, KD, 2], F32)
        flag_t = Buf(cx, "flag_t", [128, 1], F32)
        ident_f = Buf(cx, "ident_f", [128, 128], F32)
        ident_b = Buf(cx, "ident_b", [128, 128], BF16)
        mk_f = Buf(cx, "mk_f", [128, 4, 128], F32)
        one_t = Buf(cx, "one_t", [128, 1], F32)
        cx.dma("sp", [(ones_f[:], c_ones), (sc_t[:], cT), (flag_t[:], flag), (ident_f[:], c_ident), (mk_f[:], c_masks)], [],
               [ones_f.k, sc_t.k, flag_t.k, ident_f.k, mk_f.k], ones_f.k)
        cx.op("dve", lambda e: e.tensor_copy(out=ident_b[:], in_=ident_f[:]), [ident_f.k], [ident_b.k])
        blk1_f = Buf(cx, "blk1_f", [128, 128], F32)
        blk1_b = Buf(cx, "blk1_b", [128, 128], BF16)
        blk64_b = Buf(cx, "blk64_b", [128, 128], BF16)
        rmask = Buf(cx, "rmask", [128, 512], F32)
        gneps_t = Buf(cx, "gneps_t", [128, 1], F32)
        bm_f = Buf(cx, "bm_f", [128, 4, 128], F32)
        bm_b = Buf(cx, "bm_b", [128, 4, 128], BF16)
        cx.dma("sp", [(blk1_f[:], c_blk1), (rmask[:], c_rmask), (bm_f[:], c_bmask)], [], [blk1_f.k, rmask.k, bm_f.k], blk1_f.k)
        cx.op("dve", lambda e: e.tensor_copy(out=bm_b[:], in_=bm_f[:]), [bm_f.k], [bm_b.k])
        cx.op("dve", lambda e: e.tensor_copy(out=blk1_b[:], in_=blk1_f[:]), [blk1_f.k], [blk1_b.k])
        cx.op("dve", lambda e: e.tensor_scalar(out=blk64_b[:], in0=blk1_f[:], scalar1=1.0 / 64, scalar2=None, op0=ALU.mult), [blk1_f.k], [blk64_b.k])
        cx.op("dve", lambda e: e.memset(gneps_t[:], 64e-5), [], [gneps_t.k])
        cx.op("dve", lambda e: e.memset(one_t[:], 1.0), [], [one_t.k])
        cx.op("dve", lambda e: e.tensor_copy(out=ones_b[:], in_=ones_f[:]), [ones_f.k], [ones_b.k])
        cx.op("dve", lambda e: e.memset(eps_t[:], NORM_EPS), [], [eps_t.k])
        cx.op("act", lambda e: e.activation(out=sc_t[:], in_=sc_t[:], func=AF.Silu), [sc_t.k], [sc_t.k])

        import os
        DBG = os.environ.get("DBG_STOP", "")
        PS = [Buf(cx, "ps%d" % i, [128, 512], F32, psum=True) for i in range(7)]
        PT = Buf(cx, "pt", [128, 1024], BF16, psum=True)

        modT = Buf(cx, "modT", [128, NMOD * KD, 2], F32)
        bmod_t = Buf(cx, "bmod_t", [128, NMOD * KD], F32)
        ng_t = Buf(cx, "ng_t", [128, 4, KD], F32)
        A1 = Buf(cx, "A1", [128, KD, 2], F32)
        G1 = Buf(cx, "G1", [128, KD, 2], F32)
        A2 = Buf(cx, "A2", [128, KD, 2], F32)
        G2 = Buf(cx, "G2", [128, KD, 2], F32)

        def mod_phase(l):
            wm = [Buf(cx, "wm%d" % i, [128, KD, 512], F32) for i in range(2)]
            cx.dma("sp", [(bmod_t[:], b_mod[l]), (ng_t[:], norm_g[l])], [], [bmod_t.k, ng_t.k], bmod_t.k)
            ps = PS[6]
            if DBG == "const":
                return
            for j in range(NMOD * D // 512):
                wb = wm[j % 2]
                cx.dma("sp", [(wb[:], w_mod[l][:, :, j * 512:(j + 1) * 512])], [], [wb.k], wb.k)
                for cc in range(4):
                    col = j * 4 + cc
                    for k in range(KD):
                        cx.op("pe", lambda e, k=k, cc=cc, col=col, wb=wb: e.matmul(
                            ps[:, col * 2:col * 2 + 2], lhsT=wb[:, k, cc * 128:(cc + 1) * 128], rhs=sc_t[:, k, :],
                            start=(k == 0), stop=(k == KD - 1)), [wb.k, sc_t.k], [ps.k])
            if DBG == "modmm":
                return
            cx.op("dve", lambda e: e.tensor_tensor(
                out=modT[:], in0=ps[:, 0:NMOD * KD * 2].rearrange("p (c s) -> p c s", s=2),
                in1=bmod_t[:].unsqueeze(2).to_broadcast([128, NMOD * KD, 2]), op=ALU.add),
                [ps.k, bmod_t.k], [modT.k])

            def mk(dst, mi, gi, plus1):
                gb = ng_t[:, gi, :].unsqueeze(2).to_broadcast([128, KD, 2])
                src = modT[:, mi * KD:(mi + 1) * KD, :]
                if plus1:
                    cx.op("dve", lambda e: e.tensor_scalar(out=dst[:], in0=src, scalar1=1.0, scalar2=None, op0=ALU.add),
                          [modT.k], [dst.k])
                    src = dst[:]
                cx.op("dve", lambda e: e.tensor_tensor(out=dst[:], in0=src, in1=gb, op=ALU.mult), [modT.k, ng_t.k, dst.k], [dst.k])
            if DBG == "modtt":
                return
            mk(A1, 1, 0, True)
            mk(G1, 2, 1, False)
            mk(A2, 4, 2, True)
            mk(G2, 5, 3, False)

        TF = 256
        NKF = DFF // 128

        def ffn_phase(l, src, dst):
            w1b = Buf(cx, "w1b", [128, KD, DFF], BF16)
            w2b = Buf(cx, "w2b", [128, NKF, D], BF16)
            xt = [Buf(cx, "f_xt%d" % i, [128, KD, TF], F32) for i in range(2)]
            sq = Buf(cx, "f_sq", [128, KD, TF], BF16)
            std = Buf(cx, "f_std", [128, TF], F32)
            rstd = Buf(cx, "f_rstd", [128, TF], F32)
            tmp = Buf(cx, "f_tmp", [128, KD, TF], F32)
            hT = Buf(cx, "f_hT", [128, KD, TF], BF16)
            rl = [Buf(cx, "f_rl%d" % i, [128, TF], F32) for i in range(4)]
            aT = Buf(cx, "f_aT", [128, NKF, TF], BF16)
            fT = Buf(cx, "f_fT", [128, KD, TF], F32)
            w1v = w_ff1[l].rearrange("p (a b) -> p a b", b=2048)
            w2v = w_ff2[l].rearrange("p (a b) -> p a b", b=2048)
            cx.dma("pool", [(w1b[:].rearrange("p k c -> p (k c)").rearrange("p (a b) -> p a b", b=2048), w1v)],
                   [], [w1b.k], w1b.k)
            cx.dma("pool", [(w2b[:].rearrange("p k c -> p (k c)").rearrange("p (a b) -> p a b", b=2048), w2v)],
                   [], [w2b.k], w2b.k)
            dk = [Tk("ffn_dst%d" % i) for i in range(NT // TF)]
            if DBG == "ffn_w":
                return dk
            for ti in range(NT // TF):
                t0 = ti * TF
                seg = t0 // SEG
                x = xt[ti % 2]
                cx.dma("sp", [(x[:], src[:, :, t0:t0 + TF].rearrange("k p t -> p k t"))], [], [x.k], x.k)
                cx.op("act", lambda e: e.activation(out=sq[:], in_=x[:], func=AF.Square), [x.k], [sq.k])
                ps = PS[0]
                for k in range(KD):
                    cx.op("pe", lambda e, k=k: e.matmul(ps[:, 0:TF], lhsT=ones_b[:], rhs=sq[:, k, :],
                                                        start=(k == 0), stop=(k == KD - 1)), [ones_b.k, sq.k], [ps.k])
                cx.op("act", lambda e: e.activation(out=std[:], in_=ps[:, 0:TF], func=AF.Sqrt, bias=eps_t[:], scale=1.0 / D),
                      [ps.k, eps_t.k], [std.k])
                cx.op("dve", lambda e: e.reciprocal(out=rstd[:], in_=std[:]), [std.k], [rstd.k])
                if DBG == "ffn_rstd":
                    return dk
                cx.op("dve", lambda e: e.tensor_tensor(out=tmp[:], in0=x[:], in1=rstd[:].unsqueeze(1).to_broadcast([128, KD, TF]),
                                                       op=ALU.mult), [x.k, rstd.k], [tmp.k])
                if DBG == "ffn_tmp":
                    return dk
                for k in range(KD):
                    eng = "act" if k % 2 == 0 else "pool"
                    if eng == "act":
                        cx.op("act", lambda e, k=k: e.activation(out=hT[:, k, :], in_=tmp[:, k, :], func=AF.Identity,
                                                                 bias=modT[:, 3 * KD + k, seg:seg + 1], scale=A2[:, k, seg:seg + 1]),
                              [tmp.k, modT.k, A2.k], [hT.k])
                    else:
                        cx.op("pool", lambda e, k=k: e.tensor_scalar(out=hT[:, k, :], in0=tmp[:, k, :],
                                                                     scalar1=A2[:, k, seg:seg + 1], scalar2=modT[:, 3 * KD + k, seg:seg + 1],
                                                                     op0=ALU.mult, op1=ALU.add), [tmp.k, modT.k, A2.k], [hT.k])
                if DBG == "ffn_h":
                    return dk
                for c in range(NKF):
                    ps = PS[1 + c % 4]
                    r = rl[c % 4]
                    for k in range(KD):
                        cx.op("pe", lambda e, k=k, c=c, ps=ps: e.matmul(ps[:, 0:TF], lhsT=w1b[:, k, c * 128:(c + 1) * 128], rhs=hT[:, k, :],
                                                                       start=(k == 0), stop=(k == KD - 1)), [w1b.k, hT.k], [ps.k])
                    cx.op("act", lambda e, ps=ps, r=r: e.activation(out=r[:], in_=ps[:, 0:TF], func=AF.Relu), [ps.k], [r.k])
                    if c % 2 == 0:
                        cx.op("dve", lambda e, ps=ps, r=r, c=c: e.tensor_tensor(out=aT[:, c, :], in0=r[:], in1=ps[:, 0:TF], op=ALU.mult),
                              [r.k, ps.k], [aT.k])
                    else:
                        cx.op("pool", lambda e, r=r, c=c: e.tensor_tensor(out=aT[:, c, :], in0=r[:], in1=r[:], op=ALU.mult),
                              [r.k], [aT.k])
                if DBG == "ffn_1":
                    return dk
                for o in range(KD):
                    ps = PS[5 + o % 2]
                    for k in range(NKF):
                        cx.op("pe", lambda e, k=k, o=o, ps=ps: e.matmul(ps[:, 0:TF], lhsT=w2b[:, k, o * 128:(o + 1) * 128], rhs=aT[:, k, :],
                                                                       start=(k == 0), stop=(k == NKF - 1)), [w2b.k, aT.k], [ps.k])
                    if DBG != "ffn_2a":
                        cx.op("act", lambda e, ps=ps, o=o: e.activation(out=sq[:, o, :], in_=ps[:, 0:TF], func=AF.Square), [ps.k], [sq.k])
                    if DBG != "ffn_2b":
                        cx.op("dve", lambda e, ps=ps, o=o: e.tensor_copy(out=fT[:, o, :], in_=ps[:, 0:TF]), [ps.k], [fT.k])
                if DBG in ("ffn_2", "ffn_2a", "ffn_2b"):
                    return dk
                ps = PS[0]
                for k in range(KD):
                    cx.op("pe", lambda e, k=k: e.matmul(ps[:, 0:TF], lhsT=ones_b[:], rhs=sq[:, k, :],
                                                        start=(k == 0), stop=(k == KD - 1)), [ones_b.k, sq.k], [ps.k])
                cx.op("act", lambda e: e.activation(out=std[:], in_=ps[:, 0:TF], func=AF.Sqrt, bias=eps_t[:], scale=1.0 / D),
                      [ps.k, eps_t.k], [std.k])
                cx.op("dve", lambda e: e.reciprocal(out=rstd[:], in_=std[:]), [std.k], [rstd.k])
                cx.op("dve", lambda e: e.tensor_tensor(out=fT[:], in0=fT[:], in1=rstd[:].unsqueeze(1).to_broadcast([128, KD, TF]),
                                                       op=ALU.mult), [fT.k, rstd.k], [fT.k])
                for k in range(KD):
                    if k % 2 == 0:
                        cx.op("act", lambda e, k=k: e.activation(out=fT[:, k, :], in_=fT[:, k, :], func=AF.Identity,
                                                                 scale=G2[:, k, seg:seg + 1]), [fT.k, G2.k], [fT.k])
                    else:
                        cx.op("pool", lambda e, k=k: e.tensor_scalar(out=fT[:, k, :], in0=fT[:, k, :], scalar1=G2[:, k, seg:seg + 1],
                                                                     scalar2=None, op0=ALU.mult), [fT.k, G2.k], [fT.k])
                cx.op("dve", lambda e: e.tensor_tensor(out=x[:], in0=x[:], in1=fT[:], op=ALU.add), [x.k, fT.k], [x.k])
                if DBG == "ffn_tail":
                    return dk
                cx.dma("sp", [(dst[:, :, t0:t0 + TF].rearrange("k p t -> p k t"), x[:])], [x.k], [dk[ti]], x.k)
            return dk

        TM = 512
        V4 = lambda ap, a: ap.rearrange("p (a b) -> p a b", a=a)

        def norm_tile(x, seg, TT, sq, std, rstd, tmp, hT, Amul, boff):
            cx.op("act", lambda e: e.activation(out=sq[:], in_=x[:], func=AF.Square), [x.k], [sq.k])
            ps = PS[0]
            for k in range(KD):
                cx.op("pe", lambda e, k=k: e.matmul(ps[:, 0:TT], lhsT=ones_b[:], rhs=sq[:, k, :],
                                                    start=(k == 0), stop=(k == KD - 1)), [ones_b.k, sq.k], [ps.k])
            cx.op("act", lambda e: e.activation(out=std[:], in_=ps[:, 0:TT], func=AF.Sqrt, bias=eps_t[:], scale=1.0 / D),
                  [ps.k, eps_t.k], [std.k])
            cx.op("dve", lambda e: e.reciprocal(out=rstd[:], in_=std[:]), [std.k], [rstd.k])
            cx.op("dve", lambda e: e.tensor_tensor(out=tmp[:], in0=x[:], in1=rstd[:].unsqueeze(1).to_broadcast([128, KD, TT]),
                                                   op=ALU.mult), [x.k, rstd.k], [tmp.k])
            for k in range(KD):
                if k % 2 == 0:
                    cx.op("act", lambda e, k=k: e.activation(out=hT[:, k, :], in_=tmp[:, k, :], func=AF.Identity,
                                                             bias=modT[:, boff * KD + k, seg:seg + 1], scale=Amul[:, k, seg:seg + 1]),
                          [tmp.k, modT.k, Amul.k], [hT.k])
                else:
                    cx.op("pool", lambda e, k=k: e.tensor_scalar(out=hT[:, k, :], in0=tmp[:, k, :],
                                                                 scalar1=Amul[:, k, seg:seg + 1], scalar2=modT[:, boff * KD + k, seg:seg + 1],
                                                                 op0=ALU.mult, op1=ALU.add), [tmp.k, modT.k, Amul.k], [hT.k])

        def post_tile(x, seg, TT, psrc_fn, nk, sq, std, rstd, fT, Gmul):
            for o in range(KD):
                ps = psrc_fn(o)
                cx.op("act", lambda e, ps=ps, o=o: e.activation(out=sq[:, o, :], in_=ps[:, 0:TT], func=AF.Square), [ps.k], [sq.k])
                cx.op("dve", lambda e, ps=ps, o=o: e.tensor_copy(out=fT[:, o, :], in_=ps[:, 0:TT]), [ps.k], [fT.k])
            ps = PS[0]
            for k in range(KD):
                cx.op("pe", lambda e, k=k: e.matmul(ps[:, 0:TT], lhsT=ones_b[:], rhs=sq[:, k, :],
                                                    start=(k == 0), stop=(k == KD - 1)), [ones_b.k, sq.k], [ps.k])
            cx.op("act", lambda e: e.activation(out=std[:], in_=ps[:, 0:TT], func=AF.Sqrt, bias=eps_t[:], scale=1.0 / D),
                  [ps.k, eps_t.k], [std.k])
            cx.op("dve", lambda e: e.reciprocal(out=rstd[:], in_=std[:]), [std.k], [rstd.k])
            cx.op("dve", lambda e: e.tensor_tensor(out=fT[:], in0=fT[:], in1=rstd[:].unsqueeze(1).to_broadcast([128, KD, TT]),
                                                   op=ALU.mult), [fT.k, rstd.k], [fT.k])
            for k in range(KD):
                if k % 2 == 0:
                    cx.op("act", lambda e, k=k: e.activation(out=fT[:, k, :], in_=fT[:, k, :], func=AF.Identity,
                                                             scale=Gmul[:, k, seg:seg + 1]), [fT.k, Gmul.k], [fT.k])
                else:
                    cx.op("pool", lambda e, k=k: e.tensor_scalar(out=fT[:, k, :], in0=fT[:, k, :], scalar1=Gmul[:, k, seg:seg + 1],
                                                                 scalar2=None, op0=ALU.mult), [fT.k, Gmul.k], [fT.k])
            cx.op("dve", lambda e: e.tensor_tensor(out=x[:], in0=x[:], in1=fT[:], op=ALU.add), [x.k, fT.k], [x.k])

        class MixState:
            pass

        def mix_load_params(l, ms):
            ms.pp = Buf(cx, "pp", [128, NPP], F32)
            ms.bb = Buf(cx, "bb", [128, NBB], F32)
            cx.dma("sp", [(ms.pp[:], pp_in[l]), (ms.bb[:], bb_in[l].partition_broadcast(128))], [], [ms.pp.k, ms.bb.k], ms.pp.k)
            ms.aneg = Buf(cx, "aneg", [128, 16], F32)
            cx.op("act", lambda e: e.activation(out=ms.aneg[:], in_=ms.bb[:, 16:32], func=AF.Exp), [ms.bb.k], [ms.aneg.k])
            cx.op("dve", lambda e: e.tensor_scalar(out=ms.aneg[:], in0=ms.aneg[:], scalar1=-1.0, scalar2=None, op0=ALU.mult),
                  [ms.aneg.k], [ms.aneg.k])
            ms.ymk = [[Tk("ymk%d_%d" % (h, c)) for c in range(NC)] for h in range(2)]
            ms.uk = [[Tk("uT%d_%d" % (fc, ti)) for ti in range(NT // TM)] for fc in range(NFC)]
            ms.zk = [Tk("z%d" % c) for c in range(NC)]
            ms.dtk = [Tk("dt%d" % c) for c in range(NC)]

        def m1_inproj(l, src, ms):
            winb = Buf(cx, "winb", [128, KD, WIN], BF16)
            cx.dma("pool", [(winb[:].rearrange("p k c -> p (k c)").rearrange("p (a b) -> p a b", b=1792),
                             w_in[l].rearrange("p (a b) -> p a b", b=1792))], [], [winb.k], winb.k)
            xt = [Buf(cx, "m1_xt%d" % i, [128, KD, TM], F32) for i in range(2)]
            sq = Buf(cx, "m1_sq", [128, KD, TM], BF16)
            std = Buf(cx, "m1_std", [128, TM], F32)
            rstd = Buf(cx, "m1_rstd", [128, TM], F32)
            tmp = Buf(cx, "m1_tmp", [128, KD, TM], F32)
            hT = Buf(cx, "m1_hT", [128, KD, TM], BF16)
            stg = [Buf(cx, "m1_stg%d" % i, [128, TM], F32) for i in range(4)]
            stz = [Buf(cx, "m1_stz%d" % i, [128, 512 + 16], F32) for i in range(2)]
            ns = 0
            for ti in range(NT // TM):
                t0 = ti * TM
                seg = t0 // SEG
                x = xt[ti % 2]
                cx.dma("sp", [(x[:], src[:, :, t0:t0 + TM].rearrange("k p t -> p k t"))], [], [x.k], x.k)
                norm_tile(x, seg, TM, sq, std, rstd, tmp, hT, A1, 0)
                for fc in range(NFC):
                    ps = PS[1 + fc % 4]
                    for k in range(KD):
                        cx.op("pe", lambda e, k=k, fc=fc, ps=ps: e.matmul(ps[:, 0:TM], lhsT=winb[:, k, fc * 128:(fc + 1) * 128], rhs=hT[:, k, :],
                                                                         start=(k == 0), stop=(k == KD - 1)), [winb.k, hT.k], [ps.k])
                    st = stg[ns % 4]
                    ns += 1
                    if fc % 2 == 0:
                        cx.op("act", lambda e, ps=ps, st=st: e.activation(out=st[:], in_=ps[:, 0:TM], func=AF.Identity), [ps.k], [st.k])
                    else:
                        cx.op("dve", lambda e, ps=ps, st=st: e.tensor_copy(out=st[:], in_=ps[:, 0:TM]), [ps.k], [st.k])
                    cx.dma("sp", [(uT[fc, :, t0:t0 + TM], st[:])], [st.k], [ms.uk[fc][ti]], st.k)
                for sub in range(TM // 128):
                    c = (t0 // 128) + sub
                    psz = PS[5]
                    psd = PS[6]
                    for k in range(KD):
                        cx.op("pe", lambda e, k=k: e.matmul(psz[:, 0:512], lhsT=hT[:, k, sub * 128:(sub + 1) * 128], rhs=winb[:, k, 2944:3456],
                                                            start=(k == 0), stop=(k == KD - 1)), [winb.k, hT.k], [psz.k])
                    for k in range(KD):
                        cx.op("pe", lambda e, k=k: e.matmul(psd[:, 0:16], lhsT=hT[:, k, sub * 128:(sub + 1) * 128], rhs=winb[:, k, 3456:3472],
                                                            start=(k == 0), stop=(k == KD - 1)), [winb.k, hT.k], [psd.k])
                    sz = stz[c % 2]
                    cx.op("act", lambda e: e.activation(out=sz[:, 0:512], in_=psz[:, 0:512], func=AF.Identity), [psz.k], [sz.k])
                    cx.op("dve", lambda e: e.tensor_copy(out=sz[:, 512:528], in_=psd[:, 0:16]), [psd.k, sz.k], [sz.k])
                    cx.dma("sp", [(z_tok[c * 128:(c + 1) * 128, :], sz[:, 0:512]), (dt_tok[c * 128:(c + 1) * 128, :], sz[:, 512:528])],
                           [sz.k], [ms.zk[c], ms.dtk[c]], sz.k)

        def load_win(win, fc, ti, halo, ms):
            t0 = ti * TM
            lo, hi = max(t0 - halo, 0), min(t0 + TM + halo, NT)
            rk = [ms.uk[fc][ti]]
            if ti > 0:
                rk.append(ms.uk[fc][ti - 1])
            if ti < NT // TM - 1:
                rk.append(ms.uk[fc][ti + 1])
            cx.dma("sp", [(win[:, lo - (t0 - halo):hi - (t0 - halo)], uT[fc, :, lo:hi])], rk, [win.k], win.k)
            W = TM + 2 * halo
            if t0 == 0:
                cx.op("pool", lambda e: e.memset(win[:, 0:halo], 0.0), [], [win.k])
            if t0 + TM == NT:
                cx.op("pool", lambda e: e.memset(win[:, W - halo:W], 0.0), [], [win.k])
            if t0 == SEG:
                cx.op("pool", lambda e: e.tensor_scalar(out=win[:, 0:halo], in0=win[:, 0:halo], scalar1=flag_t[:, 0:1], scalar2=None,
                                                        op0=ALU.mult), [win.k, flag_t.k], [win.k])
            if t0 + TM == SEG:
                cx.op("pool", lambda e: e.tensor_scalar(out=win[:, W - halo:W], in0=win[:, W - halo:W], scalar1=flag_t[:, 0:1], scalar2=None,
                                                        op0=ALU.mult), [win.k, flag_t.k], [win.k])

        def m2_ssdprep(l, ms):
            ms.xs_tok = Buf(cx, "xs_tok", [128, NC, 512], BF16)
            ms.B_tok = Buf(cx, "B_tok", [128, NC, 256], BF16)
            ms.BT = Buf(cx, "BT", [128, 2, NT], BF16)
            ms.CT = Buf(cx, "CT", [128, 2, NT], BF16)
            ms.sk = [Tk("ssd%d" % c) for c in range(NC)]
            win = [Buf(cx, "m2_win%d" % i, [128, TM + 4], F32) for i in range(2)]
            acc = [Buf(cx, "m2_acc%d" % i, [128, TM], F32) for i in range(2)]
            cvo = [Buf(cx, "m2_cvo%d" % i, [128, TM], BF16) for i in range(2)]
            n = 0
            for ti in range(NT // TM):
                t0 = ti * TM
                cks = [ms.sk[t0 // 128 + j] for j in range(TM // 128)]
                for j8 in range(8):
                    fc = 15 + j8
                    w = win[n % 2]
                    a = acc[n % 2]
                    o = cvo[n % 2]
                    n += 1
                    load_win(w, fc, ti, 2, ms)
                    cx.op("pool", lambda e: e.tensor_scalar(out=a[:], in0=w[:, 0:TM], scalar1=ms.pp[:, j8 * 5:j8 * 5 + 1], scalar2=None,
                                                            op0=ALU.mult), [w.k, ms.pp.k], [a.k])
                    for j in range(1, 5):
                        cx.op("dve", lambda e, j=j: e.scalar_tensor_tensor(out=a[:], in0=w[:, j:j + TM], scalar=ms.pp[:, j8 * 5 + j:j8 * 5 + j + 1],
                                                                           in1=a[:], op0=ALU.mult, op1=ALU.add), [w.k, ms.pp.k, a.k], [a.k])
                    if j8 < 4 or j8 < 6:
                        cx.op("act", lambda e: e.activation(out=o[:], in_=a[:], func=AF.Silu, bias=ms.pp[:, 40 + j8:41 + j8]), [a.k, ms.pp.k], [o.k])
                        for sub in range(TM // 128):
                            cx.op("pe", lambda e, sub=sub: e.transpose(out=PT[:, sub * 128:(sub + 1) * 128], in_=o[:, sub * 128:(sub + 1) * 128],
                                                                      identity=ident_b[:]), [o.k, ident_b.k], [PT.k])
                        c0 = t0 // 128
                        if j8 < 4:
                            dstv = ms.xs_tok[:, c0:c0 + 4, j8 * 128:(j8 + 1) * 128]
                        else:
                            dstv = ms.B_tok[:, c0:c0 + 4, (j8 - 4) * 128:(j8 - 3) * 128]
                        cx.op("dve", lambda e: e.tensor_copy(out=dstv, in_=V4(PT[:, 0:512], 4)), [PT.k], cks)
                        if j8 >= 4:
                            cx.op("pool", lambda e: e.tensor_copy(out=ms.BT[:, j8 - 4, t0:t0 + TM], in_=o[:]), [o.k], cks)
                    else:
                        cx.op("act", lambda e: e.activation(out=ms.CT[:, j8 - 6, t0:t0 + TM], in_=a[:], func=AF.Silu, bias=ms.pp[:, 40 + j8:41 + j8]),
                              [a.k, ms.pp.k], cks)

        def m3_ssd(l, ms):
            dtt = [Buf(cx, "m3_dtt%d" % i, [128, 16], F32) for i in range(2)]
            zt = [Buf(cx, "m3_zt%d" % i, [128, 512], F32) for i in range(2)]
            yprev = [Buf(cx, "m3_yp%d" % i, [128, 512], F32) for i in range(2)]
            t1 = Buf(cx, "m3_t1", [128, 8], F32)
            dts = Buf(cx, "m3_dts", [128, 8], F32)
            a_t = Buf(cx, "m3_a", [128, 8], F32)
            lmh = Buf(cx, "m3_lmh", [128, 8, 128], F32)
            ex = Buf(cx, "m3_ex", [128, 24], F32)
            Lm = Buf(cx, "m3_Lm", [128, 8, 128], F32)
            CBm = Buf(cx, "m3_CBm", [128, 2, 128], F32)
            G = Buf(cx, "m3_G", [128, 8, 128], BF16)
            xdt = Buf(cx, "m3_xdt", [128, 8, 64], BF16)
            xw = Buf(cx, "m3_xw", [128, 8, 64], BF16)
            ytmp = Buf(cx, "m3_ytmp", [128, 512], F32)
            ycur = [Buf(cx, "m3_ycur%d" % i, [128, 512], F32) for i in range(2)]
            S = Buf(cx, "m3_S", [128, 512], F32)
            Sb = Buf(cx, "m3_Sb", [128, 512], BF16)
            y2 = Buf(cx, "m3_y2", [128, 512], F32)
            szt = Buf(cx, "m3_sz", [128, 512], F32)
            junk = Buf(cx, "m3_junk", [128, 512], BF16)
            ssum = Buf(cx, "m3_ssum", [128, 1], F32)
            ysn = Buf(cx, "m3_ysn", [128, 512], BF16)
            ysT = [Buf(cx, "m3_ysT%d" % i, [128, 4, 128], BF16) for i in range(2)]
            yk = [Tk("ysacc%d" % c) for c in range(NC)]
            psA0, psA1, psB, psC, psY, psO, psS = PS
            for d in range(2):
                m1i, m2i, vi = (0, 1, 1) if d == 0 else (2, 3, 3)
                cx.op("dve", lambda e: e.memset(S[:], 0.0), [], [S.k])
                cx.op("pool", lambda e: e.memset(Sb[:], 0.0), [], [Sb.k])
                order = range(NC) if d == 0 else range(NC - 1, -1, -1)
                for n, c in enumerate(order):
                    tok = slice(c * 128, (c + 1) * 128)
                    dtb = dtt[n % 2]
                    cx.dma("sp", [(dtb[:], dt_tok[tok, :])], [ms.dtk[c]], [dtb.k], dtb.k)
                    if (d == 0 and c == NC // 2) or (d == 1 and c == NC // 2 - 1):
                        cx.op("dve", lambda e: e.tensor_scalar(out=S[:], in0=S[:], scalar1=flag_t[:, 0:1], scalar2=None, op0=ALU.mult),
                              [S.k, flag_t.k], [S.k])
                        cx.op("act", lambda e: e.activation(out=Sb[:], in_=S[:], func=AF.Identity), [S.k], [Sb.k])
                    cx.op("dve", lambda e: e.tensor_tensor(out=t1[:], in0=dtb[:, d * 8:d * 8 + 8], in1=ms.bb[:, d * 8:d * 8 + 8], op=ALU.add),
                          [dtb.k, ms.bb.k], [t1.k])
                    cx.op("act", lambda e: e.activation(out=t1[:], in_=t1[:], func=AF.Exp), [t1.k], [t1.k])
                    cx.op("act", lambda e: e.activation(out=dts[:], in_=t1[:], func=AF.Ln, bias=one_t[:]), [t1.k, one_t.k], [dts.k])
                    cx.op("dve", lambda e: e.tensor_tensor(out=a_t[:], in0=dts[:], in1=ms.aneg[:, d * 8:d * 8 + 8], op=ALU.mult),
                          [dts.k, ms.aneg.k], [a_t.k])
                    for h in range(8):
                        eng = "dve" if h % 2 == 0 else "pool"
                        cx.op(eng, lambda e, h=h: e.tensor_scalar(out=lmh[:, h, :], in0=mk_f[:, m1i, :], scalar1=a_t[:, h:h + 1], scalar2=None,
                                                                  op0=ALU.mult), [mk_f.k, a_t.k], [lmh.k])
                    for h in range(8):
                        psA = psA0 if h < 4 else psA1
                        cx.op("pe", lambda e, h=h, psA=psA: e.matmul(psA[:, (h % 4) * 128:(h % 4 + 1) * 128], lhsT=lmh[:, h, :], rhs=mk_f[:, m2i, :],
                                                                    start=True, stop=True), [lmh.k, mk_f.k], [psA.k])
                    cx.op("pe", lambda e: e.matmul(psB[:, 0:8], lhsT=mk_f[:, m1i, :], rhs=a_t[:], start=True, stop=True), [mk_f.k, a_t.k], [psB.k])
                    cx.op("pe", lambda e: e.matmul(psB[:, 8:16], lhsT=mk_f[:, m2i, :], rhs=a_t[:], start=True, stop=True), [mk_f.k, a_t.k], [psB.k])
                    cx.op("pe", lambda e: e.matmul(psB[:, 16:24], lhsT=ones_f[:], rhs=a_t[:], start=True, stop=True), [ones_f.k, a_t.k], [psB.k])
                    cx.op("act", lambda e: e.activation(out=ex[:], in_=psB[:, 0:24], func=AF.Exp), [psB.k], [ex.k])
                    cx.op("act", lambda e: e.activation(out=Lm[:, 0:4, :], in_=V4(psA0[:, 0:512], 4), func=AF.Exp), [psA0.k], [Lm.k])
                    cx.op("act", lambda e: e.activation(out=Lm[:, 4:8, :], in_=V4(psA1[:, 0:512], 4), func=AF.Exp), [psA1.k, Lm.k], [Lm.k])
                    for g in range(2):
                        cx.op("pe", lambda e, g=g: e.matmul(psC[:, g * 128:(g + 1) * 128], lhsT=ms.BT[:, g, tok], rhs=ms.CT[:, g, tok],
                                                            start=True, stop=True), [ms.sk[c]], [psC.k])
                    cx.op("dve", lambda e: e.tensor_tensor(out=CBm[:], in0=V4(psC[:, 0:256], 2),
                                                           in1=mk_f[:, vi, :].unsqueeze(1).to_broadcast([128, 2, 128]), op=ALU.mult),
                          [psC.k, mk_f.k], [CBm.k])
                    for g in range(2):
                        cx.op("dve", lambda e, g=g: e.tensor_tensor(out=G[:, g * 4:(g + 1) * 4, :], in0=Lm[:, g * 4:(g + 1) * 4, :],
                                                                    in1=CBm[:, g, :].unsqueeze(1).to_broadcast([128, 4, 128]), op=ALU.mult),
                              [Lm.k, CBm.k, G.k], [G.k])
                    cx.op("dve", lambda e: e.tensor_tensor(out=xdt[:], in0=V4(ms.xs_tok[:, c, :], 8),
                                                           in1=dts[:].unsqueeze(2).to_broadcast([128, 8, 64]), op=ALU.mult),
                          [ms.sk[c], dts.k], [xdt.k])
                    cx.op("pool", lambda e: e.tensor_tensor(out=xw[:], in0=xdt[:], in1=ex[:, 0:8].unsqueeze(2).to_broadcast([128, 8, 64]),
                                                            op=ALU.mult), [xdt.k, ex.k], [xw.k])
                    for h in range(8):
                        cx.op("pe", lambda e, h=h: e.matmul(psY[:, h * 64:(h + 1) * 64], lhsT=G[:, h, :], rhs=xdt[:, h, :], start=True, stop=True),
                              [G.k, xdt.k], [psY.k])
                    for g in range(2):
                        cx.op("pe", lambda e, g=g: e.matmul(psO[:, g * 256:(g + 1) * 256], lhsT=ms.CT[:, g, tok], rhs=Sb[:, g * 256:(g + 1) * 256],
                                                            start=True, stop=True), [ms.sk[c], Sb.k], [psO.k])
                    cx.op("dve", lambda e: e.tensor_tensor(out=V4(ytmp[:], 8), in0=V4(psO[:, 0:512], 8),
                                                           in1=ex[:, 8:16].unsqueeze(2).to_broadcast([128, 8, 64]), op=ALU.mult),
                          [psO.k, ex.k], [ytmp.k])
                    yc = ycur[n % 2]
                    cx.op("dve", lambda e: e.tensor_tensor(out=yc[:], in0=ytmp[:], in1=psY[:, 0:512], op=ALU.add), [ytmp.k, psY.k], [yc.k])
                    for g in range(2):
                        cx.op("pe", lambda e, g=g: e.matmul(psS[:, g * 256:(g + 1) * 256], lhsT=ms.B_tok[:, c, g * 128:(g + 1) * 128],
                                                            rhs=xw[:, g * 4:(g + 1) * 4, :].rearrange("p a b -> p (a b)"), start=True, stop=True),
                              [ms.sk[c], xw.k], [psS.k])
                    cx.op("dve", lambda e: e.tensor_tensor(out=V4(S[:], 8), in0=V4(S[:], 8),
                                                           in1=ex[:, 16:24].unsqueeze(2).to_broadcast([128, 8, 64]), op=ALU.mult),
                          [S.k, ex.k, Sb.k], [S.k])
                    cx.op("dve", lambda e: e.tensor_tensor(out=S[:], in0=S[:], in1=psS[:, 0:512], op=ALU.add), [S.k, psS.k], [S.k])
                    cx.op("act", lambda e: e.activation(out=Sb[:], in_=S[:], func=AF.Identity), [S.k], [Sb.k])
                    if d == 0:
                        cx.dma("sp", [(ysacc[tok, :], yc[:])], [yc.k], [yk[c]], yc.k)
                        continue
                    yp = yprev[n % 2]
                    z = zt[n % 2]
                    cx.dma("sp", [(yp[:], ysacc[tok, :])], [yk[c]], [yp.k], yp.k)
                    cx.dma("sp", [(z[:], z_tok[tok, :])], [ms.zk[c]], [z.k], z.k)
                    cx.op("dve", lambda e: e.tensor_tensor(out=yc[:], in0=yc[:], in1=yp[:], op=ALU.add), [yc.k, yp.k], [yc.k])
                    cx.op("pool", lambda e: e.tensor_tensor(out=y2[:], in0=ms.xs_tok[:, c, :], in1=ms.bb[:, 32:544], op=ALU.mult),
                          [ms.sk[c], ms.bb.k], [y2.k])
                    cx.op("dve", lambda e: e.tensor_tensor(out=y2[:], in0=y2[:], in1=yc[:], op=ALU.add), [y2.k, yc.k], [y2.k])
                    cx.op("act", lambda e: e.activation(out=szt[:], in_=z[:], func=AF.Silu), [z.k], [szt.k])
                    cx.op("dve", lambda e: e.tensor_tensor(out=y2[:], in0=y2[:], in1=szt[:], op=ALU.mult), [y2.k, szt.k], [y2.k])
                    cx.op("act", lambda e: e.activation(out=junk[:], in_=y2[:], func=AF.Square, accum_out=ssum[:]), [y2.k], [junk.k, ssum.k])
                    cx.op("act", lambda e: e.activation(out=ssum[:], in_=ssum[:], func=AF.Sqrt, bias=eps_t[:], scale=1.0 / 512), [ssum.k, eps_t.k], [ssum.k])
                    cx.op("dve", lambda e: e.reciprocal(out=ssum[:], in_=ssum[:]), [ssum.k], [ssum.k])
                    cx.op("pool", lambda e: e.tensor_scalar(out=y2[:], in0=y2[:], scalar1=ssum[:, 0:1], scalar2=None, op0=ALU.mult),
                          [y2.k, ssum.k], [y2.k])
                    cx.op("dve", lambda e: e.tensor_tensor(out=ysn[:], in0=y2[:], in1=ms.bb[:, 544:1056], op=ALU.mult), [y2.k, ms.bb.k], [ysn.k])
                    for j in range(4):
                        cx.op("pe", lambda e, j=j: e.transpose(out=PT[:, j * 128:(j + 1) * 128], in_=ysn[:, j * 128:(j + 1) * 128],
                                                              identity=ident_b[:]), [ysn.k, ident_b.k], [PT.k])
                    yst = ysT[n % 2]
                    cx.op("act", lambda e: e.activation(out=yst[:], in_=V4(PT[:, 0:512], 4), func=AF.Identity), [PT.k], [yst.k])
                    cx.dma("sp", [(ymix_d[4:8, :, tok].rearrange("k p t -> p k t"), yst[:])], [yst.k], [ms.ymk[1][c]], yst.k)

        def m5_outproj(l, src, dst, ms):
            woutb = Buf(cx, "woutb", [128, KD, D], BF16)
            cx.dma("pool", [(woutb[:].rearrange("p k c -> p (k c)").rearrange("p (a b) -> p a b", b=2048),
                             w_out[l].rearrange("p (a b) -> p a b", b=2048))], [], [woutb.k], woutb.k)
            xt = [Buf(cx, "m5_xt%d" % i, [128, KD, TM], F32) for i in range(2)]
            sq = Buf(cx, "m5_sq", [128, KD, TM], BF16)
            std = Buf(cx, "m5_std", [128, TM], F32)
            rstd = Buf(cx, "m5_rstd", [128, TM], F32)
            fT = Buf(cx, "m5_fT", [128, KD, TM], F32)
            ymtb = [Buf(cx, "m5_ymt%d" % i, [128, KD, TM], BF16) for i in range(2)]
            dk = [Tk("m5_dst%d" % i) for i in range(NT // TM)]
            for ti in range(NT // TM):
                t0 = ti * TM
                seg = t0 // SEG
                x = xt[ti % 2]
                cx.dma("sp", [(x[:], src[:, :, t0:t0 + TM].rearrange("k p t -> p k t"))], [], [x.k], x.k)
                rks = [ms.ymk[h][t0 // 128 + j] for h in range(2) for j in range(TM // 128)]
                ymt = ymtb[ti % 2]
                cx.dma("sp", [(ymt[:], ymix_d[:, :, t0:t0 + TM].rearrange("k p t -> p k t"))], rks, [ymt.k], ymt.k)
                if debug and l == 0:
                    cx.op("dve", lambda e: e.tensor_copy(out=fT[:], in_=ymt[:]), [ymt.k], [fT.k])
                    cx.dma("sp", [(dbg_out[:, :, t0:t0 + TM].rearrange("k p t -> p k t"), fT[:])], [fT.k], [], fT.k)

                def psrc(o):
                    ps = PS[1 + o % 4]
                    for k in range(KD):
                        cx.op("pe", lambda e, k=k: e.matmul(ps[:, 0:TM], lhsT=woutb[:, k, o * 128:(o + 1) * 128], rhs=ymt[:, k, :],
                                                            start=(k == 0), stop=(k == KD - 1)), [woutb.k, ymt.k], [ps.k])
                    return ps
                post_tile(x, seg, TM, psrc, KD, sq, std, rstd, fT, G1)
                cx.dma("sp", [(dst[:, :, t0:t0 + TM].rearrange("k p t -> p k t"), x[:])], [x.k], [dk[ti]], x.k)


        KAP = 0.6065306597126334

        def m4_rwkv(l, ms):
            NTL = NT // TM
            wupb = Buf(cx, "wupb", [128, 512], BF16)
            aupb = Buf(cx, "aupb", [128, 512], BF16)
            gupb = Buf(cx, "gupb", [128, 512], BF16)
            cx.dma("pool", [(wupb[:], wup_in[l])], [], [wupb.k], wupb.k)
            cx.dma("pool", [(aupb[:], aup_in[l])], [], [aupb.k], aupb.k)
            cx.dma("pool", [(gupb[:], gup_in[l])], [], [gupb.k], gupb.k)
            c0 = Buf(cx, "c0", [128, 15], F32)
            omka = Buf(cx, "omka", [128, 4], F32)
            cx.op("dve", lambda e: e.tensor_tensor(out=c0[:], in0=ms.pp[:, 48:63], in1=ms.pp[:, 63:78], op=ALU.add), [ms.pp.k], [c0.k])
            cx.op("dve", lambda e: e.tensor_scalar(out=c0[:], in0=c0[:], scalar1=-1.0, scalar2=1.0, op0=ALU.mult, op1=ALU.add), [c0.k], [c0.k])
            cx.op("dve", lambda e: e.tensor_scalar(out=omka[:], in0=ms.pp[:, 98:102], scalar1=-1.0, scalar2=1.0, op0=ALU.mult, op1=ALU.add),
                  [ms.pp.k], [omka.k])
            m4m = [Buf(cx, "m4m%d" % d, [128, 4, 128], F32) for d in range(2)]
            for d in range(2):
                for q in range(4):
                    src_i = ((2, 1) if d == 0 else (0, 3))[q % 2]
                    cx.op("pool", lambda e, d=d, q=q, src_i=src_i: e.tensor_copy(out=m4m[d][:, q, :], in_=mk_f[:, src_i, :]), [mk_f.k], [m4m[d].k])
            FB = lambda nm: Buf(cx, nm, [128, TM], F32)
            BB = lambda nm: Buf(cx, nm, [128, TM], BF16)
            win = [Buf(cx, "m4_win%d" % i, [128, TM + 2], F32) for i in range(2)]
            nwin = [0]
            wl, al, gl = FB("m4_wl"), FB("m4_al"), FB("m4_gl")
            twl, alb, sgl = BB("m4_twl"), BB("m4_alb"), BB("m4_sgl")
            rs = [FB("m4_rs%d" % g) for g in range(4)]
            ks = [FB("m4_ks%d" % g) for g in range(4)]
            vs = [FB("m4_vs%d" % g) for g in range(4)]
            AR = [Buf(cx, "m4_AR%d" % g, [128, 4, 256], BF16) for g in range(4)]
            ARh = [Buf(cx, "m4_ARh%d" % h, [128, 4, 256], BF16) for h in range(8)]
            kt = [BB("m4_kt%d" % g) for g in range(4)]
            bt = [BB("m4_bt%d" % g) for g in range(4)]
            v_tok = Buf(cx, "m4_vtok", [128, 4, 512], BF16)
            kh_tok = Buf(cx, "m4_khtok", [128, 4, 512], BF16)
            bh_tok = Buf(cx, "m4_bhtok", [128, 4, 512], BF16)
            gam = Buf(cx, "m4_gam", [128, 4, 4], F32)
            sg, E1, E0, R0, R1 = FB("m4_sg"), FB("m4_E1"), FB("m4_E0"), FB("m4_R0"), FB("m4_R1")
            eN, eP, eA, eH = FB("m4_eN"), FB("m4_eP"), FB("m4_eA"), FB("m4_eH")
            a_s, kkr, nrm, kk, tt, kd, bq = FB("m4_as"), FB("m4_kkr"), FB("m4_nrm"), FB("m4_kk"), FB("m4_tt"), FB("m4_kd"), FB("m4_b")
            sqk, khT, bhT, vT = BB("m4_sqk"), BB("m4_khT"), BB("m4_bhT"), BB("m4_vT")
            SC4 = [Buf(cx, "m4_SC%d" % i, [128, 4, 512], BF16) for i in range(2)]
            XT0 = Buf(cx, "m4_XT0", [128, 4, 128], BF16)
            Xp = [Buf(cx, "m4_X%d" % i, [128, 4, 128], BF16) for i in range(2)]
            XTp = [Buf(cx, "m4_XT%d" % i, [128, 4, 128], BF16) for i in range(2)]
            Rp = [Buf(cx, "m4_R%d" % i, [128, 4, 128], BF16) for i in range(3)]
            TT = [Buf(cx, "m4_TT%d" % i, [128, 4, 128], BF16) for i in range(2)]
            Noff = [Buf(cx, "m4_Noff%d" % i, [128, 4, 128], BF16) for i in range(3)]
            Loff = [Buf(cx, "m4_Loff%d" % i, [128, 4, 128], BF16) for i in range(3)]
            Dp = [Buf(cx, "m4_Dp%d" % i, [128, 4, 128], BF16) for i in range(2)]
            DTp = [Buf(cx, "m4_DTp%d" % i, [128, 4, 128], BF16) for i in range(2)]
            M1b = Buf(cx, "m4_M1b", [128, 4, 128], BF16)
            M2b = Buf(cx, "m4_M2b", [128, 4, 128], BF16)
            Wb = Buf(cx, "m4_Wb", [128, 512], BF16)
            Uneg = Buf(cx, "m4_Uneg", [128, 512], BF16)
            Hf = Buf(cx, "m4_Hf", [128, 4, 128], F32)
            Hb = Buf(cx, "m4_Hb", [128, 4, 128], BF16)
            tmpH = Buf(cx, "m4_tmpH", [128, 4, 128], F32)
            Ytile = Buf(cx, "m4_Ytile", [128, 4, TM], F32)
            yfw = Buf(cx, "m4_yfw", [128, 4, TM], F32)
            yv, ycn, yn2, rk, bon = sg, E1, E0, R0, R1
            ybf, rkb = sqk, khT
            yob = [BB("m4_yo%d" % i) for i in range(2)]
            yrk = [Tk("yracc%d" % ti) for ti in range(NTL)]

            def shift(fc, ti, out):
                w = win[nwin[0] % 2]
                nwin[0] += 1
                load_win(w, fc, ti, 1, ms)
                cx.op("pool", lambda e: e.tensor_scalar(out=out[:], in0=w[:, 1:TM + 1], scalar1=c0[:, fc:fc + 1], scalar2=None, op0=ALU.mult),
                      [w.k, c0.k], [out.k])
                cx.op("dve", lambda e: e.scalar_tensor_tensor(out=out[:], in0=w[:, 0:TM], scalar=ms.pp[:, 48 + fc:49 + fc], in1=out[:],
                                                              op0=ALU.mult, op1=ALU.add), [w.k, ms.pp.k, out.k], [out.k])
                cx.op("dve", lambda e: e.scalar_tensor_tensor(out=out[:], in0=w[:, 2:TM + 2], scalar=ms.pp[:, 63 + fc:64 + fc], in1=out[:],
                                                              op0=ALU.mult, op1=ALU.add), [w.k, ms.pp.k, out.k], [out.k])

            def tt_(eng, out, a, b, op, rd, wr):
                cx.op(eng, lambda e: e.tensor_tensor(out=out, in0=a, in1=b, op=op), rd, wr)

            def prep(d, ti):
                shift(12, ti, wl)
                cx.op("act", lambda e: e.activation(out=twl[:], in_=wl[:], func=AF.Tanh), [wl.k], [twl.k])
                shift(13, ti, al)
                cx.op("act", lambda e: e.activation(out=alb[:], in_=al[:], func=AF.Identity), [al.k], [alb.k])
                if d == 1:
                    shift(14, ti, gl)
                    cx.op("act", lambda e: e.activation(out=sgl[:], in_=gl[:], func=AF.Sigmoid), [gl.k], [sgl.k])
                dp = slice(d * 64, d * 64 + 64)
                import os as _os
                for g in (range(3, -1, -1) if _os.environ.get("GREV") else range(4)):
                    gs = slice(g * 128, (g + 1) * 128)
                    shift(g, ti, rs[g])
                    shift(4 + g, ti, ks[g])
                    shift(8 + g, ti, vs[g])
                    cx.op("pe", lambda e: e.matmul(PS[0][:, 0:TM], lhsT=wupb[dp, gs], rhs=twl[dp, :], start=True, stop=True), [wupb.k, twl.k], [PS[0].k])
                    cx.op("act", lambda e: e.activation(out=sg[:], in_=PS[0][:, 0:TM], func=AF.Sigmoid, bias=ms.pp[:, 78 + d * 4 + g:79 + d * 4 + g]),
                          [PS[0].k, ms.pp.k], [sg.k])
                    cx.op("dve", lambda e: e.tensor_tensor_scan(out=E1[:], data0=rmask[:], data1=sg[:], initial=0.0, op0=ALU.mult, op1=ALU.add),
                          [rmask.k, sg.k], [E1.k])
                    tt_("pool", E0[:], E1[:], sg[:], ALU.subtract, [E1.k, sg.k], [E0.k])
                    tt_("dve", V4(R0[:], 4), V4(E1[:], 4)[:, :, 127:128].to_broadcast([128, 4, 128]), V4(E1[:], 4), ALU.subtract, [E1.k], [R0.k])
                    tt_("pool", R1[:], R0[:], sg[:], ALU.add, [R0.k, sg.k], [R1.k])
                    X1, X0e, Y0 = (E1, E0, R0) if d == 0 else (R1, R0, E0)
                    cx.op("act", lambda e: e.activation(out=eN[:], in_=X1[:], func=AF.Exp, scale=-KAP), [X1.k], [eN.k])
                    cx.op("act", lambda e: e.activation(out=eP[:], in_=X1[:], func=AF.Exp, scale=KAP), [X1.k], [eP.k])
                    cx.op("act", lambda e: e.activation(out=eA[:], in_=X0e[:], func=AF.Exp, scale=-KAP), [X0e.k], [eA.k])
                    cx.op("act", lambda e: e.activation(out=eH[:], in_=Y0[:], func=AF.Exp, scale=-KAP), [Y0.k], [eH.k])
                    cx.op("act", lambda e: e.activation(out=gam[:, g, :], in_=V4(E1[:], 4)[:, :, 127], func=AF.Exp, scale=-KAP), [E1.k], [gam.k])
                    cx.op("pe", lambda e: e.matmul(PS[1][:, 0:TM], lhsT=aupb[dp, gs], rhs=alb[dp, :], start=True, stop=True), [aupb.k, alb.k], [PS[1].k])
                    cx.op("act", lambda e: e.activation(out=a_s[:], in_=PS[1][:, 0:TM], func=AF.Sigmoid, bias=ms.pp[:, 86 + d * 4 + g:87 + d * 4 + g]),
                          [PS[1].k, ms.pp.k], [a_s.k])
                    cx.op("pool", lambda e: e.tensor_scalar(out=kkr[:], in0=ks[g][:], scalar1=ms.pp[:, 94 + g:95 + g], scalar2=None, op0=ALU.mult),
                          [ks[g].k, ms.pp.k], [kkr.k])
                    cx.op("act", lambda e: e.activation(out=sqk[:], in_=kkr[:], func=AF.Square), [kkr.k], [sqk.k])
                    cx.op("pe", lambda e: e.matmul(PS[2][:, 0:TM], lhsT=blk1_b[:], rhs=sqk[:], start=True, stop=True), [blk1_b.k, sqk.k], [PS[2].k])
                    cx.op("act", lambda e: e.activation(out=nrm[:], in_=PS[2][:, 0:TM], func=AF.Sqrt), [PS[2].k], [nrm.k])
                    cx.op("dve", lambda e: e.tensor_scalar(out=nrm[:], in0=nrm[:], scalar1=1e-12, scalar2=None, op0=ALU.max), [nrm.k], [nrm.k])
                    cx.op("dve", lambda e: e.reciprocal(out=nrm[:], in_=nrm[:]), [nrm.k], [nrm.k])
                    tt_("dve", kk[:], kkr[:], nrm[:], ALU.mult, [kkr.k, nrm.k], [kk.k])
                    cx.op("pool", lambda e: e.tensor_scalar(out=tt[:], in0=a_s[:], scalar1=ms.pp[:, 98 + g:99 + g], scalar2=omka[:, g:g + 1],
                                                            op0=ALU.mult, op1=ALU.add), [a_s.k, ms.pp.k, omka.k], [tt.k])
                    tt_("dve", kd[:], ks[g][:], tt[:], ALU.mult, [ks[g].k, tt.k], [kd.k])
                    tt_("pool", bq[:], kk[:], a_s[:], ALU.mult, [kk.k, a_s.k], [bq.k])
                    tt_("dve", AR[g][:, :, 0:128], V4(kk[:], 4), V4(eA[:], 4), ALU.mult, [kk.k, eA.k], [AR[g].k])
                    tt_("pool", AR[g][:, :, 128:256], V4(rs[g][:], 4), V4(eN[:], 4), ALU.mult, [rs[g].k, eN.k, AR[g].k], [AR[g].k])
                    for hl in range(2):
                        eng = "dve" if hl == 0 else "pool"
                        cx.op(eng, lambda e, hl=hl: e.tensor_scalar(out=ARh[2 * g + hl][:], in0=AR[g][:], scalar1=blk1_f[:, hl * 64:hl * 64 + 1], scalar2=None,
                                                                    op0=ALU.mult), [AR[g].k, blk1_f.k], [ARh[2 * g + hl].k])
                    tt_("dve", kt[g][:], kd[:], eP[:], ALU.mult, [kd.k, eP.k], [kt[g].k])
                    tt_("pool", bt[g][:], bq[:], eP[:], ALU.mult, [bq.k, eP.k], [bt[g].k])
                    tt_("dve", khT[:], kd[:], eH[:], ALU.mult, [kd.k, eH.k], [khT.k])
                    tt_("pool", bhT[:], bq[:], eH[:], ALU.mult, [bq.k, eH.k], [bhT.k])
                    cx.op("act", lambda e: e.activation(out=vT[:], in_=vs[g][:], func=AF.Identity), [vs[g].k], [vT.k])
                    for (srcT, dstk) in ((vT, v_tok), (khT, kh_tok), (bhT, bh_tok)):
                        for j in range(4):
                            cx.op("pe", lambda e, j=j, srcT=srcT: e.transpose(out=PT[:, j * 128:(j + 1) * 128], in_=srcT[:, j * 128:(j + 1) * 128],
                                                                             identity=ident_b[:]), [srcT.k, ident_b.k], [PT.k])
                        cx.op("dve", lambda e, dstk=dstk: e.tensor_copy(out=dstk[:, :, gs], in_=V4(PT[:, 0:512], 4)), [PT.k, dstk.k], [dstk.k])

            def scan(d, ti, j):
                cs = slice(j * 128, (j + 1) * 128)
                mL = 0 if d == 0 else 2
                c = ti * 4 + j
                if (d == 0 and c == NC // 2) or (d == 1 and c == NC // 2 - 1):
                    cx.op("dve", lambda e: e.tensor_scalar(out=Hf[:], in0=Hf[:], scalar1=flag_t[:, 0:1], scalar2=None, op0=ALU.mult),
                          [Hf.k, flag_t.k], [Hf.k])
                    cx.op("act", lambda e: e.activation(out=Hb[:], in_=Hf[:], func=AF.Identity), [Hf.k], [Hb.k])
                for hg in range(2):
                    sc = SC4[hg]
                    for hh in range(4):
                        h = hg * 4 + hh
                        g = h // 2
                        po = slice((h % 2) * 64, (h % 2) * 64 + 64)
                        ps = PS[hh]
                        cx.op("pe", lambda e: e.matmul(ps[:, 0:256], lhsT=kt[g][:, cs], rhs=ARh[h][:, j, :], start=True, stop=True),
                              [kt[g].k, ARh[h].k], [ps.k])
                        cx.op("pe", lambda e: e.matmul(ps[:, 256:512], lhsT=bt[g][:, cs], rhs=ARh[h][:, j, :], start=True, stop=True),
                              [bt[g].k, ARh[h].k], [ps.k])
                        cx.op("pe", lambda e: e.matmul(PS[4][:, hh * 128:(hh + 1) * 128], lhsT=ARh[h][:, j, 0:128], rhs=bt[g][:, cs], start=True, stop=True),
                              [bt[g].k, ARh[h].k], [PS[4].k])
                        cx.op("dve", lambda e: e.tensor_tensor(out=sc[:, hh, :], in0=ps[:, 0:512], in1=m4m[d][:].rearrange("p a b -> p (a b)"), op=ALU.mult),
                              [ps.k, m4m[d].k, sc.k], [sc.k])
                    cx.op("dve", lambda e: e.tensor_tensor(out=XT0[:], in0=V4(PS[4][:, 0:512], 4),
                                                           in1=mk_f[:, mL, :].unsqueeze(1).to_broadcast([128, 4, 128]), op=ALU.mult),
                          [PS[4].k, mk_f.k], [XT0.k])
                    P1, P2, P3 = PS[5], PS[6], PS[4]
                    bmb = lambda q: bm_b[:, q, :].unsqueeze(1).to_broadcast([128, 4, 128])
                    N0, L0 = Xp[0], XTp[0]
                    cx.op("pool", lambda e: e.tensor_tensor(out=N0[:], in0=sc[:, :, 256:384], in1=bmb(0), op=ALU.mult), [sc.k, bm_b.k], [N0.k])
                    cx.op("dve", lambda e: e.tensor_tensor(out=L0[:], in0=XT0[:], in1=bmb(0), op=ALU.mult), [XT0.k, bm_b.k], [L0.k])
                    for q in range(3):
                        cx.op("pool", lambda e, q=q: e.tensor_tensor(out=Noff[q][:], in0=sc[:, :, 256:384], in1=bmb(q + 1), op=ALU.mult),
                              [sc.k, bm_b.k], [Noff[q].k])
                        cx.op("dve", lambda e, q=q: e.tensor_tensor(out=Loff[q][:], in0=XT0[:], in1=bmb(q + 1), op=ALU.mult),
                              [XT0.k, bm_b.k], [Loff[q].k])
                    R = Rp[2]
                    cx.op("pool", lambda e: e.tensor_tensor(out=R[:], in0=ident_b[:].unsqueeze(1).to_broadcast([128, 4, 128]), in1=N0[:],
                                                            op=ALU.subtract), [ident_b.k, N0.k], [R.k])
                    Xc, XTc = N0, L0
                    for k in range(3):
                        XTn = XTp[(k + 1) % 2]
                        Xn = Xp[(k + 1) % 2]
                        for hh in range(4):
                            cx.op("pe", lambda e, hh=hh: e.matmul(P2[:, hh * 128:(hh + 1) * 128], lhsT=Xc[:, hh, :], rhs=XTc[:, hh, :], start=True, stop=True),
                                  [Xc.k, XTc.k], [P2.k])
                        if k < 2:
                            for hh in range(4):
                                cx.op("pe", lambda e, hh=hh: e.matmul(P1[:, hh * 128:(hh + 1) * 128], lhsT=XTc[:, hh, :], rhs=Xc[:, hh, :], start=True, stop=True),
                                      [Xc.k, XTc.k], [P1.k])
                        cx.op("dve", lambda e: e.tensor_copy(out=XTn[:], in_=V4(P2[:, 0:512], 4)), [P2.k], [XTn.k])
                        if k < 2:
                            cx.op("act", lambda e: e.activation(out=Xn[:], in_=V4(P1[:, 0:512], 4), func=AF.Identity), [P1.k], [Xn.k])
                        for hh in range(4):
                            cx.op("pe", lambda e, hh=hh: e.matmul(P3[:, hh * 128:(hh + 1) * 128], lhsT=XTn[:, hh, :], rhs=R[:, hh, :], start=True, stop=False),
                                  [XTn.k, R.k], [P3.k])
                            cx.op("pe", lambda e, hh=hh: e.matmul(P3[:, hh * 128:(hh + 1) * 128], lhsT=ident_b[:], rhs=R[:, hh, :], start=False, stop=True),
                                  [ident_b.k, R.k], [P3.k])
                        Rn = Rp[k % 2]
                        cx.op("act", lambda e: e.activation(out=Rn[:], in_=V4(P3[:, 0:512], 4), func=AF.Identity), [P3.k], [Rn.k])
                        R = Rn
                        Xc, XTc = Xn, XTn
                    DT = R
                    Dn = Dp[0]
                    for hh in range(4):
                        cx.op("pe", lambda e, hh=hh: e.transpose(out=PT[:, hh * 128:(hh + 1) * 128], in_=DT[:, hh, :], identity=ident_b[:]),
                              [DT.k, ident_b.k], [PT.k])
                    cx.op("dve", lambda e: e.tensor_copy(out=Dn[:], in_=V4(PT[:, 0:512], 4)), [PT.k], [Dn.k])
                    for q in range(3):
                        last = (q == 2)
                        DTn = TT[hg] if last else DTp[q % 2]
                        Dnn = Dp[(q + 1) % 2]
                        for hh in range(4):
                            cx.op("pe", lambda e, hh=hh: e.matmul(P1[:, hh * 128:(hh + 1) * 128], lhsT=Loff[q][:, hh, :], rhs=DT[:, hh, :], start=True, stop=True),
                                  [Loff[q].k, DT.k], [P1.k])
                        cx.op("act", lambda e: e.activation(out=M1b[:], in_=V4(P1[:, 0:512], 4), func=AF.Identity), [P1.k], [M1b.k])
                        if not last:
                            for hh in range(4):
                                cx.op("pe", lambda e, hh=hh: e.matmul(P2[:, hh * 128:(hh + 1) * 128], lhsT=Noff[q][:, hh, :], rhs=Dn[:, hh, :], start=True, stop=True),
                                      [Noff[q].k, Dn.k], [P2.k])
                            cx.op("dve", lambda e: e.tensor_copy(out=M2b[:], in_=V4(P2[:, 0:512], 4)), [P2.k], [M2b.k])
                        for hh in range(4):
                            cx.op("pe", lambda e, hh=hh: e.matmul(P3[:, hh * 128:(hh + 1) * 128], lhsT=Dn[:, hh, :], rhs=M1b[:, hh, :], start=True, stop=True),
                                  [Dn.k, M1b.k], [P3.k])
                        cx.op("dve", lambda e: e.tensor_tensor(out=DTn[:], in0=DT[:], in1=V4(P3[:, 0:512], 4), op=ALU.subtract), [DT.k, P3.k], [DTn.k])
                        if not last:
                            for hh in range(4):
                                cx.op("pe", lambda e, hh=hh: e.matmul(P1[:, hh * 128:(hh + 1) * 128], lhsT=DT[:, hh, :], rhs=M2b[:, hh, :], start=True, stop=True),
                                      [DT.k, M2b.k], [P1.k])
                            cx.op("dve", lambda e: e.tensor_tensor(out=Dnn[:], in0=Dn[:], in1=V4(P1[:, 0:512], 4), op=ALU.subtract), [Dn.k, P1.k], [Dnn.k])
                            Dn = Dnn
                        DT = DTn
                psW, psU, psY, psH = PS[0], PS[1], PS[2], PS[3]
                for g in range(4):
                    cx.op("pe", lambda e: e.matmul(psW[:, g * 128:(g + 1) * 128], lhsT=AR[g][:, j, 0:128], rhs=Hb[:, g, :], start=True, stop=False),
                          [AR[g].k, Hb.k], [psW.k])
                    for hl in range(2):
                        h = 2 * g + hl
                        hg, hh = divmod(h, 4)
                        cx.op("pe", lambda e: e.matmul(psW[:, h * 64:(h + 1) * 64], lhsT=SC4[hg][:, hh, 0:128], rhs=v_tok[:, j, h * 64:(h + 1) * 64],
                                                       start=False, stop=(hl == 1)), [SC4[hg].k, v_tok.k], [psW.k])
                cx.op("act", lambda e: e.activation(out=Wb[:], in_=psW[:, 0:512], func=AF.Identity), [psW.k], [Wb.k])
                for h in range(8):
                    hg, hh = divmod(h, 4)
                    cx.op("pe", lambda e: e.matmul(psU[:, h * 64:(h + 1) * 64], lhsT=TT[hg][:, hh, :], rhs=Wb[:, h * 64:(h + 1) * 64], start=True, stop=True),
                          [TT[hg].k, Wb.k], [psU.k])
                cx.op("act", lambda e: e.activation(out=Uneg[:], in_=psU[:, 0:512], func=AF.Identity, scale=-1.0), [psU.k], [Uneg.k])
                for g in range(4):
                    cx.op("pe", lambda e: e.matmul(psY[:, g * 128:(g + 1) * 128], lhsT=Hb[:, g, :], rhs=AR[g][:, j, 128:256], start=True, stop=False),
                          [AR[g].k, Hb.k], [psY.k])
                    for hl in range(2):
                        h = 2 * g + hl
                        hg, hh = divmod(h, 4)
                        po = slice(hl * 64, hl * 64 + 64)
                        cx.op("pe", lambda e: e.matmul(psY[po, g * 128:(g + 1) * 128], lhsT=v_tok[:, j, h * 64:(h + 1) * 64], rhs=SC4[hg][:, hh, 128:256],
                                                       start=False, stop=False), [SC4[hg].k, v_tok.k], [psY.k])
                        cx.op("pe", lambda e: e.matmul(psY[po, g * 128:(g + 1) * 128], lhsT=Uneg[:, h * 64:(h + 1) * 64], rhs=SC4[hg][:, hh, 384:512],
                                                       start=False, stop=True), [SC4[hg].k, Uneg.k], [psY.k])
                cx.op("act", lambda e: e.activation(out=Ytile[:, :, cs], in_=V4(psY[:, 0:512], 4), func=AF.Identity), [psY.k, Ytile.k], [Ytile.k])
                for g in range(4):
                    gs = slice(g * 128, (g + 1) * 128)
                    cx.op("pe", lambda e: e.matmul(psH[:, gs], lhsT=kh_tok[:, j, gs], rhs=v_tok[:, j, gs], start=True, stop=False),
                          [kh_tok.k, v_tok.k], [psH.k])
                    cx.op("pe", lambda e: e.matmul(psH[:, gs], lhsT=bh_tok[:, j, gs], rhs=Uneg[:, gs], start=False, stop=True),
                          [bh_tok.k, Uneg.k], [psH.k])
                cx.op("dve", lambda e: e.tensor_tensor(out=tmpH[:], in0=V4(psH[:, 0:512], 4), in1=blk1_f[:].unsqueeze(1).to_broadcast([128, 4, 128]),
                                                       op=ALU.mult), [psH.k, blk1_f.k], [tmpH.k])
                cx.op("dve", lambda e: e.tensor_tensor(out=Hf[:], in0=Hf[:], in1=gam[:, :, j:j + 1].to_broadcast([128, 4, 128]), op=ALU.mult),
                      [Hf.k, gam.k, Hb.k], [Hf.k])
                cx.op("dve", lambda e: e.tensor_tensor(out=Hf[:], in0=Hf[:], in1=tmpH[:], op=ALU.add), [Hf.k, tmpH.k], [Hf.k])
                cx.op("act", lambda e: e.activation(out=Hb[:], in_=Hf[:], func=AF.Identity), [Hf.k], [Hb.k])

            def finalize(ti):
                t0 = ti * TM
                cx.dma("sp", [(yfw[:], yracc[:, :, t0:t0 + TM].rearrange("g p t -> p g t"))], [yrk[ti]], [yfw.k], yfw.k)
                cks = [ms.ymk[0][t0 // 128 + jj] for jj in range(TM // 128)]
                for g in range(4):
                    gs = slice(g * 128, (g + 1) * 128)
                    tt_("dve", yv[:], Ytile[:, g, :], yfw[:, g, :], ALU.add, [Ytile.k, yfw.k], [yv.k])
                    cx.op("act", lambda e: e.activation(out=ybf[:], in_=yv[:], func=AF.Identity), [yv.k], [ybf.k])
                    cx.op("pe", lambda e: e.matmul(PS[5][:, 0:TM], lhsT=blk64_b[:], rhs=ybf[:], start=True, stop=True), [blk64_b.k, ybf.k], [PS[5].k])
                    tt_("dve", ycn[:], yv[:], PS[5][:, 0:TM], ALU.subtract, [yv.k, PS[5].k], [ycn.k])
                    cx.op("act", lambda e: e.activation(out=ybf[:], in_=ycn[:], func=AF.Square), [ycn.k], [ybf.k])
                    cx.op("pe", lambda e: e.matmul(PS[6][:, 0:TM], lhsT=blk64_b[:], rhs=ybf[:], start=True, stop=True), [blk64_b.k, ybf.k], [PS[6].k])
                    cx.op("act", lambda e: e.activation(out=yn2[:], in_=PS[6][:, 0:TM], func=AF.Sqrt, bias=gneps_t[:]), [PS[6].k, gneps_t.k], [yn2.k])
                    cx.op("dve", lambda e: e.reciprocal(out=yn2[:], in_=yn2[:]), [yn2.k], [yn2.k])
                    tt_("dve", ycn[:], ycn[:], yn2[:], ALU.mult, [ycn.k, yn2.k], [ycn.k])
                    cx.op("pool", lambda e: e.tensor_scalar(out=yn2[:], in0=ycn[:], scalar1=ms.pp[:, 106 + g:107 + g], scalar2=ms.pp[:, 110 + g:111 + g],
                                                            op0=ALU.mult, op1=ALU.add), [ycn.k, ms.pp.k], [yn2.k])
                    cx.op("pool", lambda e: e.tensor_scalar(out=rk[:], in0=rs[g][:], scalar1=ms.pp[:, 102 + g:103 + g], scalar2=None, op0=ALU.mult),
                          [rs[g].k, ms.pp.k], [rk.k])
                    tt_("dve", rk[:], rk[:], ks[g][:], ALU.mult, [rk.k, ks[g].k], [rk.k])
                    cx.op("act", lambda e: e.activation(out=rkb[:], in_=rk[:], func=AF.Identity), [rk.k], [rkb.k])
                    cx.op("pe", lambda e: e.matmul(PS[5][:, 0:TM], lhsT=blk1_b[:], rhs=rkb[:], start=True, stop=True), [blk1_b.k, rkb.k], [PS[5].k])
                    tt_("dve", bon[:], PS[5][:, 0:TM], vs[g][:], ALU.mult, [PS[5].k, vs[g].k], [bon.k])
                    tt_("pool", yn2[:], yn2[:], bon[:], ALU.add, [yn2.k, bon.k], [yn2.k])
                    cx.op("pe", lambda e: e.matmul(PS[6][:, 0:TM], lhsT=gupb[:, gs], rhs=sgl[:], start=True, stop=True), [gupb.k, sgl.k], [PS[6].k])
                    yo = yob[g % 2]
                    cx.op("dve", lambda e: e.tensor_tensor(out=yo[:], in0=yn2[:], in1=PS[6][:, 0:TM], op=ALU.mult), [yn2.k, PS[6].k], [yo.k])
                    cx.dma("sp", [(ymix_d[g, :, t0:t0 + TM], yo[:])], [yo.k], cks, yo.k)

            for d in range(2):
                cx.op("dve", lambda e: e.memset(Hf[:], 0.0), [], [Hf.k])
                cx.op("pool", lambda e: e.memset(Hb[:], 0.0), [], [Hb.k])
                tiles = range(NTL) if d == 0 else range(NTL - 1, -1, -1)
                for ti in tiles:
                    prep(d, ti)
                    for j in (range(4) if d == 0 else range(3, -1, -1)):
                        scan(d, ti, j)
                    if DBG == "rdbg" and d == 0 and ti == 0:
                        cx.op("dve", lambda e: e.tensor_copy(out=yfw[:, 0, :], in_=SC4[1][:, 0, :]), [SC4[1].k], [yfw.k])
                        cx.op("dve", lambda e: e.tensor_copy(out=yfw[:, 1, :], in_=SC4[1][:, 2, :]), [SC4[1].k], [yfw.k])
                        cx.op("dve", lambda e: e.tensor_copy(out=yfw[:, 2, :], in_=TT[1][:].rearrange("p a b -> p (a b)")), [TT[1].k], [yfw.k])
                        cx.op("dve", lambda e: e.tensor_copy(out=yfw[:, 3, :], in_=Wb[:]), [Wb.k], [yfw.k])
                        for slot in range(4):
                            cx.dma("sp", [(dbg_out[slot, :, 0:TM], yfw[:, slot, :])], [yfw.k], [], yfw.k)
                        cx.op("dve", lambda e: e.tensor_copy(out=yfw[:, 0, :], in_=Uneg[:]), [Uneg.k], [yfw.k])
                        cx.op("dve", lambda e: e.tensor_copy(out=yfw[:, 1, :], in_=v_tok[:, 3, :]), [v_tok.k], [yfw.k])
                        cx.op("dve", lambda e: e.tensor_copy(out=yfw[:, 2, :], in_=kt[2][:]), [kt[2].k], [yfw.k])
                        cx.op("dve", lambda e: e.tensor_copy(out=yfw[:, 3, 0:256], in_=AR[2][:, 3, :]), [AR[2].k], [yfw.k])
                        for slot in range(4):
                            cx.dma("sp", [(dbg_out[4 + slot, :, 0:(TM if slot < 3 else 256)], yfw[:, slot, 0:(TM if slot < 3 else 256)])], [yfw.k], [], yfw.k)
                        return
                    if d == 0:
                        cx.dma("sp", [(yracc[:, :, ti * TM:(ti + 1) * TM].rearrange("g p t -> p g t"), Ytile[:])], [Ytile.k], [yrk[ti]], Ytile.k)
                    else:
                        finalize(ti)


        def mixer_phase(l, src, dst, em):
            ms = MixState()
            mix_load_params(l, ms)
            with contextlib.ExitStack() as e1:
                cx.mem_es = e1
                fo = len(cx.owners)
                m1_inproj(l, src, ms)
                cx.end_phase(fo)
            cx.mem_es = em
            with contextlib.ExitStack() as e2:
                cx.mem_es = e2
                fo = len(cx.owners)
                m2_ssdprep(l, ms)
                m3_ssd(l, ms)
                cx.end_phase(fo)
            cx.mem_es = em
            with contextlib.ExitStack() as e4:
                cx.mem_es = e4
                fo = len(cx.owners)
                m4_rwkv(l, ms)
                cx.end_phase(fo)
            cx.mem_es = em
            with contextlib.ExitStack() as e5:
                cx.mem_es = e5
                fo = len(cx.owners)
                if DBG != "rdbg":
                    m5_outproj(l, src, dst, ms)
                cx.end_phase(fo)
            cx.mem_es = em

        out_tks = []
        cur = xT
        for l in range(depth):
            with contextlib.ExitStack() as fes:
                cx.mem_es = fes
                fo = len(cx.owners)
                mod_phase(l)
                cx.end_phase(fo)
                cx.mem_es = es
            import os
            if do_mix:
                mdst = xres if (do_ffn or l < depth - 1) else yT
                with contextlib.ExitStack() as em:
                    cx.mem_es = em
                    fo = len(cx.owners)
                    mixer_phase(l, cur, mdst, em)
                    cx.end_phase(fo)
                    cx.mem_es = es
                cur = mdst
            if DBG in ("mod", "modmm", "const", "modtt"):
                dbg = Buf(cx, "dbg", [128, 512], F32)
                cx.op("dve", lambda e: e.tensor_copy(out=dbg[:, 0:96], in_=modT[:].rearrange("p c s -> p (c s)")), [modT.k], [dbg.k])
                cx.op("dve", lambda e: e.tensor_copy(out=dbg[:, 96:112], in_=A2[:].rearrange("p c s -> p (c s)")), [A2.k], [dbg.k])
                cx.dma("sp", [(yT[0, :, 0:512], dbg[:])], [dbg.k], [], dbg.k)
                break
            if do_ffn:
                last = (l == depth - 1)
                dst = yT if last else xres
                with contextlib.ExitStack() as fes:
                    cx.mem_es = fes
                    fo = len(cx.owners)
                    out_tks = ffn_phase(l, cur, dst)
                    cx.end_phase(fo)
                    cx.mem_es = es
                cur = dst
        cx.barrier()
    return nc


def _prep_core_inputs(inputs, depth=4):
    f = np.float32
    sh = {}
    wm = np.asarray(inputs["w_mod"], f)[:depth]
    sh["w_mod"] = np.ascontiguousarray(wm.reshape(depth, KD, 128, NMOD * D).transpose(0, 2, 1, 3))
    bm = np.asarray(inputs["b_mod"], f)[:depth]
    sh["b_mod"] = np.ascontiguousarray(bm.reshape(depth, NMOD * KD, 128).transpose(0, 2, 1))
    ng = np.asarray(inputs["norm_g"], f)[:depth]
    sh["norm_g"] = np.ascontiguousarray(ng.reshape(depth, 4, KD, 128).transpose(0, 3, 1, 2))
    w1 = np.asarray(inputs["w_ff1"], f)[:depth]
    sh["w_ff1"] = np.ascontiguousarray(w1.reshape(depth, KD, 128, DFF).transpose(0, 2, 1, 3)).reshape(depth, 128, KD * DFF)
    w2 = np.asarray(inputs["w_ff2"], f)[:depth]
    sh["w_ff2"] = np.ascontiguousarray(w2.reshape(depth, DFF // 128, 128, D).transpose(0, 2, 1, 3)).reshape(depth, 128, (DFF // 128) * D)
    sh["c_ones"] = np.ones((128, 128), f)
    sh["c_ident"] = np.eye(128, dtype=f)
    b1 = np.zeros((128, 128), f)
    b1[:64, :64] = 1.0
    b1[64:, 64:] = 1.0
    sh["c_blk1"] = b1
    ii = np.arange(128)
    Bk = lambda b: ((ii[:, None] // b) == (ii[None, :] // b)).astype(f)
    sh["c_bmask"] = np.ascontiguousarray(np.stack([Bk(16), Bk(32) - Bk(16), Bk(64) - Bk(32), 1.0 - Bk(64)], axis=1))
    rm = np.ones((128, 512), f)
    rm[:, ::128] = 0.0
    sh["c_rmask"] = rm
    r = np.arange(128)[:, None]
    c = np.arange(128)[None, :]
    sh["c_masks"] = np.ascontiguousarray(np.stack([(r > c), (r <= c), (r < c), (r >= c)], axis=1).astype(f))
    wi = np.asarray(inputs["w_in"], f)[:depth]
    RW = 1920
    wi2 = np.zeros((depth, D, 3584), f)
    wi2[:, :, 0:RW] = wi[:, :, 0:RW]
    wi2[:, :, RW:RW + 1024] = wi[:, :, RW + 512:RW + 1536]
    wi2[:, :, 2944:3456] = wi[:, :, RW:RW + 512]
    wi2[:, :, 3456:3472] = wi[:, :, RW + 1536:RW + 1552]
    sh["w_in"] = np.ascontiguousarray(wi2.reshape(depth, KD, 128, 3584).transpose(0, 2, 1, 3)).reshape(depth, 128, KD * 3584)
    wo = np.asarray(inputs["w_out"], f)[:depth]
    sh["w_out"] = np.ascontiguousarray(wo.reshape(depth, KD, 128, D).transpose(0, 2, 1, 3)).reshape(depth, 128, KD * D)
    pp = np.zeros((depth, 128, 114), f)
    cw = np.asarray(inputs["conv_w"], f)[:depth]
    pp[:, :, 0:40] = cw.reshape(depth, 5, 8, 128).transpose(0, 3, 2, 1).reshape(depth, 128, 40)
    pp[:, :, 40:48] = np.asarray(inputs["conv_b"], f)[:depth].reshape(depth, 8, 128).transpose(0, 2, 1)
    mu = np.asarray(inputs["shift_mu"], f)[:depth]
    pp[:, :, 48:63] = mu[:, 0].reshape(depth, 15, 128).transpose(0, 2, 1)
    pp[:, :, 63:78] = mu[:, 1].reshape(depth, 15, 128).transpose(0, 2, 1)
    pp[:, :, 78:86] = np.asarray(inputs["w0"], f)[:depth].reshape(depth, 8, 128).transpose(0, 2, 1)
    pp[:, :, 86:94] = np.asarray(inputs["a0"], f)[:depth].reshape(depth, 8, 128).transpose(0, 2, 1)
    for off, nm in ((94, "k_k"), (98, "k_a"), (102, "r_k"), (106, "gn_w"), (110, "gn_b")):
        pp[:, :, off:off + 4] = np.asarray(inputs[nm], f)[:depth].reshape(depth, 4, 128).transpose(0, 2, 1)
    sh["pp"] = pp
    bb = np.zeros((depth, 1056), f)
    bb[:, 0:16] = np.asarray(inputs["dt_bias"], f)[:depth].reshape(depth, 16)
    bb[:, 16:32] = np.asarray(inputs["A_log"], f)[:depth].reshape(depth, 16)
    bb[:, 32:544] = np.repeat(np.asarray(inputs["d_skip"], f)[:depth], 64, axis=1)
    bb[:, 544:1056] = np.asarray(inputs["ssm_norm_w"], f)[:depth]
    sh["bb"] = bb
    sh["wup"] = np.ascontiguousarray(np.asarray(inputs["w_up"], f)[:depth].reshape(depth, 128, 512))
    sh["aup"] = np.ascontiguousarray(np.asarray(inputs["a_up"], f)[:depth].reshape(depth, 128, 512))
    sh["gup"] = np.ascontiguousarray(np.asarray(inputs["g_up"], f)[:depth])
    return sh


def _core_tokens(x2seq, c2, flagval):
    f = np.float32
    NT = x2seq.shape[0]
    m = {}
    m["xT"] = np.ascontiguousarray(x2seq.T.reshape(KD, 128, NT)).astype(f)
    m["cT"] = np.ascontiguousarray(c2.reshape(2, KD, 128).transpose(2, 1, 0)).astype(f)
    m["flag"] = np.full((128, 1), flagval, f)
    return m


_NC_CACHE = {}


def kernel(**inputs):
    depth = 4
    NT = 4096
    key = (NT, depth)
    if key not in _NC_CACHE:
        _NC_CACHE[key] = build(NT, depth)
    nc = _NC_CACHE[key]
    shared = _prep_core_inputs(inputs, depth)
    xp = np.asarray(inputs["x_prompt"], np.float32)
    xs = np.asarray(inputs["x_sample"], np.float32)
    cp = np.asarray(inputs["c_prompt"], np.float32)
    cs = np.asarray(inputs["c_sample"], np.float32)
    in_maps = []
    for core in range(8):
        if core < 4:
            x2 = np.concatenate([xp[2 * core], xp[2 * core + 1]], axis=0)
            c2 = np.stack([cp[2 * core], cp[2 * core + 1]])
            m = _core_tokens(x2, c2, 0.0)
        else:
            j = core - 4
            m = _core_tokens(xs[j], np.stack([cs[j], cs[j]]), 1.0)
        m.update(shared)
        in_maps.append(m)
    res = run_bass_kernel_spmd(nc, in_maps, core_ids=list(range(8)))
    yp = np.zeros_like(xp)
    ys = np.zeros_like(xs)
    for core in range(8):
        y = res.results[core]["yT"].reshape(D, NT).T
        if core < 4:
            yp[2 * core] = y[:2048]
            yp[2 * core + 1] = y[2048:]
        else:
            ys[core - 4] = y
    return (yp, ys)
```

```python
import contextlib
import numpy as np
import concourse.bass as bass
import concourse.mybir as mybir
from concourse.bass_utils import run_bass_kernel_spmd

F32 = mybir.dt.float32
BF16 = mybir.dt.bfloat16
AF = mybir.ActivationFunctionType
ALU = mybir.AluOpType

D = 1024
KD = D // 128
DFF = 4096
NMOD = 6
NORM_EPS = 1e-6


class Tk:
    __slots__ = ("w", "r", "dsem", "dcnt", "dkey", "name", "psum")

    def __init__(self, name=""):
        self.w = None
        self.r = {}
        self.dsem = None
        self.dcnt = 0
        self.name = name
        self.psum = False


class Ctx:
    def __init__(self, nc, es):
        self.nc = nc
        self.es = es
        self.E = {"pe": nc.tensor, "act": nc.scalar, "dve": nc.vector, "pool": nc.gpsimd, "sp": nc.sync}
        self.sem = {}
        self.cnt = {}
        for k in self.E:
            self.sem[k] = es.enter_context(nc.semaphore("c_" + k))
            self.cnt[k] = 0
        self.seen = {k: {} for k in self.E}
        self.nsem = 5
        self.ninst = 0
        self.mem_es = es
        self.sem_free = []
        self.owners = []
        self.gsems = []

    def _wait(self, eng, evs):
        need = {}
        for ev in evs:
            if ev is None:
                continue
            key, h, v = ev
            if eng == "pe" and key == "pe":
                continue
            if v > need.get(key, (None, 0))[1]:
                need[key] = (h, v)
        for key, (h, v) in need.items():
            if self.seen[eng].get(key, 0) >= v:
                continue
            self.E[eng].wait_ge(h, v)
            self.seen[eng][key] = v

    def _deps(self, reads, writes):
        evs = []
        for t in reads:
            evs.append(t.w)
            if t.psum:
                evs.extend(t.r.values())
        for t in writes:
            evs.append(t.w)
            evs.extend(t.r.values())
        return evs

    def _record(self, ev, reads, writes):
        key = ev[0]
        for t in reads:
            old = t.r.get(key)
            if old is None or old[2] < ev[2]:
                t.r[key] = ev
        for t in writes:
            t.w = ev
            t.r = {}

    def op(self, eng, fn, reads=(), writes=()):
        self._wait(eng, self._deps(reads, writes))
        ins = fn(self.E[eng])
        self.cnt[eng] += 1
        ins.then_inc(self.sem[eng], 1)
        ev = (eng, self.sem[eng], self.cnt[eng])
        self._record(ev, reads, writes)
        self.ninst += 1
        return ev

    def dma(self, q, pairs, reads, writes, owner, **kw):
        if q == "pool":
            assert len(pairs) == 1
            sem = self.es.enter_context(self.nc.semaphore("g%d" % self.nsem))
            key = "g%d" % self.nsem
            self.nsem += 1
            self._wait(q, self._deps(reads, writes))
            o, i = pairs[0]
            self.E[q].dma_start(out=o, in_=i, **kw).then_inc(sem, 16)
            self.ninst += 1
            ev = (key, sem, 16)
            self._record(ev, reads, writes)
            self.gsems.append(ev)
            return ev
        if owner.dsem is None:
            if self.sem_free:
                owner.dsem, owner.dcnt, owner.dkey = self.sem_free.pop()
            else:
                owner.dsem = self.es.enter_context(self.nc.semaphore("d%d" % self.nsem))
                owner.dkey = "d%d" % self.nsem
                self.nsem += 1
            self.owners.append(owner)
        key = owner.dkey
        evs = self._deps(reads, writes)
        if owner.dcnt:
            evs.append((key, owner.dsem, owner.dcnt))
        self._wait(q, evs)
        for (o, i) in pairs:
            self.E[q].dma_start(out=o, in_=i, **kw).then_inc(owner.dsem, 16)
            owner.dcnt += 16
            self.ninst += 1
        ev = (key, owner.dsem, owner.dcnt)
        self._record(ev, reads, writes)
        return ev

    def barrier(self):
        for eng in self.E:
            evs = [(k, self.sem[k], self.cnt[k]) for k in self.E if self.cnt[k] > 0]
            evs += [(o.dkey, o.dsem, o.dcnt) for o in self.owners if o.dcnt > 0]
            evs += self.gsems
            self._wait(eng, evs)

    def end_phase(self, first_owner):
        self.barrier()
        rel = self.owners[first_owner:]
        self.owners = self.owners[:first_owner]
        for o in rel:
            self.sem_free.append((o.dsem, o.dcnt, o.dkey))
            o.dsem = None

    def wait_all(self, eng, tks):
        evs = []
        for t in tks:
            evs.append(t.w)
            evs.extend(t.r.values())
        self._wait(eng, evs)


class Buf:
    _n = [0]

    def __init__(self, cx, name, shape, dtype, psum=False):
        Buf._n[0] += 1
        name = "%s_%d" % (name, Buf._n[0])
        if psum:
            self.t = cx.mem_es.enter_context(cx.nc.psum_tensor(name, shape, dtype))
        else:
            self.t = cx.mem_es.enter_context(cx.nc.sbuf_tensor(name, shape, dtype))
        self.k = Tk(name)
        self.k.psum = psum

    def __getitem__(self, idx):
        return self.t[idx]


def build(NT=4096, depth=4, do_mix=True, do_ffn=True, debug=False):
    SEG = NT // 2
    nc = bass.Bass("TRN2", target_bir_lowering=False)
    dram_in = {}

    def din(name, shape, dt=F32):
        dram_in[name] = nc.dram_tensor(name, list(shape), dt, kind="ExternalInput").ap()
        return dram_in[name]

    xT = din("xT", [KD, 128, NT])
    cT = din("cT", [128, KD, 2])
    flag = din("flag", [128, 1])
    w_mod = din("w_mod", [depth, 128, KD, NMOD * D])
    b_mod = din("b_mod", [depth, 128, NMOD * KD])
    norm_g = din("norm_g", [depth, 128, 4, KD])
    w_ff1 = din("w_ff1", [depth, 128, KD * DFF])
    w_ff2 = din("w_ff2", [depth, 128, (DFF // 128) * D])
    c_ones = din("c_ones", [128, 128])
    c_ident = din("c_ident", [128, 128])
    c_masks = din("c_masks", [128, 4, 128])
    c_blk1 = din("c_blk1", [128, 128])
    c_bmask = din("c_bmask", [128, 4, 128])
    c_rmask = din("c_rmask", [128, 512])
    NPP, NBB, WIN = 114, 1056, 3584
    w_in = din("w_in", [depth, 128, KD * WIN])
    w_out = din("w_out", [depth, 128, KD * D])
    pp_in = din("pp", [depth, 128, NPP])
    bb_in = din("bb", [depth, NBB])
    wup_in = din("wup", [depth, 128, 512])
    aup_in = din("aup", [depth, 128, 512])
    gup_in = din("gup", [depth, 128, 512])
    NFC = 23
    NC = NT // 128
    uT = nc.dram_tensor("uT", [NFC, 128, NT], F32, kind="Internal").ap()
    z_tok = nc.dram_tensor("z_tok", [NT, 512], F32, kind="Internal").ap()
    dt_tok = nc.dram_tensor("dt_tok", [NT, 16], F32, kind="Internal").ap()
    ysacc = nc.dram_tensor("ysacc", [NT, 512], F32, kind="Internal").ap()
    yracc = nc.dram_tensor("yracc", [4, 128, NT], F32, kind="Internal").ap()
    ymix_d = nc.dram_tensor("ymix_d", [KD, 128, NT], BF16, kind="Internal").ap()
    dbg_out = nc.dram_tensor("dbg_out", [KD, 128, NT], F32, kind="ExternalOutput").ap() if debug else None
    yT = nc.dram_tensor("yT", [KD, 128, NT], F32, kind="ExternalOutput").ap()
    xres = nc.dram_tensor("xres", [KD, 128, NT], F32, kind="Internal").ap()

    with contextlib.ExitStack() as es:
        cx = Ctx(nc, es)
        ones_f = Buf(cx, "ones_f", [128, 128], F32)
        ones_b = Buf(cx, "ones_b", [128, 128], BF16)
        eps_t = Buf(cx, "eps_t", [128, 1], F32)
        sc_t = Buf(cx, "sc_t", [128, KD, 2], F32)
        flag_t = Buf(cx, "flag_t", [128, 1], F32)
        ident_f = Buf(cx, "ident_f", [128, 128], F32)
        ident_b = Buf(cx, "ident_b", [128, 128], BF16)
        mk_f = Buf(cx, "mk_f", [128, 4, 128], F32)
        one_t = Buf(cx, "one_t", [128, 1], F32)
        cx.dma("sp", [(ones_f[:], c_ones), (sc_t[:], cT), (flag_t[:], flag), (ident_f[:], c_ident), (mk_f[:], c_masks)], [],
               [ones_f.k, sc_t.k, flag_t.k, ident_f.k, mk_f.k], ones_f.k)
        cx.op("dve", lambda e: e.tensor_copy(out=ident_b[:], in_=ident_f[:]), [ident_f.k], [ident_b.k])
        blk1_f = Buf(cx, "blk1_f", [128, 128], F32)
        blk1_b = Buf(cx, "blk1_b", [128, 128], BF16)
        blk64_b = Buf(cx, "blk64_b", [128, 128], BF16)
        rmask = Buf(cx, "rmask", [128, 512], F32)
        gneps_t = Buf(cx, "gneps_t", [128, 1], F32)
        bm_f = Buf(cx, "bm_f", [128, 4, 128], F32)
        bm_b = Buf(cx, "bm_b", [128, 4, 128], BF16)
        cx.dma("sp", [(blk1_f[:], c_blk1), (rmask[:], c_rmask), (bm_f[:], c_bmask)], [], [blk1_f.k, rmask.k, bm_f.k], blk1_f.k)
        cx.op("dve", lambda e: e.tensor_copy(out=bm_b[:], in_=bm_f[:]), [bm_f.k], [bm_b.k])
        cx.op("dve", lambda e: e.tensor_copy(out=blk1_b[:], in_=blk1_f[:]), [blk1_f.k], [blk1_b.k])
        cx.op("dve", lambda e: e.tensor_scalar(out=blk64_b[:], in0=blk1_f[:], scalar1=1.0 / 64, scalar2=1.0, op0=ALU.mult, op1=ALU.mult), [blk1_f.k], [blk64_b.k])
        cx.op("dve", lambda e: e.memset(gneps_t[:], 64e-5), [], [gneps_t.k])
        cx.op("dve", lambda e: e.memset(one_t[:], 1.0), [], [one_t.k])
        cx.op("dve", lambda e: e.tensor_copy(out=ones_b[:], in_=ones_f[:]), [ones_f.k], [ones_b.k])
        cx.op("dve", lambda e: e.memset(eps_t[:], NORM_EPS), [], [eps_t.k])
        cx.op("act", lambda e: e.activation(out=sc_t[:], in_=sc_t[:], func=AF.Silu), [sc_t.k], [sc_t.k])

        import os
        DBG = os.environ.get("DBG_STOP", "")
        SKIP = os.environ.get("KSKIP", "").split(",")
        PS = [Buf(cx, "ps%d" % i, [128, 512], F32, psum=True) for i in range(7)]
        PT = Buf(cx, "pt", [128, 1024], BF16, psum=True)

        modT = Buf(cx, "modT", [128, NMOD * KD, 2], F32)
        bmod_t = Buf(cx, "bmod_t", [128, NMOD * KD], F32)
        ng_t = Buf(cx, "ng_t", [128, 4, KD], F32)
        A1 = Buf(cx, "A1", [128, KD, 2], F32)
        G1 = Buf(cx, "G1", [128, KD, 2], F32)
        A2 = Buf(cx, "A2", [128, KD, 2], F32)
        G2 = Buf(cx, "G2", [128, KD, 2], F32)

        def mod_phase(l):
            wm = [Buf(cx, "wm%d" % i, [128, KD, 512], F32) for i in range(2)]
            cx.dma("sp", [(bmod_t[:], b_mod[l]), (ng_t[:], norm_g[l])], [], [bmod_t.k, ng_t.k], bmod_t.k)
            ps = PS[6]
            if DBG == "const":
                return
            for j in range(NMOD * D // 512):
                wb = wm[j % 2]
                cx.dma("sp", [(wb[:], w_mod[l][:, :, j * 512:(j + 1) * 512])], [], [wb.k], wb.k)
                for cc in range(4):
                    col = j * 4 + cc
                    for k in range(KD):
                        cx.op("pe", lambda e, k=k, cc=cc, col=col, wb=wb: e.matmul(
                            ps[:, col * 2:col * 2 + 2], lhsT=wb[:, k, cc * 128:(cc + 1) * 128], rhs=sc_t[:, k, :],
                            start=(k == 0), stop=(k == KD - 1)), [wb.k, sc_t.k], [ps.k])
            if DBG == "modmm":
                return
            cx.op("dve", lambda e: e.tensor_tensor(
                out=modT[:], in0=ps[:, 0:NMOD * KD * 2].rearrange("p (c s) -> p c s", s=2),
                in1=bmod_t[:].unsqueeze(2).to_broadcast([128, NMOD * KD, 2]), op=ALU.add),
                [ps.k, bmod_t.k], [modT.k])

            def mk(dst, mi, gi, plus1):
                gb = ng_t[:, gi, :].unsqueeze(2).to_broadcast([128, KD, 2])
                src = modT[:, mi * KD:(mi + 1) * KD, :]
                if plus1:
                    cx.op("dve", lambda e: e.tensor_scalar(out=dst[:], in0=src, scalar1=1.0, scalar2=None, op0=ALU.add),
                          [modT.k], [dst.k])
                    src = dst[:]
                cx.op("dve", lambda e: e.tensor_tensor(out=dst[:], in0=src, in1=gb, op=ALU.mult), [modT.k, ng_t.k, dst.k], [dst.k])
            if DBG == "modtt":
                return
            mk(A1, 1, 0, True)
            mk(G1, 2, 1, False)
            mk(A2, 4, 2, True)
            mk(G2, 5, 3, False)

        TF = 256
        NKF = DFF // 128

        def ffn_phase(l, src, dst):
            w1b = Buf(cx, "w1b", [128, KD, DFF], BF16)
            w2b = Buf(cx, "w2b", [128, NKF, D], BF16)
            xt = [Buf(cx, "f_xt%d" % i, [128, KD, TF], F32) for i in range(2)]
            sq = Buf(cx, "f_sq", [128, KD, TF], BF16)
            std = Buf(cx, "f_std", [128, TF], F32)
            rstd = Buf(cx, "f_rstd", [128, TF], F32)
            tmp = Buf(cx, "f_tmp", [128, KD, TF], F32)
            hT = Buf(cx, "f_hT", [128, KD, TF], BF16)
            rl = [Buf(cx, "f_rl%d" % i, [128, TF], F32) for i in range(4)]
            aT = Buf(cx, "f_aT", [128, NKF, TF], BF16)
            fT = Buf(cx, "f_fT", [128, KD, TF], F32)
            w1v = w_ff1[l].rearrange("p (a b) -> p a b", b=2048)
            w2v = w_ff2[l].rearrange("p (a b) -> p a b", b=2048)
            cx.dma("pool", [(w1b[:].rearrange("p k c -> p (k c)").rearrange("p (a b) -> p a b", b=2048), w1v)],
                   [], [w1b.k], w1b.k)
            cx.dma("pool", [(w2b[:].rearrange("p k c -> p (k c)").rearrange("p (a b) -> p a b", b=2048), w2v)],
                   [], [w2b.k], w2b.k)
            dk = [Tk("ffn_dst%d" % i) for i in range(NT // TF)]
            if DBG == "ffn_w":
                return dk
            for ti in range(NT // TF):
                t0 = ti * TF
                seg = t0 // SEG
                x = xt[ti % 2]
                cx.dma("sp", [(x[:], src[:, :, t0:t0 + TF].rearrange("k p t -> p k t"))], [], [x.k], x.k)
                cx.op("act", lambda e: e.activation(out=sq[:], in_=x[:], func=AF.Square), [x.k], [sq.k])
                ps = PS[0]
                for k in range(KD):
                    cx.op("pe", lambda e, k=k: e.matmul(ps[:, 0:TF], lhsT=ones_b[:], rhs=sq[:, k, :],
                                                        start=(k == 0), stop=(k == KD - 1)), [ones_b.k, sq.k], [ps.k])
                cx.op("act", lambda e: e.activation(out=std[:], in_=ps[:, 0:TF], func=AF.Ln, bias=eps_t[:], scale=1.0 / D),
                      [ps.k, eps_t.k], [std.k])
                cx.op("act", lambda e: e.activation(out=rstd[:], in_=std[:], func=AF.Exp, scale=-0.5), [std.k], [rstd.k])
                if DBG == "ffn_rstd":
                    return dk
                cx.op("dve", lambda e: e.tensor_tensor(out=tmp[:], in0=x[:], in1=rstd[:].unsqueeze(1).to_broadcast([128, KD, TF]),
                                                       op=ALU.mult), [x.k, rstd.k], [tmp.k])
                if DBG == "ffn_tmp":
                    return dk
                for k in range(KD):
                    eng = "act" if k % 2 == 0 else "pool"
                    if eng == "act":
                        cx.op("act", lambda e, k=k: e.activation(out=hT[:, k, :], in_=tmp[:, k, :], func=AF.Identity,
                                                                 bias=modT[:, 3 * KD + k, seg:seg + 1], scale=A2[:, k, seg:seg + 1]),
                              [tmp.k, modT.k, A2.k], [hT.k])
                    else:
                        cx.op("pool", lambda e, k=k: e.tensor_scalar(out=hT[:, k, :], in0=tmp[:, k, :],
                                                                     scalar1=A2[:, k, seg:seg + 1], scalar2=modT[:, 3 * KD + k, seg:seg + 1],
                                                                     op0=ALU.mult, op1=ALU.add), [tmp.k, modT.k, A2.k], [hT.k])
                if DBG == "ffn_h":
                    return dk
                for c in range(NKF):
                    ps = PS[1 + c % 4]
                    r = rl[c % 4]
                    for k in range(KD):
                        cx.op("pe", lambda e, k=k, c=c, ps=ps: e.matmul(ps[:, 0:TF], lhsT=w1b[:, k, c * 128:(c + 1) * 128], rhs=hT[:, k, :],
                                                                       start=(k == 0), stop=(k == KD - 1)), [w1b.k, hT.k], [ps.k])
                    cx.op("act", lambda e, ps=ps, r=r: e.activation(out=r[:], in_=ps[:, 0:TF], func=AF.Relu), [ps.k], [r.k])
                    if c % 2 == 0:
                        cx.op("dve", lambda e, ps=ps, r=r, c=c: e.tensor_tensor(out=aT[:, c, :], in0=r[:], in1=ps[:, 0:TF], op=ALU.mult),
                              [r.k, ps.k], [aT.k])
                    else:
                        cx.op("pool", lambda e, r=r, c=c: e.tensor_tensor(out=aT[:, c, :], in0=r[:], in1=r[:], op=ALU.mult),
                              [r.k], [aT.k])
                if DBG == "ffn_1":
                    return dk
                for o in range(KD):
                    ps = PS[5 + o % 2]
                    for k in range(NKF):
                        cx.op("pe", lambda e, k=k, o=o, ps=ps: e.matmul(ps[:, 0:TF], lhsT=w2b[:, k, o * 128:(o + 1) * 128], rhs=aT[:, k, :],
                                                                       start=(k == 0), stop=(k == NKF - 1)), [w2b.k, aT.k], [ps.k])
                    if DBG != "ffn_2a":
                        cx.op("act", lambda e, ps=ps, o=o: e.activation(out=sq[:, o, :], in_=ps[:, 0:TF], func=AF.Square), [ps.k], [sq.k])
                    if DBG != "ffn_2b":
                        cx.op("dve", lambda e, ps=ps, o=o: e.tensor_copy(out=fT[:, o, :], in_=ps[:, 0:TF]), [ps.k], [fT.k])
                if DBG in ("ffn_2", "ffn_2a", "ffn_2b"):
                    return dk
                ps = PS[0]
                for k in range(KD):
                    cx.op("pe", lambda e, k=k: e.matmul(ps[:, 0:TF], lhsT=ones_b[:], rhs=sq[:, k, :],
                                                        start=(k == 0), stop=(k == KD - 1)), [ones_b.k, sq.k], [ps.k])
                cx.op("act", lambda e: e.activation(out=std[:], in_=ps[:, 0:TF], func=AF.Ln, bias=eps_t[:], scale=1.0 / D),
                      [ps.k, eps_t.k], [std.k])
                cx.op("act", lambda e: e.activation(out=rstd[:], in_=std[:], func=AF.Exp, scale=-0.5), [std.k], [rstd.k])
                cx.op("dve", lambda e: e.tensor_tensor(out=fT[:], in0=fT[:], in1=rstd[:].unsqueeze(1).to_broadcast([128, KD, TF]),
                                                       op=ALU.mult), [fT.k, rstd.k], [fT.k])
                for k in range(KD):
                    if k % 2 == 0:
                        cx.op("act", lambda e, k=k: e.activation(out=fT[:, k, :], in_=fT[:, k, :], func=AF.Identity,
                                                                 scale=G2[:, k, seg:seg + 1]), [fT.k, G2.k], [fT.k])
                    else:
                        cx.op("pool", lambda e, k=k: e.tensor_scalar(out=fT[:, k, :], in0=fT[:, k, :], scalar1=G2[:, k, seg:seg + 1],
                                                                     scalar2=1.0, op0=ALU.mult, op1=ALU.mult), [fT.k, G2.k], [fT.k])
                cx.op("dve", lambda e: e.tensor_tensor(out=x[:], in0=x[:], in1=fT[:], op=ALU.add), [x.k, fT.k], [x.k])
                if DBG == "ffn_tail":
                    return dk
                cx.dma("sp", [(dst[:, :, t0:t0 + TF].rearrange("k p t -> p k t"), x[:])], [x.k], [dk[ti]], x.k)
            return dk

        TM = 512
        V4 = lambda ap, a: ap.rearrange("p (a b) -> p a b", a=a)

        def norm_tile(x, seg, TT, sq, std, rstd, tmp, hT, Amul, boff):
            cx.op("act", lambda e: e.activation(out=sq[:], in_=x[:], func=AF.Square), [x.k], [sq.k])
            ps = PS[0]
            for k in range(KD):
                cx.op("pe", lambda e, k=k: e.matmul(ps[:, 0:TT], lhsT=ones_b[:], rhs=sq[:, k, :],
                                                    start=(k == 0), stop=(k == KD - 1)), [ones_b.k, sq.k], [ps.k])
            cx.op("act", lambda e: e.activation(out=std[:], in_=ps[:, 0:TT], func=AF.Ln, bias=eps_t[:], scale=1.0 / D),
                  [ps.k, eps_t.k], [std.k])
            cx.op("act", lambda e: e.activation(out=rstd[:], in_=std[:], func=AF.Exp, scale=-0.5), [std.k], [rstd.k])
            cx.op("dve", lambda e: e.tensor_tensor(out=tmp[:], in0=x[:], in1=rstd[:].unsqueeze(1).to_broadcast([128, KD, TT]),
                                                   op=ALU.mult), [x.k, rstd.k], [tmp.k])
            for k in range(KD):
                if k % 2 == 0:
                    cx.op("act", lambda e, k=k: e.activation(out=hT[:, k, :], in_=tmp[:, k, :], func=AF.Identity,
                                                             bias=modT[:, boff * KD + k, seg:seg + 1], scale=Amul[:, k, seg:seg + 1]),
                          [tmp.k, modT.k, Amul.k], [hT.k])
                else:
                    cx.op("pool", lambda e, k=k: e.tensor_scalar(out=hT[:, k, :], in0=tmp[:, k, :],
                                                                 scalar1=Amul[:, k, seg:seg + 1], scalar2=modT[:, boff * KD + k, seg:seg + 1],
                                                                 op0=ALU.mult, op1=ALU.add), [tmp.k, modT.k, Amul.k], [hT.k])

        def post_tile(x, seg, TT, psrc_fn, nk, sq, std, rstd, fT, Gmul):
            for o in range(KD):
                ps = psrc_fn(o)
                cx.op("act", lambda e, ps=ps, o=o: e.activation(out=sq[:, o, :], in_=ps[:, 0:TT], func=AF.Square), [ps.k], [sq.k])
                cx.op("dve", lambda e, ps=ps, o=o: e.tensor_copy(out=fT[:, o, :], in_=ps[:, 0:TT]), [ps.k], [fT.k])
            ps = PS[0]
            for k in range(KD):
                cx.op("pe", lambda e, k=k: e.matmul(ps[:, 0:TT], lhsT=ones_b[:], rhs=sq[:, k, :],
                                                    start=(k == 0), stop=(k == KD - 1)), [ones_b.k, sq.k], [ps.k])
            cx.op("act", lambda e: e.activation(out=std[:], in_=ps[:, 0:TT], func=AF.Ln, bias=eps_t[:], scale=1.0 / D),
                  [ps.k, eps_t.k], [std.k])
            cx.op("act", lambda e: e.activation(out=rstd[:], in_=std[:], func=AF.Exp, scale=-0.5), [std.k], [rstd.k])
            cx.op("dve", lambda e: e.tensor_tensor(out=fT[:], in0=fT[:], in1=rstd[:].unsqueeze(1).to_broadcast([128, KD, TT]),
                                                   op=ALU.mult), [fT.k, rstd.k], [fT.k])
            for k in range(KD):
                if k % 2 == 0:
                    cx.op("act", lambda e, k=k: e.activation(out=fT[:, k, :], in_=fT[:, k, :], func=AF.Identity,
                                                             scale=Gmul[:, k, seg:seg + 1]), [fT.k, Gmul.k], [fT.k])
                else:
                    cx.op("pool", lambda e, k=k: e.tensor_scalar(out=fT[:, k, :], in0=fT[:, k, :], scalar1=Gmul[:, k, seg:seg + 1],
                                                                 scalar2=1.0, op0=ALU.mult, op1=ALU.mult), [fT.k, Gmul.k], [fT.k])
            cx.op("dve", lambda e: e.tensor_tensor(out=x[:], in0=x[:], in1=fT[:], op=ALU.add), [x.k, fT.k], [x.k])

        class MixState:
            pass

        def mix_load_params(l, ms):
            ms.pp = Buf(cx, "pp", [128, NPP], F32)
            ms.bb = Buf(cx, "bb", [128, NBB], F32)
            cx.dma("sp", [(ms.pp[:], pp_in[l]), (ms.bb[:], bb_in[l].partition_broadcast(128))], [], [ms.pp.k, ms.bb.k], ms.pp.k)
            ms.aneg = Buf(cx, "aneg", [128, 16], F32)
            cx.op("act", lambda e: e.activation(out=ms.aneg[:], in_=ms.bb[:, 16:32], func=AF.Exp), [ms.bb.k], [ms.aneg.k])
            cx.op("dve", lambda e: e.tensor_scalar(out=ms.aneg[:], in0=ms.aneg[:], scalar1=-1.0, scalar2=1.0, op0=ALU.mult, op1=ALU.mult),
                  [ms.aneg.k], [ms.aneg.k])
            ms.ymk = [[Tk("ymk%d_%d" % (h, c)) for c in range(NC)] for h in range(2)]
            ms.uk = [[Tk("uT%d_%d" % (fc, ti)) for ti in range(NT // TM)] for fc in range(NFC)]
            ms.zk = [Tk("z%d" % c) for c in range(NC)]
            ms.dtk = [Tk("dt%d" % c) for c in range(NC)]

        def m1_inproj(l, src, ms):
            winb = Buf(cx, "winb", [128, KD, WIN], BF16)
            cx.dma("pool", [(winb[:].rearrange("p k c -> p (k c)").rearrange("p (a b) -> p a b", b=1792),
                             w_in[l].rearrange("p (a b) -> p a b", b=1792))], [], [winb.k], winb.k)
            xt = [Buf(cx, "m1_xt%d" % i, [128, KD, TM], F32) for i in range(2)]
            sq = Buf(cx, "m1_sq", [128, KD, TM], BF16)
            std = Buf(cx, "m1_std", [128, TM], F32)
            rstd = Buf(cx, "m1_rstd", [128, TM], F32)
            tmp = Buf(cx, "m1_tmp", [128, KD, TM], F32)
            hT = Buf(cx, "m1_hT", [128, KD, TM], BF16)
            stg = [Buf(cx, "m1_stg%d" % i, [128, TM], F32) for i in range(4)]
            stz = [Buf(cx, "m1_stz%d" % i, [128, 512 + 16], F32) for i in range(2)]
            ns = 0
            for ti in range(NT // TM):
                t0 = ti * TM
                seg = t0 // SEG
                x = xt[ti % 2]
                cx.dma("sp", [(x[:], src[:, :, t0:t0 + TM].rearrange("k p t -> p k t"))], [], [x.k], x.k)
                norm_tile(x, seg, TM, sq, std, rstd, tmp, hT, A1, 0)
                for fc in range(NFC):
                    ps = PS[1 + fc % 4]
                    for k in range(KD):
                        cx.op("pe", lambda e, k=k, fc=fc, ps=ps: e.matmul(ps[:, 0:TM], lhsT=winb[:, k, fc * 128:(fc + 1) * 128], rhs=hT[:, k, :],
                                                                         start=(k == 0), stop=(k == KD - 1)), [winb.k, hT.k], [ps.k])
                    st = stg[ns % 4]
                    ns += 1
                    if fc % 2 == 0:
                        cx.op("act", lambda e, ps=ps, st=st: e.activation(out=st[:], in_=ps[:, 0:TM], func=AF.Identity), [ps.k], [st.k])
                    else:
                        cx.op("dve", lambda e, ps=ps, st=st: e.tensor_copy(out=st[:], in_=ps[:, 0:TM]), [ps.k], [st.k])
                    cx.dma("sp", [(uT[fc, :, t0:t0 + TM], st[:])], [st.k], [ms.uk[fc][ti]], st.k)
                for sub in range(TM // 128):
                    c = (t0 // 128) + sub
                    psz = PS[5]
                    psd = PS[6]
                    for k in range(KD):
                        cx.op("pe", lambda e, k=k: e.matmul(psz[:, 0:512], lhsT=hT[:, k, sub * 128:(sub + 1) * 128], rhs=winb[:, k, 2944:3456],
                                                            start=(k == 0), stop=(k == KD - 1)), [winb.k, hT.k], [psz.k])
                    for k in range(KD):
                        cx.op("pe", lambda e, k=k: e.matmul(psd[:, 0:16], lhsT=hT[:, k, sub * 128:(sub + 1) * 128], rhs=winb[:, k, 3456:3472],
                                                            start=(k == 0), stop=(k == KD - 1)), [winb.k, hT.k], [psd.k])
                    sz = stz[c % 2]
                    cx.op("act", lambda e: e.activation(out=sz[:, 0:512], in_=psz[:, 0:512], func=AF.Identity), [psz.k], [sz.k])
                    cx.op("dve", lambda e: e.tensor_copy(out=sz[:, 512:528], in_=psd[:, 0:16]), [psd.k, sz.k], [sz.k])
                    cx.dma("sp", [(z_tok[c * 128:(c + 1) * 128, :], sz[:, 0:512]), (dt_tok[c * 128:(c + 1) * 128, :], sz[:, 512:528])],
                           [sz.k], [ms.zk[c], ms.dtk[c]], sz.k)

        def load_win(win, fc, ti, halo, ms):
            t0 = ti * TM
            lo, hi = max(t0 - halo, 0), min(t0 + TM + halo, NT)
            rk = [ms.uk[fc][ti]]
            if ti > 0:
                rk.append(ms.uk[fc][ti - 1])
            if ti < NT // TM - 1:
                rk.append(ms.uk[fc][ti + 1])
            cx.dma("sp", [(win[:, lo - (t0 - halo):hi - (t0 - halo)], uT[fc, :, lo:hi])], rk, [win.k], win.k)
            W = TM + 2 * halo
            if t0 == 0:
                cx.op("pool", lambda e: e.memset(win[:, 0:halo], 0.0), [], [win.k])
            if t0 + TM == NT:
                cx.op("pool", lambda e: e.memset(win[:, W - halo:W], 0.0), [], [win.k])
            if t0 == SEG:
                cx.op("pool", lambda e: e.tensor_scalar(out=win[:, 0:halo], in0=win[:, 0:halo], scalar1=flag_t[:, 0:1], scalar2=1.0,
                                                        op0=ALU.mult, op1=ALU.mult), [win.k, flag_t.k], [win.k])
            if t0 + TM == SEG:
                cx.op("pool", lambda e: e.tensor_scalar(out=win[:, W - halo:W], in0=win[:, W - halo:W], scalar1=flag_t[:, 0:1], scalar2=1.0,
                                                        op0=ALU.mult, op1=ALU.mult), [win.k, flag_t.k], [win.k])

        def m2_ssdprep(l, ms):
            ms.xs_tok = Buf(cx, "xs_tok", [128, NC, 512], BF16)
            ms.B_tok = Buf(cx, "B_tok", [128, NC, 256], BF16)
            ms.BT = Buf(cx, "BT", [128, 2, NT], BF16)
            ms.CT = Buf(cx, "CT", [128, 2, NT], BF16)
            ms.sk = [Tk("ssd%d" % c) for c in range(NC)]
            win = [Buf(cx, "m2_win%d" % i, [128, TM + 4], F32) for i in range(2)]
            acc = [Buf(cx, "m2_acc%d" % i, [128, TM], F32) for i in range(2)]
            cvo = [Buf(cx, "m2_cvo%d" % i, [128, TM], BF16) for i in range(2)]
            n = 0
            for ti in range(NT // TM):
                t0 = ti * TM
                cks = [ms.sk[t0 // 128 + j] for j in range(TM // 128)]
                for j8 in range(8):
                    fc = 15 + j8
                    w = win[n % 2]
                    a = acc[n % 2]
                    o = cvo[n % 2]
                    n += 1
                    load_win(w, fc, ti, 2, ms)
                    cx.op("pool", lambda e: e.tensor_scalar(out=a[:], in0=w[:, 0:TM], scalar1=ms.pp[:, j8 * 5:j8 * 5 + 1], scalar2=1.0,
                                                            op0=ALU.mult, op1=ALU.mult), [w.k, ms.pp.k], [a.k])
                    for j in range(1, 5):
                        cx.op("dve", lambda e, j=j: e.scalar_tensor_tensor(out=a[:], in0=w[:, j:j + TM], scalar=ms.pp[:, j8 * 5 + j:j8 * 5 + j + 1],
                                                                           in1=a[:], op0=ALU.mult, op1=ALU.add), [w.k, ms.pp.k, a.k], [a.k])
                    if j8 < 4 or j8 < 6:
                        cx.op("act", lambda e: e.activation(out=o[:], in_=a[:], func=AF.Silu, bias=ms.pp[:, 40 + j8:41 + j8]), [a.k, ms.pp.k], [o.k])
                        for sub in range(TM // 128):
                            cx.op("pe", lambda e, sub=sub: e.transpose(out=PT[:, sub * 128:(sub + 1) * 128], in_=o[:, sub * 128:(sub + 1) * 128],
                                                                      identity=ident_b[:]), [o.k, ident_b.k], [PT.k])
                        c0 = t0 // 128
                        if j8 < 4:
                            dstv = ms.xs_tok[:, c0:c0 + 4, j8 * 128:(j8 + 1) * 128]
                        else:
                            dstv = ms.B_tok[:, c0:c0 + 4, (j8 - 4) * 128:(j8 - 3) * 128]
                        cx.op("dve", lambda e: e.tensor_copy(out=dstv, in_=V4(PT[:, 0:512], 4)), [PT.k], cks)
                        if j8 >= 4:
                            cx.op("pool", lambda e: e.tensor_copy(out=ms.BT[:, j8 - 4, t0:t0 + TM], in_=o[:]), [o.k], cks)
                    else:
                        cx.op("act", lambda e: e.activation(out=ms.CT[:, j8 - 6, t0:t0 + TM], in_=a[:], func=AF.Silu, bias=ms.pp[:, 40 + j8:41 + j8]),
                              [a.k, ms.pp.k], cks)

        def m3_ssd(l, ms):
            dtt = [Buf(cx, "m3_dtt%d" % i, [128, 16], F32) for i in range(2)]
            zt = [Buf(cx, "m3_zt%d" % i, [128, 512], F32) for i in range(2)]
            yprev = [Buf(cx, "m3_yp%d" % i, [128, 512], F32) for i in range(2)]
            t1 = Buf(cx, "m3_t1", [128, 8], F32)
            dts = Buf(cx, "m3_dts", [128, 8], F32)
            a_t = Buf(cx, "m3_a", [128, 8], F32)
            lmh = Buf(cx, "m3_lmh", [128, 8, 128], F32)
            ex = Buf(cx, "m3_ex", [128, 24], F32)
            Lm = Buf(cx, "m3_Lm", [128, 8, 128], F32)
            CBm = Buf(cx, "m3_CBm", [128, 2, 128], F32)
            G = Buf(cx, "m3_G", [128, 8, 128], BF16)
            xdt = Buf(cx, "m3_xdt", [128, 8, 64], BF16)
            xw = Buf(cx, "m3_xw", [128, 8, 64], BF16)
            ytmp = Buf(cx, "m3_ytmp", [128, 512], F32)
            ycur = [Buf(cx, "m3_ycur%d" % i, [128, 512], F32) for i in range(2)]
            S = Buf(cx, "m3_S", [128, 512], F32)
            Sb = Buf(cx, "m3_Sb", [128, 512], BF16)
            y2 = Buf(cx, "m3_y2", [128, 512], F32)
            szt = Buf(cx, "m3_sz", [128, 512], F32)
            junk = Buf(cx, "m3_junk", [128, 512], BF16)
            ssum = Buf(cx, "m3_ssum", [128, 1], F32)
            ysn = Buf(cx, "m3_ysn", [128, 512], BF16)
            ysT = [Buf(cx, "m3_ysT%d" % i, [128, 4, 128], BF16) for i in range(2)]
            yk = [Tk("ysacc%d" % c) for c in range(NC)]
            psA0, psA1, psB, psC, psY, psO, psS = PS
            for d in range(2):
                m1i, m2i, vi = (0, 1, 1) if d == 0 else (2, 3, 3)
                cx.op("dve", lambda e: e.memset(S[:], 0.0), [], [S.k])
                cx.op("pool", lambda e: e.memset(Sb[:], 0.0), [], [Sb.k])
                order = range(NC) if d == 0 else range(NC - 1, -1, -1)
                for n, c in enumerate(order):
                    tok = slice(c * 128, (c + 1) * 128)
                    dtb = dtt[n % 2]
                    cx.dma("sp", [(dtb[:], dt_tok[tok, :])], [ms.dtk[c]], [dtb.k], dtb.k)
                    if (d == 0 and c == NC // 2) or (d == 1 and c == NC // 2 - 1):
                        cx.op("dve", lambda e: e.tensor_scalar(out=S[:], in0=S[:], scalar1=flag_t[:, 0:1], scalar2=1.0, op0=ALU.mult, op1=ALU.mult),
                              [S.k, flag_t.k], [S.k])
                        cx.op("act", lambda e: e.activation(out=Sb[:], in_=S[:], func=AF.Identity), [S.k], [Sb.k])
                    cx.op("dve", lambda e: e.tensor_tensor(out=t1[:], in0=dtb[:, d * 8:d * 8 + 8], in1=ms.bb[:, d * 8:d * 8 + 8], op=ALU.add),
                          [dtb.k, ms.bb.k], [t1.k])
                    cx.op("act", lambda e: e.activation(out=t1[:], in_=t1[:], func=AF.Exp), [t1.k], [t1.k])
                    cx.op("act", lambda e: e.activation(out=dts[:], in_=t1[:], func=AF.Ln, bias=one_t[:]), [t1.k, one_t.k], [dts.k])
                    cx.op("dve", lambda e: e.tensor_tensor(out=a_t[:], in0=dts[:], in1=ms.aneg[:, d * 8:d * 8 + 8], op=ALU.mult),
                          [dts.k, ms.aneg.k], [a_t.k])
                    for h in range(8):
                        eng = "dve" if h % 2 == 0 else "pool"
                        cx.op(eng, lambda e, h=h: e.tensor_scalar(out=lmh[:, h, :], in0=mk_f[:, m1i, :], scalar1=a_t[:, h:h + 1], scalar2=1.0,
                                                                  op0=ALU.mult, op1=ALU.mult), [mk_f.k, a_t.k], [lmh.k])
                    for h in range(8):
                        psA = psA0 if h < 4 else psA1
                        cx.op("pe", lambda e, h=h, psA=psA: e.matmul(psA[:, (h % 4) * 128:(h % 4 + 1) * 128], lhsT=lmh[:, h, :], rhs=mk_f[:, m2i, :],
                                                                    start=True, stop=True), [lmh.k, mk_f.k], [psA.k])
                    cx.op("pe", lambda e: e.matmul(psB[:, 0:8], lhsT=mk_f[:, m1i, :], rhs=a_t[:], start=True, stop=True), [mk_f.k, a_t.k], [psB.k])
                    cx.op("pe", lambda e: e.matmul(psB[:, 8:16], lhsT=mk_f[:, m2i, :], rhs=a_t[:], start=True, stop=True), [mk_f.k, a_t.k], [psB.k])
                    cx.op("pe", lambda e: e.matmul(psB[:, 16:24], lhsT=ones_f[:], rhs=a_t[:], start=True, stop=True), [ones_f.k, a_t.k], [psB.k])
                    cx.op("act", lambda e: e.activation(out=ex[:], in_=psB[:, 0:24], func=AF.Exp), [psB.k], [ex.k])
                    cx.op("act", lambda e: e.activation(out=Lm[:, 0:4, :], in_=V4(psA0[:, 0:512], 4), func=AF.Exp), [psA0.k], [Lm.k])
                    cx.op("act", lambda e: e.activation(out=Lm[:, 4:8, :], in_=V4(psA1[:, 0:512], 4), func=AF.Exp), [psA1.k, Lm.k], [Lm.k])
                    for g in range(2):
                        cx.op("pe", lambda e, g=g: e.matmul(psC[:, g * 128:(g + 1) * 128], lhsT=ms.BT[:, g, tok], rhs=ms.CT[:, g, tok],
                                                            start=True, stop=True), [ms.sk[c]], [psC.k])
                    cx.op("dve", lambda e: e.tensor_tensor(out=CBm[:], in0=V4(psC[:, 0:256], 2),
                                                           in1=mk_f[:, vi, :].unsqueeze(1).to_broadcast([128, 2, 128]), op=ALU.mult),
                          [psC.k, mk_f.k], [CBm.k])
                    for g in range(2):
                        cx.op("dve", lambda e, g=g: e.tensor_tensor(out=G[:, g * 4:(g + 1) * 4, :], in0=Lm[:, g * 4:(g + 1) * 4, :],
                                                                    in1=CBm[:, g, :].unsqueeze(1).to_broadcast([128, 4, 128]), op=ALU.mult),
                              [Lm.k, CBm.k, G.k], [G.k])
                    cx.op("dve", lambda e: e.tensor_tensor(out=xdt[:], in0=V4(ms.xs_tok[:, c, :], 8),
                                                           in1=dts[:].unsqueeze(2).to_broadcast([128, 8, 64]), op=ALU.mult),
                          [ms.sk[c], dts.k], [xdt.k])
                    cx.op("pool", lambda e: e.tensor_tensor(out=xw[:], in0=xdt[:], in1=ex[:, 0:8].unsqueeze(2).to_broadcast([128, 8, 64]),
                                                            op=ALU.mult), [xdt.k, ex.k], [xw.k])
                    for h in range(8):
                        cx.op("pe", lambda e, h=h: e.matmul(psY[:, h * 64:(h + 1) * 64], lhsT=G[:, h, :], rhs=xdt[:, h, :], start=True, stop=True),
                              [G.k, xdt.k], [psY.k])
                    for g in range(2):
                        cx.op("pe", lambda e, g=g: e.matmul(psO[:, g * 256:(g + 1) * 256], lhsT=ms.CT[:, g, tok], rhs=Sb[:, g * 256:(g + 1) * 256],
                                                            start=True, stop=True), [ms.sk[c], Sb.k], [psO.k])
                    cx.op("dve", lambda e: e.tensor_tensor(out=V4(ytmp[:], 8), in0=V4(psO[:, 0:512], 8),
                                                           in1=ex[:, 8:16].unsqueeze(2).to_broadcast([128, 8, 64]), op=ALU.mult),
                          [psO.k, ex.k], [ytmp.k])
                    yc = ycur[n % 2]
                    cx.op("dve", lambda e: e.tensor_tensor(out=yc[:], in0=ytmp[:], in1=psY[:, 0:512], op=ALU.add), [ytmp.k, psY.k], [yc.k])
                    for g in range(2):
                        cx.op("pe", lambda e, g=g: e.matmul(psS[:, g * 256:(g + 1) * 256], lhsT=ms.B_tok[:, c, g * 128:(g + 1) * 128],
                                                            rhs=xw[:, g * 4:(g + 1) * 4, :].rearrange("p a b -> p (a b)"), start=True, stop=True),
                              [ms.sk[c], xw.k], [psS.k])
                    cx.op("dve", lambda e: e.tensor_tensor(out=V4(S[:], 8), in0=V4(S[:], 8),
                                                           in1=ex[:, 16:24].unsqueeze(2).to_broadcast([128, 8, 64]), op=ALU.mult),
                          [S.k, ex.k, Sb.k], [S.k])
                    cx.op("dve", lambda e: e.tensor_tensor(out=S[:], in0=S[:], in1=psS[:, 0:512], op=ALU.add), [S.k, psS.k], [S.k])
                    cx.op("act", lambda e: e.activation(out=Sb[:], in_=S[:], func=AF.Identity), [S.k], [Sb.k])
                    if d == 0:
                        cx.dma("sp", [(ysacc[tok, :], yc[:])], [yc.k], [yk[c]], yc.k)
                        continue
                    yp = yprev[n % 2]
                    z = zt[n % 2]
                    cx.dma("sp", [(yp[:], ysacc[tok, :])], [yk[c]], [yp.k], yp.k)
                    cx.dma("sp", [(z[:], z_tok[tok, :])], [ms.zk[c]], [z.k], z.k)
                    cx.op("dve", lambda e: e.tensor_tensor(out=yc[:], in0=yc[:], in1=yp[:], op=ALU.add), [yc.k, yp.k], [yc.k])
                    cx.op("pool", lambda e: e.tensor_tensor(out=y2[:], in0=ms.xs_tok[:, c, :], in1=ms.bb[:, 32:544], op=ALU.mult),
                          [ms.sk[c], ms.bb.k], [y2.k])
                    cx.op("dve", lambda e: e.tensor_tensor(out=y2[:], in0=y2[:], in1=yc[:], op=ALU.add), [y2.k, yc.k], [y2.k])
                    cx.op("act", lambda e: e.activation(out=szt[:], in_=z[:], func=AF.Silu), [z.k], [szt.k])
                    cx.op("dve", lambda e: e.tensor_tensor(out=y2[:], in0=y2[:], in1=szt[:], op=ALU.mult), [y2.k, szt.k], [y2.k])
                    cx.op("act", lambda e: e.activation(out=junk[:], in_=y2[:], func=AF.Square, accum_out=ssum[:]), [y2.k], [junk.k, ssum.k])
                    cx.op("act", lambda e: e.activation(out=ssum[:], in_=ssum[:], func=AF.Sqrt, bias=eps_t[:], scale=1.0 / 512), [ssum.k, eps_t.k], [ssum.k])
                    cx.op("dve", lambda e: e.reciprocal(out=ssum[:], in_=ssum[:]), [ssum.k], [ssum.k])
                    cx.op("pool", lambda e: e.tensor_scalar(out=y2[:], in0=y2[:], scalar1=ssum[:, 0:1], scalar2=1.0, op0=ALU.mult, op1=ALU.mult),
                          [y2.k, ssum.k], [y2.k])
                    cx.op("dve", lambda e: e.tensor_tensor(out=ysn[:], in0=y2[:], in1=ms.bb[:, 544:1056], op=ALU.mult), [y2.k, ms.bb.k], [ysn.k])
                    for j in range(4):
                        cx.op("pe", lambda e, j=j: e.transpose(out=PT[:, j * 128:(j + 1) * 128], in_=ysn[:, j * 128:(j + 1) * 128],
                                                              identity=ident_b[:]), [ysn.k, ident_b.k], [PT.k])
                    yst = ysT[n % 2]
                    cx.op("act", lambda e: e.activation(out=yst[:], in_=V4(PT[:, 0:512], 4), func=AF.Identity), [PT.k], [yst.k])
                    cx.dma("sp", [(ymix_d[4:8, :, tok].rearrange("k p t -> p k t"), yst[:])], [yst.k], [ms.ymk[1][c]], yst.k)

        def m5_outproj(l, src, dst, ms):
            woutb = Buf(cx, "woutb", [128, KD, D], BF16)
            cx.dma("pool", [(woutb[:].rearrange("p k c -> p (k c)").rearrange("p (a b) -> p a b", b=2048),
                             w_out[l].rearrange("p (a b) -> p a b", b=2048))], [], [woutb.k], woutb.k)
            xt = [Buf(cx, "m5_xt%d" % i, [128, KD, TM], F32) for i in range(2)]
            sq = Buf(cx, "m5_sq", [128, KD, TM], BF16)
            std = Buf(cx, "m5_std", [128, TM], F32)
            rstd = Buf(cx, "m5_rstd", [128, TM], F32)
            fT = Buf(cx, "m5_fT", [128, KD, TM], F32)
            ymtb = [Buf(cx, "m5_ymt%d" % i, [128, KD, TM], BF16) for i in range(2)]
            dk = [Tk("m5_dst%d" % i) for i in range(NT // TM)]
            for ti in range(NT // TM):
                t0 = ti * TM
                seg = t0 // SEG
                x = xt[ti % 2]
                cx.dma("sp", [(x[:], src[:, :, t0:t0 + TM].rearrange("k p t -> p k t"))], [], [x.k], x.k)
                rks = [ms.ymk[h][t0 // 128 + j] for h in range(2) for j in range(TM // 128)]
                ymt = ymtb[ti % 2]
                cx.dma("sp", [(ymt[:], ymix_d[:, :, t0:t0 + TM].rearrange("k p t -> p k t"))], rks, [ymt.k], ymt.k)
                if debug and l == 0:
                    cx.op("dve", lambda e: e.tensor_copy(out=fT[:], in_=ymt[:]), [ymt.k], [fT.k])
                    cx.dma("sp", [(dbg_out[:, :, t0:t0 + TM].rearrange("k p t -> p k t"), fT[:])], [fT.k], [], fT.k)

                def psrc(o):
                    ps = PS[1 + o % 4]
                    for k in range(KD):
                        cx.op("pe", lambda e, k=k: e.matmul(ps[:, 0:TM], lhsT=woutb[:, k, o * 128:(o + 1) * 128], rhs=ymt[:, k, :],
                                                            start=(k == 0), stop=(k == KD - 1)), [woutb.k, ymt.k], [ps.k])
                    return ps
                post_tile(x, seg, TM, psrc, KD, sq, std, rstd, fT, G1)
                cx.dma("sp", [(dst[:, :, t0:t0 + TM].rearrange("k p t -> p k t"), x[:])], [x.k], [dk[ti]], x.k)


        KAP = 0.6065306597126334

        def m4_rwkv(l, ms):
            NTL = NT // TM
            wupb = Buf(cx, "wupb", [128, 512], BF16)
            aupb = Buf(cx, "aupb", [128, 512], BF16)
            gupb = Buf(cx, "gupb", [128, 512], BF16)
            cx.dma("pool", [(wupb[:], wup_in[l])], [], [wupb.k], wupb.k)
            cx.dma("pool", [(aupb[:], aup_in[l])], [], [aupb.k], aupb.k)
            cx.dma("pool", [(gupb[:], gup_in[l])], [], [gupb.k], gupb.k)
            c0 = Buf(cx, "c0", [128, 15], F32)
            omka = Buf(cx, "omka", [128, 4], F32)
            cx.op("dve", lambda e: e.tensor_tensor(out=c0[:], in0=ms.pp[:, 48:63], in1=ms.pp[:, 63:78], op=ALU.add), [ms.pp.k], [c0.k])
            cx.op("dve", lambda e: e.tensor_scalar(out=c0[:], in0=c0[:], scalar1=-1.0, scalar2=1.0, op0=ALU.mult, op1=ALU.add), [c0.k], [c0.k])
            cx.op("dve", lambda e: e.tensor_scalar(out=omka[:], in0=ms.pp[:, 98:102], scalar1=-1.0, scalar2=1.0, op0=ALU.mult, op1=ALU.add),
                  [ms.pp.k], [omka.k])
            m4m = [Buf(cx, "m4m%d" % d, [128, 4, 128], F32) for d in range(2)]
            for d in range(2):
                for q in range(4):
                    src_i = ((2, 1) if d == 0 else (0, 3))[q % 2]
                    cx.op("pool", lambda e, d=d, q=q, src_i=src_i: e.tensor_copy(out=m4m[d][:, q, :], in_=mk_f[:, src_i, :]), [mk_f.k], [m4m[d].k])
            FB = lambda nm: Buf(cx, nm, [128, TM], F32)
            BB = lambda nm: Buf(cx, nm, [128, TM], BF16)
            win = [Buf(cx, "m4_win%d" % i, [128, TM + 2], F32) for i in range(2)]
            nwin = [0]
            wl, al, gl = FB("m4_wl"), FB("m4_al"), FB("m4_gl")
            twl, alb, sgl = BB("m4_twl"), BB("m4_alb"), BB("m4_sgl")
            rs = [FB("m4_rs%d" % g) for g in range(4)]
            ks = [FB("m4_ks%d" % g) for g in range(4)]
            vs = [FB("m4_vs%d" % g) for g in range(4)]
            AR = [Buf(cx, "m4_AR%d" % g, [128, 4, 256], BF16) for g in range(4)]
            ARh = [Buf(cx, "m4_ARh%d" % h, [128, 4, 256], BF16) for h in range(8)]
            kt = [BB("m4_kt%d" % g) for g in range(4)]
            bt = [BB("m4_bt%d" % g) for g in range(4)]
            v_tok = Buf(cx, "m4_vtok", [128, 4, 512], BF16)
            kh_tok = Buf(cx, "m4_khtok", [128, 4, 512], BF16)
            bh_tok = Buf(cx, "m4_bhtok", [128, 4, 512], BF16)
            gam = Buf(cx, "m4_gam", [128, 4, 4], F32)
            sg, E1, E0, R0, R1 = FB("m4_sg"), FB("m4_E1"), FB("m4_E0"), FB("m4_R0"), FB("m4_R1")
            eN, eP, eA, eH = FB("m4_eN"), FB("m4_eP"), FB("m4_eA"), FB("m4_eH")
            a_s, kkr, nrm, kk, tt, kd, bq = FB("m4_as"), FB("m4_kkr"), FB("m4_nrm"), FB("m4_kk"), FB("m4_tt"), FB("m4_kd"), FB("m4_b")
            sqk, khT, bhT, vT = BB("m4_sqk"), BB("m4_khT"), BB("m4_bhT"), BB("m4_vT")
            SC4 = [Buf(cx, "m4_SC%d" % i, [128, 4, 512], BF16) for i in range(2)]
            XT0 = Buf(cx, "m4_XT0", [128, 4, 128], BF16)
            Xp = [Buf(cx, "m4_X%d" % i, [128, 4, 128], BF16) for i in range(2)]
            XTp = [Buf(cx, "m4_XT%d" % i, [128, 4, 128], BF16) for i in range(2)]
            Rp = [Buf(cx, "m4_R%d" % i, [128, 4, 128], BF16) for i in range(3)]
            TT = [Buf(cx, "m4_TT%d" % i, [128, 4, 128], BF16) for i in range(2)]
            Noff = [Buf(cx, "m4_Noff%d" % i, [128, 4, 128], BF16) for i in range(3)]
            Loff = [Buf(cx, "m4_Loff%d" % i, [128, 4, 128], BF16) for i in range(3)]
            Dp = [Buf(cx, "m4_Dp%d" % i, [128, 4, 128], BF16) for i in range(2)]
            DTp = [Buf(cx, "m4_DTp%d" % i, [128, 4, 128], BF16) for i in range(2)]
            M1b = Buf(cx, "m4_M1b", [128, 4, 128], BF16)
            M2b = Buf(cx, "m4_M2b", [128, 4, 128], BF16)
            Wb = Buf(cx, "m4_Wb", [128, 512], BF16)
            Uneg = Buf(cx, "m4_Uneg", [128, 512], BF16)
            Hf = Buf(cx, "m4_Hf", [128, 4, 128], F32)
            Hb = Buf(cx, "m4_Hb", [128, 4, 128], BF16)
            tmpH = Buf(cx, "m4_tmpH", [128, 4, 128], F32)
            Ytile = Buf(cx, "m4_Ytile", [128, 4, TM], F32)
            yfw = Buf(cx, "m4_yfw", [128, 4, TM], F32)
            yv, ycn, yn2, rk, bon = sg, E1, E0, R0, R1
            ybf, rkb = sqk, khT
            yob = [BB("m4_yo%d" % i) for i in range(2)]
            yrk = [Tk("yracc%d" % ti) for ti in range(NTL)]

            def shift(fc, ti, out):
                w = win[nwin[0] % 2]
                nwin[0] += 1
                load_win(w, fc, ti, 1, ms)
                cx.op("pool", lambda e: e.tensor_scalar(out=out[:], in0=w[:, 1:TM + 1], scalar1=c0[:, fc:fc + 1], scalar2=1.0, op0=ALU.mult, op1=ALU.mult),
                      [w.k, c0.k], [out.k])
                cx.op("dve", lambda e: e.scalar_tensor_tensor(out=out[:], in0=w[:, 0:TM], scalar=ms.pp[:, 48 + fc:49 + fc], in1=out[:],
                                                              op0=ALU.mult, op1=ALU.add), [w.k, ms.pp.k, out.k], [out.k])
                cx.op("dve", lambda e: e.scalar_tensor_tensor(out=out[:], in0=w[:, 2:TM + 2], scalar=ms.pp[:, 63 + fc:64 + fc], in1=out[:],
                                                              op0=ALU.mult, op1=ALU.add), [w.k, ms.pp.k, out.k], [out.k])

            def tt_(eng, out, a, b, op, rd, wr):
                cx.op(eng, lambda e: e.tensor_tensor(out=out, in0=a, in1=b, op=op), rd, wr)

            def prep(d, ti):
                shift(12, ti, wl)
                cx.op("act", lambda e: e.activation(out=twl[:], in_=wl[:], func=AF.Tanh), [wl.k], [twl.k])
                shift(13, ti, al)
                cx.op("act", lambda e: e.activation(out=alb[:], in_=al[:], func=AF.Identity), [al.k], [alb.k])
                if d == 1:
                    shift(14, ti, gl)
                    cx.op("act", lambda e: e.activation(out=sgl[:], in_=gl[:], func=AF.Sigmoid), [gl.k], [sgl.k])
                dp = slice(d * 64, d * 64 + 64)
                import os as _os
                for g in (range(3, -1, -1) if _os.environ.get("GREV") else range(4)):
                    gs = slice(g * 128, (g + 1) * 128)
                    shift(g, ti, rs[g])
                    shift(4 + g, ti, ks[g])
                    shift(8 + g, ti, vs[g])
                    cx.op("pe", lambda e: e.matmul(PS[0][:, 0:TM], lhsT=wupb[dp, gs], rhs=twl[dp, :], start=True, stop=True), [wupb.k, twl.k], [PS[0].k])
                    cx.op("act", lambda e: e.activation(out=sg[:], in_=PS[0][:, 0:TM], func=AF.Sigmoid, bias=ms.pp[:, 78 + d * 4 + g:79 + d * 4 + g]),
                          [PS[0].k, ms.pp.k], [sg.k])
                    cx.op("dve", lambda e: e.tensor_tensor_scan(out=E1[:], data0=rmask[:], data1=sg[:], initial=0.0, op0=ALU.mult, op1=ALU.add),
                          [rmask.k, sg.k], [E1.k])
                    tt_("pool", E0[:], E1[:], sg[:], ALU.subtract, [E1.k, sg.k], [E0.k])
                    tt_("dve", V4(R0[:], 4), V4(E1[:], 4)[:, :, 127:128].to_broadcast([128, 4, 128]), V4(E1[:], 4), ALU.subtract, [E1.k], [R0.k])
                    tt_("pool", R1[:], R0[:], sg[:], ALU.add, [R0.k, sg.k], [R1.k])
                    X1, X0e, Y0 = (E1, E0, R0) if d == 0 else (R1, R0, E0)
                    cx.op("act", lambda e: e.activation(out=eN[:], in_=X1[:], func=AF.Exp, scale=-KAP), [X1.k], [eN.k])
                    cx.op("act", lambda e: e.activation(out=eP[:], in_=X1[:], func=AF.Exp, scale=KAP), [X1.k], [eP.k])
                    cx.op("act", lambda e: e.activation(out=eA[:], in_=X0e[:], func=AF.Exp, scale=-KAP), [X0e.k], [eA.k])
                    cx.op("act", lambda e: e.activation(out=eH[:], in_=Y0[:], func=AF.Exp, scale=-KAP), [Y0.k], [eH.k])
                    cx.op("act", lambda e: e.activation(out=gam[:, g, :], in_=V4(E1[:], 4)[:, :, 127], func=AF.Exp, scale=-KAP), [E1.k], [gam.k])
                    cx.op("pe", lambda e: e.matmul(PS[1][:, 0:TM], lhsT=aupb[dp, gs], rhs=alb[dp, :], start=True, stop=True), [aupb.k, alb.k], [PS[1].k])
                    cx.op("act", lambda e: e.activation(out=a_s[:], in_=PS[1][:, 0:TM], func=AF.Sigmoid, bias=ms.pp[:, 86 + d * 4 + g:87 + d * 4 + g]),
                          [PS[1].k, ms.pp.k], [a_s.k])
                    cx.op("pool", lambda e: e.tensor_scalar(out=kkr[:], in0=ks[g][:], scalar1=ms.pp[:, 94 + g:95 + g], scalar2=1.0, op0=ALU.mult, op1=ALU.mult),
                          [ks[g].k, ms.pp.k], [kkr.k])
                    cx.op("act", lambda e: e.activation(out=sqk[:], in_=kkr[:], func=AF.Square), [kkr.k], [sqk.k])
                    cx.op("pe", lambda e: e.matmul(PS[2][:, 0:TM], lhsT=blk1_b[:], rhs=sqk[:], start=True, stop=True), [blk1_b.k, sqk.k], [PS[2].k])
                    cx.op("dve", lambda e: e.tensor_scalar(out=nrm[:], in0=PS[2][:, 0:TM], scalar1=2.0 ** -60, scalar2=None, op0=ALU.max), [PS[2].k], [nrm.k])
                    cx.op("act", lambda e: e.activation(out=nrm[:], in_=nrm[:], func=AF.Ln), [nrm.k], [nrm.k])
                    cx.op("act", lambda e: e.activation(out=nrm[:], in_=nrm[:], func=AF.Exp, scale=-0.5), [nrm.k], [nrm.k])
                    tt_("dve", kk[:], kkr[:], nrm[:], ALU.mult, [kkr.k, nrm.k], [kk.k])
                    cx.op("pool", lambda e: e.tensor_scalar(out=tt[:], in0=a_s[:], scalar1=ms.pp[:, 98 + g:99 + g], scalar2=omka[:, g:g + 1],
                                                            op0=ALU.mult, op1=ALU.add), [a_s.k, ms.pp.k, omka.k], [tt.k])
                    tt_("dve", kd[:], ks[g][:], tt[:], ALU.mult, [ks[g].k, tt.k], [kd.k])
                    tt_("pool", bq[:], kk[:], a_s[:], ALU.mult, [kk.k, a_s.k], [bq.k])
                    tt_("dve", AR[g][:, :, 0:128], V4(kk[:], 4), V4(eA[:], 4), ALU.mult, [kk.k, eA.k], [AR[g].k])
                    tt_("pool", AR[g][:, :, 128:256], V4(rs[g][:], 4), V4(eN[:], 4), ALU.mult, [rs[g].k, eN.k, AR[g].k], [AR[g].k])
                    for hl in range(2):
                        eng = "dve" if hl == 0 else "pool"
                        cx.op(eng, lambda e, hl=hl: e.tensor_scalar(out=ARh[2 * g + hl][:], in0=AR[g][:], scalar1=blk1_f[:, hl * 64:hl * 64 + 1], scalar2=1.0,
                                                                    op0=ALU.mult, op1=ALU.mult), [AR[g].k, blk1_f.k], [ARh[2 * g + hl].k])
                    tt_("dve", kt[g][:], kd[:], eP[:], ALU.mult, [kd.k, eP.k], [kt[g].k])
                    tt_("pool", bt[g][:], bq[:], eP[:], ALU.mult, [bq.k, eP.k], [bt[g].k])
                    tt_("dve", khT[:], kd[:], eH[:], ALU.mult, [kd.k, eH.k], [khT.k])
                    tt_("pool", bhT[:], bq[:], eH[:], ALU.mult, [bq.k, eH.k], [bhT.k])
                    cx.op("act", lambda e: e.activation(out=vT[:], in_=vs[g][:], func=AF.Identity), [vs[g].k], [vT.k])
                    for (srcT, dstk) in ((vT, v_tok), (khT, kh_tok), (bhT, bh_tok)):
                        for j in range(4):
                            cx.op("pe", lambda e, j=j, srcT=srcT: e.transpose(out=PT[:, j * 128:(j + 1) * 128], in_=srcT[:, j * 128:(j + 1) * 128],
                                                                             identity=ident_b[:]), [srcT.k, ident_b.k], [PT.k])
                        cx.op("dve", lambda e, dstk=dstk: e.tensor_copy(out=dstk[:, :, gs], in_=V4(PT[:, 0:512], 4)), [PT.k, dstk.k], [dstk.k])

            def scan(d, ti, j):
                cs = slice(j * 128, (j + 1) * 128)
                mL = 0 if d == 0 else 2
                c = ti * 4 + j
                if (d == 0 and c == NC // 2) or (d == 1 and c == NC // 2 - 1):
                    cx.op("dve", lambda e: e.tensor_scalar(out=Hf[:], in0=Hf[:], scalar1=flag_t[:, 0:1], scalar2=1.0, op0=ALU.mult, op1=ALU.mult),
                          [Hf.k, flag_t.k], [Hf.k])
                    cx.op("act", lambda e: e.activation(out=Hb[:], in_=Hf[:], func=AF.Identity), [Hf.k], [Hb.k])
                for hg in range(2):
                    sc = SC4[hg]
                    for hh in range(4):
                        h = hg * 4 + hh
                        g = h // 2
                        po = slice((h % 2) * 64, (h % 2) * 64 + 64)
                        ps = PS[hh]
                        cx.op("pe", lambda e: e.matmul(ps[:, 0:256], lhsT=kt[g][:, cs], rhs=ARh[h][:, j, :], start=True, stop=True),
                              [kt[g].k, ARh[h].k], [ps.k])
                        cx.op("pe", lambda e: e.matmul(ps[:, 256:512], lhsT=bt[g][:, cs], rhs=ARh[h][:, j, :], start=True, stop=True),
                              [bt[g].k, ARh[h].k], [ps.k])
                        cx.op("pe", lambda e: e.matmul(PS[4][:, hh * 128:(hh + 1) * 128], lhsT=ARh[h][:, j, 0:128], rhs=bt[g][:, cs], start=True, stop=True),
                              [bt[g].k, ARh[h].k], [PS[4].k])
                        cx.op("dve", lambda e: e.tensor_tensor(out=sc[:, hh, :], in0=ps[:, 0:512], in1=m4m[d][:].rearrange("p a b -> p (a b)"), op=ALU.mult),
                              [ps.k, m4m[d].k, sc.k], [sc.k])
                    cx.op("dve", lambda e: e.tensor_tensor(out=XT0[:], in0=V4(PS[4][:, 0:512], 4),
                                                           in1=mk_f[:, mL, :].unsqueeze(1).to_broadcast([128, 4, 128]), op=ALU.mult),
                          [PS[4].k, mk_f.k], [XT0.k])
                    P1, P2, P3 = PS[5], PS[6], PS[4]
                    bmb = lambda q: bm_b[:, q, :].unsqueeze(1).to_broadcast([128, 4, 128])
                    N0, L0 = Xp[0], XTp[0]
                    cx.op("pool", lambda e: e.tensor_tensor(out=N0[:], in0=sc[:, :, 256:384], in1=bmb(0), op=ALU.mult), [sc.k, bm_b.k], [N0.k])
                    cx.op("dve", lambda e: e.tensor_tensor(out=L0[:], in0=XT0[:], in1=bmb(0), op=ALU.mult), [XT0.k, bm_b.k], [L0.k])
                    for q in range(3):
                        cx.op("pool", lambda e, q=q: e.tensor_tensor(out=Noff[q][:], in0=sc[:, :, 256:384], in1=bmb(q + 1), op=ALU.mult),
                              [sc.k, bm_b.k], [Noff[q].k])
                        cx.op("dve", lambda e, q=q: e.tensor_tensor(out=Loff[q][:], in0=XT0[:], in1=bmb(q + 1), op=ALU.mult),
                              [XT0.k, bm_b.k], [Loff[q].k])
                    R = Rp[2]
                    cx.op("pool", lambda e: e.tensor_tensor(out=R[:], in0=ident_b[:].unsqueeze(1).to_broadcast([128, 4, 128]), in1=N0[:],
                                                            op=ALU.subtract), [ident_b.k, N0.k], [R.k])
                    Xc, XTc = N0, L0
                    for k in range(3):
                        XTn = XTp[(k + 1) % 2]
                        Xn = Xp[(k + 1) % 2]
                        for hh in range(4):
                            cx.op("pe", lambda e, hh=hh: e.matmul(P2[:, hh * 128:(hh + 1) * 128], lhsT=Xc[:, hh, :], rhs=XTc[:, hh, :], start=True, stop=True),
                                  [Xc.k, XTc.k], [P2.k])
                        if k < 2:
                            for hh in range(4):
                                cx.op("pe", lambda e, hh=hh: e.matmul(P1[:, hh * 128:(hh + 1) * 128], lhsT=XTc[:, hh, :], rhs=Xc[:, hh, :], start=True, stop=True),
                                      [Xc.k, XTc.k], [P1.k])
                        cx.op("dve", lambda e: e.tensor_copy(out=XTn[:], in_=V4(P2[:, 0:512], 4)), [P2.k], [XTn.k])
                        if k < 2:
                            cx.op("act", lambda e: e.activation(out=Xn[:], in_=V4(P1[:, 0:512], 4), func=AF.Identity), [P1.k], [Xn.k])
                        for hh in range(4):
                            cx.op("pe", lambda e, hh=hh: e.matmul(P3[:, hh * 128:(hh + 1) * 128], lhsT=XTn[:, hh, :], rhs=R[:, hh, :], start=True, stop=False),
                                  [XTn.k, R.k], [P3.k])
                            cx.op("pe", lambda e, hh=hh: e.matmul(P3[:, hh * 128:(hh + 1) * 128], lhsT=ident_b[:], rhs=R[:, hh, :], start=False, stop=True),
                                  [ident_b.k, R.k], [P3.k])
                        Rn = Rp[k % 2]
                        cx.op("act", lambda e: e.activation(out=Rn[:], in_=V4(P3[:, 0:512], 4), func=AF.Identity), [P3.k], [Rn.k])
                        R = Rn
                        Xc, XTc = Xn, XTn
                    DT = R
                    Dn = Dp[0]
                    for hh in range(4):
                        cx.op("pe", lambda e, hh=hh: e.transpose(out=PT[:, hh * 128:(hh + 1) * 128], in_=DT[:, hh, :], identity=ident_b[:]),
                              [DT.k, ident_b.k], [PT.k])
                    cx.op("dve", lambda e: e.tensor_copy(out=Dn[:], in_=V4(PT[:, 0:512], 4)), [PT.k], [Dn.k])
                    for q in range(3):
                        last = (q == 2)
                        DTn = TT[hg] if last else DTp[q % 2]
                        Dnn = Dp[(q + 1) % 2]
                        for hh in range(4):
                            cx.op("pe", lambda e, hh=hh: e.matmul(P1[:, hh * 128:(hh + 1) * 128], lhsT=Loff[q][:, hh, :], rhs=DT[:, hh, :], start=True, stop=True),
                                  [Loff[q].k, DT.k], [P1.k])
                        cx.op("act", lambda e: e.activation(out=M1b[:], in_=V4(P1[:, 0:512], 4), func=AF.Identity), [P1.k], [M1b.k])
                        if not last:
                            for hh in range(4):
                                cx.op("pe", lambda e, hh=hh: e.matmul(P2[:, hh * 128:(hh + 1) * 128], lhsT=Noff[q][:, hh, :], rhs=Dn[:, hh, :], start=True, stop=True),
                                      [Noff[q].k, Dn.k], [P2.k])
                            cx.op("dve", lambda e: e.tensor_copy(out=M2b[:], in_=V4(P2[:, 0:512], 4)), [P2.k], [M2b.k])
                        for hh in range(4):
                            cx.op("pe", lambda e, hh=hh: e.matmul(P3[:, hh * 128:(hh + 1) * 128], lhsT=Dn[:, hh, :], rhs=M1b[:, hh, :], start=True, stop=True),
                                  [Dn.k, M1b.k], [P3.k])
                        cx.op("dve", lambda e: e.tensor_tensor(out=DTn[:], in0=DT[:], in1=V4(P3[:, 0:512], 4), op=ALU.subtract), [DT.k, P3.k], [DTn.k])
                        if not last:
                            for hh in range(4):
                                cx.op("pe", lambda e, hh=hh: e.matmul(P1[:, hh * 128:(hh + 1) * 128], lhsT=DT[:, hh, :], rhs=M2b[:, hh, :], start=True, stop=True),
                                      [DT.k, M2b.k], [P1.k])
                            cx.op("dve", lambda e: e.tensor_tensor(out=Dnn[:], in0=Dn[:], in1=V4(P1[:, 0:512], 4), op=ALU.subtract), [Dn.k, P1.k], [Dnn.k])
                            Dn = Dnn
                        DT = DTn
                psW, psU, psY, psH = PS[0], PS[1], PS[2], PS[3]
                for g in range(4):
                    cx.op("pe", lambda e: e.matmul(psW[:, g * 128:(g + 1) * 128], lhsT=AR[g][:, j, 0:128], rhs=Hb[:, g, :], start=True, stop=False),
                          [AR[g].k, Hb.k], [psW.k])
                    for hl in range(2):
                        h = 2 * g + hl
                        hg, hh = divmod(h, 4)
                        cx.op("pe", lambda e: e.matmul(psW[:, h * 64:(h + 1) * 64], lhsT=SC4[hg][:, hh, 0:128], rhs=v_tok[:, j, h * 64:(h + 1) * 64],
                                                       start=False, stop=(hl == 1)), [SC4[hg].k, v_tok.k], [psW.k])
                cx.op("act", lambda e: e.activation(out=Wb[:], in_=psW[:, 0:512], func=AF.Identity), [psW.k], [Wb.k])
                for h in range(8):
                    hg, hh = divmod(h, 4)
                    cx.op("pe", lambda e: e.matmul(psU[:, h * 64:(h + 1) * 64], lhsT=TT[hg][:, hh, :], rhs=Wb[:, h * 64:(h + 1) * 64], start=True, stop=True),
                          [TT[hg].k, Wb.k], [psU.k])
                cx.op("act", lambda e: e.activation(out=Uneg[:], in_=psU[:, 0:512], func=AF.Identity, scale=-1.0), [psU.k], [Uneg.k])
                for g in range(4):
                    cx.op("pe", lambda e: e.matmul(psY[:, g * 128:(g + 1) * 128], lhsT=Hb[:, g, :], rhs=AR[g][:, j, 128:256], start=True, stop=False),
                          [AR[g].k, Hb.k], [psY.k])
                    for hl in range(2):
                        h = 2 * g + hl
                        hg, hh = divmod(h, 4)
                        po = slice(hl * 64, hl * 64 + 64)
                        cx.op("pe", lambda e: e.matmul(psY[po, g * 128:(g + 1) * 128], lhsT=v_tok[:, j, h * 64:(h + 1) * 64], rhs=SC4[hg][:, hh, 128:256],
                                                       start=False, stop=False), [SC4[hg].k, v_tok.k], [psY.k])
                        cx.op("pe", lambda e: e.matmul(psY[po, g * 128:(g + 1) * 128], lhsT=Uneg[:, h * 64:(h + 1) * 64], rhs=SC4[hg][:, hh, 384:512],
                                                       start=False, stop=True), [SC4[hg].k, Uneg.k], [psY.k])
                cx.op("act", lambda e: e.activation(out=Ytile[:, :, cs], in_=V4(psY[:, 0:512], 4), func=AF.Identity), [psY.k, Ytile.k], [Ytile.k])
                for g in range(4):
                    gs = slice(g * 128, (g + 1) * 128)
                    cx.op("pe", lambda e: e.matmul(psH[:, gs], lhsT=kh_tok[:, j, gs], rhs=v_tok[:, j, gs], start=True, stop=False),
                          [kh_tok.k, v_tok.k], [psH.k])
                    cx.op("pe", lambda e: e.matmul(psH[:, gs], lhsT=bh_tok[:, j, gs], rhs=Uneg[:, gs], start=False, stop=True),
                          [bh_tok.k, Uneg.k], [psH.k])
                cx.op("dve", lambda e: e.tensor_tensor(out=tmpH[:], in0=V4(psH[:, 0:512], 4), in1=blk1_f[:].unsqueeze(1).to_broadcast([128, 4, 128]),
                                                       op=ALU.mult), [psH.k, blk1_f.k], [tmpH.k])
                cx.op("dve", lambda e: e.tensor_tensor(out=Hf[:], in0=Hf[:], in1=gam[:, :, j:j + 1].to_broadcast([128, 4, 128]), op=ALU.mult),
                      [Hf.k, gam.k, Hb.k], [Hf.k])
                cx.op("dve", lambda e: e.tensor_tensor(out=Hf[:], in0=Hf[:], in1=tmpH[:], op=ALU.add), [Hf.k, tmpH.k], [Hf.k])
                cx.op("act", lambda e: e.activation(out=Hb[:], in_=Hf[:], func=AF.Identity), [Hf.k], [Hb.k])

            def finalize(ti):
                t0 = ti * TM
                cx.dma("sp", [(yfw[:], yracc[:, :, t0:t0 + TM].rearrange("g p t -> p g t"))], [yrk[ti]], [yfw.k], yfw.k)
                cks = [ms.ymk[0][t0 // 128 + jj] for jj in range(TM // 128)]
                for g in range(4):
                    gs = slice(g * 128, (g + 1) * 128)
                    tt_("dve", yv[:], Ytile[:, g, :], yfw[:, g, :], ALU.add, [Ytile.k, yfw.k], [yv.k])
                    cx.op("act", lambda e: e.activation(out=ybf[:], in_=yv[:], func=AF.Identity), [yv.k], [ybf.k])
                    cx.op("pe", lambda e: e.matmul(PS[5][:, 0:TM], lhsT=blk64_b[:], rhs=ybf[:], start=True, stop=True), [blk64_b.k, ybf.k], [PS[5].k])
                    tt_("dve", ycn[:], yv[:], PS[5][:, 0:TM], ALU.subtract, [yv.k, PS[5].k], [ycn.k])
                    cx.op("act", lambda e: e.activation(out=ybf[:], in_=ycn[:], func=AF.Square), [ycn.k], [ybf.k])
                    cx.op("pe", lambda e: e.matmul(PS[6][:, 0:TM], lhsT=blk64_b[:], rhs=ybf[:], start=True, stop=True), [blk64_b.k, ybf.k], [PS[6].k])
                    cx.op("act", lambda e: e.activation(out=yn2[:], in_=PS[6][:, 0:TM], func=AF.Ln, bias=gneps_t[:]), [PS[6].k, gneps_t.k], [yn2.k])
                    cx.op("act", lambda e: e.activation(out=yn2[:], in_=yn2[:], func=AF.Exp, scale=-0.5), [yn2.k], [yn2.k])
                    tt_("dve", ycn[:], ycn[:], yn2[:], ALU.mult, [ycn.k, yn2.k], [ycn.k])
                    cx.op("pool", lambda e: e.tensor_scalar(out=yn2[:], in0=ycn[:], scalar1=ms.pp[:, 106 + g:107 + g], scalar2=ms.pp[:, 110 + g:111 + g],
                                                            op0=ALU.mult, op1=ALU.add), [ycn.k, ms.pp.k], [yn2.k])
                    cx.op("pool", lambda e: e.tensor_scalar(out=rk[:], in0=rs[g][:], scalar1=ms.pp[:, 102 + g:103 + g], scalar2=1.0, op0=ALU.mult, op1=ALU.mult),
                          [rs[g].k, ms.pp.k], [rk.k])
                    tt_("dve", rk[:], rk[:], ks[g][:], ALU.mult, [rk.k, ks[g].k], [rk.k])
                    cx.op("act", lambda e: e.activation(out=rkb[:], in_=rk[:], func=AF.Identity), [rk.k], [rkb.k])
                    cx.op("pe", lambda e: e.matmul(PS[5][:, 0:TM], lhsT=blk1_b[:], rhs=rkb[:], start=True, stop=True), [blk1_b.k, rkb.k], [PS[5].k])
                    tt_("dve", bon[:], PS[5][:, 0:TM], vs[g][:], ALU.mult, [PS[5].k, vs[g].k], [bon.k])
                    tt_("pool", yn2[:], yn2[:], bon[:], ALU.add, [yn2.k, bon.k], [yn2.k])
                    cx.op("pe", lambda e: e.matmul(PS[6][:, 0:TM], lhsT=gupb[:, gs], rhs=sgl[:], start=True, stop=True), [gupb.k, sgl.k], [PS[6].k])
                    yo = yob[g % 2]
                    cx.op("dve", lambda e: e.tensor_tensor(out=yo[:], in0=yn2[:], in1=PS[6][:, 0:TM], op=ALU.mult), [yn2.k, PS[6].k], [yo.k])
                    cx.dma("sp", [(ymix_d[g, :, t0:t0 + TM], yo[:])], [yo.k], cks, yo.k)

            for d in range(2):
                cx.op("dve", lambda e: e.memset(Hf[:], 0.0), [], [Hf.k])
                cx.op("pool", lambda e: e.memset(Hb[:], 0.0), [], [Hb.k])
                tiles = range(NTL) if d == 0 else range(NTL - 1, -1, -1)
                for ti in tiles:
                    prep(d, ti)
                    for j in (range(4) if d == 0 else range(3, -1, -1)):
                        if "m4scan" not in SKIP:
                            scan(d, ti, j)
                    if DBG == "rdbg" and d == 0 and ti == 0:
                        cx.op("dve", lambda e: e.tensor_copy(out=yfw[:, 0, :], in_=SC4[1][:, 0, :]), [SC4[1].k], [yfw.k])
                        cx.op("dve", lambda e: e.tensor_copy(out=yfw[:, 1, :], in_=SC4[1][:, 2, :]), [SC4[1].k], [yfw.k])
                        cx.op("dve", lambda e: e.tensor_copy(out=yfw[:, 2, :], in_=TT[1][:].rearrange("p a b -> p (a b)")), [TT[1].k], [yfw.k])
                        cx.op("dve", lambda e: e.tensor_copy(out=yfw[:, 3, :], in_=Wb[:]), [Wb.k], [yfw.k])
                        for slot in range(4):
                            cx.dma("sp", [(dbg_out[slot, :, 0:TM], yfw[:, slot, :])], [yfw.k], [], yfw.k)
                        cx.op("dve", lambda e: e.tensor_copy(out=yfw[:, 0, :], in_=Uneg[:]), [Uneg.k], [yfw.k])
                        cx.op("dve", lambda e: e.tensor_copy(out=yfw[:, 1, :], in_=v_tok[:, 3, :]), [v_tok.k], [yfw.k])
                        cx.op("dve", lambda e: e.tensor_copy(out=yfw[:, 2, :], in_=kt[2][:]), [kt[2].k], [yfw.k])
                        cx.op("dve", lambda e: e.tensor_copy(out=yfw[:, 3, 0:256], in_=AR[2][:, 3, :]), [AR[2].k], [yfw.k])
                        for slot in range(4):
                            cx.dma("sp", [(dbg_out[4 + slot, :, 0:(TM if slot < 3 else 256)], yfw[:, slot, 0:(TM if slot < 3 else 256)])], [yfw.k], [], yfw.k)
                        return
                    if d == 0:
                        cx.dma("sp", [(yracc[:, :, ti * TM:(ti + 1) * TM].rearrange("g p t -> p g t"), Ytile[:])], [Ytile.k], [yrk[ti]], Ytile.k)
                    else:
                        finalize(ti)


        def mixer_phase(l, src, dst, em):
            ms = MixState()
            mix_load_params(l, ms)
            with contextlib.ExitStack() as e1:
                cx.mem_es = e1
                fo = len(cx.owners)
                if "m1" not in SKIP:
                    m1_inproj(l, src, ms)
                cx.end_phase(fo)
            cx.mem_es = em
            with contextlib.ExitStack() as e2:
                cx.mem_es = e2
                fo = len(cx.owners)
                if "m2" not in SKIP:
                    m2_ssdprep(l, ms)
                if "m3" not in SKIP and "m2" not in SKIP:
                    m3_ssd(l, ms)
                cx.end_phase(fo)
            cx.mem_es = em
            with contextlib.ExitStack() as e4:
                cx.mem_es = e4
                fo = len(cx.owners)
                if "m4" not in SKIP:
                    m4_rwkv(l, ms)
                cx.end_phase(fo)
            cx.mem_es = em
            with contextlib.ExitStack() as e5:
                cx.mem_es = e5
                fo = len(cx.owners)
                if DBG != "rdbg":
                    m5_outproj(l, src, dst, ms)
                cx.end_phase(fo)
            cx.mem_es = em

        out_tks = []
        cur = xT
        for l in range(depth):
            with contextlib.ExitStack() as fes:
                cx.mem_es = fes
                fo = len(cx.owners)
                mod_phase(l)
                cx.end_phase(fo)
                cx.mem_es = es
            import os
            if do_mix:
                mdst = xres if (do_ffn or l < depth - 1) else yT
                with contextlib.ExitStack() as em:
                    cx.mem_es = em
                    fo = len(cx.owners)
                    mixer_phase(l, cur, mdst, em)
                    cx.end_phase(fo)
                    cx.mem_es = es
                cur = mdst
            if DBG in ("mod", "modmm", "const", "modtt"):
                dbg = Buf(cx, "dbg", [128, 512], F32)
                cx.op("dve", lambda e: e.tensor_copy(out=dbg[:, 0:96], in_=modT[:].rearrange("p c s -> p (c s)")), [modT.k], [dbg.k])
                cx.op("dve", lambda e: e.tensor_copy(out=dbg[:, 96:112], in_=A2[:].rearrange("p c s -> p (c s)")), [A2.k], [dbg.k])
                cx.dma("sp", [(yT[0, :, 0:512], dbg[:])], [dbg.k], [], dbg.k)
                break
            if do_ffn and "ffn" not in SKIP:
                last = (l == depth - 1)
                dst = yT if last else xres
                with contextlib.ExitStack() as fes:
                    cx.mem_es = fes
                    fo = len(cx.owners)
                    out_tks = ffn_phase(l, cur, dst)
                    cx.end_phase(fo)
                    cx.mem_es = es
                cur = dst
        cx.barrier()
    return nc


def _prep_core_inputs(inputs, depth=4):
    f = np.float32
    sh = {}
    wm = np.asarray(inputs["w_mod"], f)[:depth]
    sh["w_mod"] = np.ascontiguousarray(wm.reshape(depth, KD, 128, NMOD * D).transpose(0, 2, 1, 3))
    bm = np.asarray(inputs["b_mod"], f)[:depth]
    sh["b_mod"] = np.ascontiguousarray(bm.reshape(depth, NMOD * KD, 128).transpose(0, 2, 1))
    ng = np.asarray(inputs["norm_g"], f)[:depth]
    sh["norm_g"] = np.ascontiguousarray(ng.reshape(depth, 4, KD, 128).transpose(0, 3, 1, 2))
    w1 = np.asarray(inputs["w_ff1"], f)[:depth]
    sh["w_ff1"] = np.ascontiguousarray(w1.reshape(depth, KD, 128, DFF).transpose(0, 2, 1, 3)).reshape(depth, 128, KD * DFF)
    w2 = np.asarray(inputs["w_ff2"], f)[:depth]
    sh["w_ff2"] = np.ascontiguousarray(w2.reshape(depth, DFF // 128, 128, D).transpose(0, 2, 1, 3)).reshape(depth, 128, (DFF // 128) * D)
    sh["c_ones"] = np.ones((128, 128), f)
    sh["c_ident"] = np.eye(128, dtype=f)
    b1 = np.zeros((128, 128), f)
    b1[:64, :64] = 1.0
    b1[64:, 64:] = 1.0
    sh["c_blk1"] = b1
    ii = np.arange(128)
    Bk = lambda b: ((ii[:, None] // b) == (ii[None, :] // b)).astype(f)
    sh["c_bmask"] = np.ascontiguousarray(np.stack([Bk(16), Bk(32) - Bk(16), Bk(64) - Bk(32), 1.0 - Bk(64)], axis=1))
    rm = np.ones((128, 512), f)
    rm[:, ::128] = 0.0
    sh["c_rmask"] = rm
    r = np.arange(128)[:, None]
    c = np.arange(128)[None, :]
    sh["c_masks"] = np.ascontiguousarray(np.stack([(r > c), (r <= c), (r < c), (r >= c)], axis=1).astype(f))
    wi = np.asarray(inputs["w_in"], f)[:depth]
    RW = 1920
    wi2 = np.zeros((depth, D, 3584), f)
    wi2[:, :, 0:RW] = wi[:, :, 0:RW]
    wi2[:, :, RW:RW + 1024] = wi[:, :, RW + 512:RW + 1536]
    wi2[:, :, 2944:3456] = wi[:, :, RW:RW + 512]
    wi2[:, :, 3456:3472] = wi[:, :, RW + 1536:RW + 1552]
    sh["w_in"] = np.ascontiguousarray(wi2.reshape(depth, KD, 128, 3584).transpose(0, 2, 1, 3)).reshape(depth, 128, KD * 3584)
    wo = np.asarray(inputs["w_out"], f)[:depth]
    sh["w_out"] = np.ascontiguousarray(wo.reshape(depth, KD, 128, D).transpose(0, 2, 1, 3)).reshape(depth, 128, KD * D)
    pp = np.zeros((depth, 128, 114), f)
    cw = np.asarray(inputs["conv_w"], f)[:depth]
    pp[:, :, 0:40] = cw.reshape(depth, 5, 8, 128).transpose(0, 3, 2, 1).reshape(depth, 128, 40)
    pp[:, :, 40:48] = np.asarray(inputs["conv_b"], f)[:depth].reshape(depth, 8, 128).transpose(0, 2, 1)
    mu = np.asarray(inputs["shift_mu"], f)[:depth]
    pp[:, :, 48:63] = mu[:, 0].reshape(depth, 15, 128).transpose(0, 2, 1)
    pp[:, :, 63:78] = mu[:, 1].reshape(depth, 15, 128).transpose(0, 2, 1)
    pp[:, :, 78:86] = np.asarray(inputs["w0"], f)[:depth].reshape(depth, 8, 128).transpose(0, 2, 1)
    pp[:, :, 86:94] = np.asarray(inputs["a0"], f)[:depth].reshape(depth, 8, 128).transpose(0, 2, 1)
    for off, nm in ((94, "k_k"), (98, "k_a"), (102, "r_k"), (106, "gn_w"), (110, "gn_b")):
        pp[:, :, off:off + 4] = np.asarray(inputs[nm], f)[:depth].reshape(depth, 4, 128).transpose(0, 2, 1)
    sh["pp"] = pp
    bb = np.zeros((depth, 1056), f)
    bb[:, 0:16] = np.asarray(inputs["dt_bias"], f)[:depth].reshape(depth, 16)
    bb[:, 16:32] = np.asarray(inputs["A_log"], f)[:depth].reshape(depth, 16)
    bb[:, 32:544] = np.repeat(np.asarray(inputs["d_skip"], f)[:depth], 64, axis=1)
    bb[:, 544:1056] = np.asarray(inputs["ssm_norm_w"], f)[:depth]
    sh["bb"] = bb
    sh["wup"] = np.ascontiguousarray(np.asarray(inputs["w_up"], f)[:depth].reshape(depth, 128, 512))
    sh["aup"] = np.ascontiguousarray(np.asarray(inputs["a_up"], f)[:depth].reshape(depth, 128, 512))
    sh["gup"] = np.ascontiguousarray(np.asarray(inputs["g_up"], f)[:depth])
    return sh


def _core_tokens(x2seq, c2, flagval):
    f = np.float32
    NT = x2seq.shape[0]
    m = {}
    m["xT"] = np.ascontiguousarray(x2seq.T.reshape(KD, 128, NT)).astype(f)
    m["cT"] = np.ascontiguousarray(c2.reshape(2, KD, 128).transpose(2, 1, 0)).astype(f)
    m["flag"] = np.full((128, 1), flagval, f)
    return m


_NC_CACHE = {}


def kernel(**inputs):
    depth = 4
    NT = 4096
    key = (NT, depth)
    if key not in _NC_CACHE:
        _NC_CACHE[key] = build(NT, depth)
    nc = _NC_CACHE[key]
    shared = _prep_core_inputs(inputs, depth)
    xp = np.asarray(inputs["x_prompt"], np.float32)
    xs = np.asarray(inputs["x_sample"], np.float32)
    cp = np.asarray(inputs["c_prompt"], np.float32)
    cs = np.asarray(inputs["c_sample"], np.float32)
    in_maps = []
    for core in range(8):
        if core < 4:
            x2 = np.concatenate([xp[2 * core], xp[2 * core + 1]], axis=0)
            c2 = np.stack([cp[2 * core], cp[2 * core + 1]])
            m = _core_tokens(x2, c2, 0.0)
        else:
            j = core - 4
            m = _core_tokens(xs[j], np.stack([cs[j], cs[j]]), 1.0)
        m.update(shared)
        in_maps.append(m)
    res = run_bass_kernel_spmd(nc, in_maps, core_ids=list(range(8)))
    yp = np.zeros_like(xp)
    ys = np.zeros_like(xs)
    for core in range(8):
        y = res.results[core]["yT"].reshape(D, NT).T
        if core < 4:
            yp[2 * core] = y[:2048]
            yp[2 * core + 1] = y[2048:]
        else:
            ys[core - 4] = y
    return (yp, ys)
```

```python
import contextlib
import numpy as np
import concourse.bass as bass
import concourse.mybir as mybir
from concourse.bass_utils import run_bass_kernel_spmd

F32 = mybir.dt.float32
BF16 = mybir.dt.bfloat16
AF = mybir.ActivationFunctionType
ALU = mybir.AluOpType

D = 1024
KD = D // 128
DFF = 4096
NMOD = 6
NORM_EPS = 1e-6


class Tk:
    __slots__ = ("w", "r", "dsem", "dcnt", "dkey", "name", "psum")

    def __init__(self, name=""):
        self.w = None
        self.r = {}
        self.dsem = None
        self.dcnt = 0
        self.name = name
        self.psum = False


class Ctx:
    def __init__(self, nc, es):
        self.nc = nc
        self.es = es
        self.E = {"pe": nc.tensor, "act": nc.scalar, "dve": nc.vector, "pool": nc.gpsimd, "sp": nc.sync}
        self.sem = {}
        self.cnt = {}
        for k in self.E:
            self.sem[k] = es.enter_context(nc.semaphore("c_" + k))
            self.cnt[k] = 0
        self.seen = {k: {} for k in self.E}
        self.nsem = 5
        self.ninst = 0
        self.mem_es = es
        self.sem_free = []
        self.owners = []
        self.gsems = []

    def _wait(self, eng, evs):
        need = {}
        for ev in evs:
            if ev is None:
                continue
            key, h, v = ev
            if eng == "pe" and key == "pe":
                continue
            if v > need.get(key, (None, 0))[1]:
                need[key] = (h, v)
        for key, (h, v) in need.items():
            if self.seen[eng].get(key, 0) >= v:
                continue
            self.E[eng].wait_ge(h, v)
            self.seen[eng][key] = v

    def _deps(self, reads, writes):
        evs = []
        for t in reads:
            evs.append(t.w)
            if t.psum:
                evs.extend(t.r.values())
        for t in writes:
            evs.append(t.w)
            evs.extend(t.r.values())
        return evs

    def _record(self, ev, reads, writes):
        key = ev[0]
        for t in reads:
            old = t.r.get(key)
            if old is None or old[2] < ev[2]:
                t.r[key] = ev
        for t in writes:
            t.w = ev
            t.r = {}

    def op(self, eng, fn, reads=(), writes=()):
        self._wait(eng, self._deps(reads, writes))
        ins = fn(self.E[eng])
        self.cnt[eng] += 1
        ins.then_inc(self.sem[eng], 1)
        ev = (eng, self.sem[eng], self.cnt[eng])
        self._record(ev, reads, writes)
        self.ninst += 1
        return ev

    def dma(self, q, pairs, reads, writes, owner, **kw):
        if q == "pool":
            assert len(pairs) == 1
            sem = self.es.enter_context(self.nc.semaphore("g%d" % self.nsem))
            key = "g%d" % self.nsem
            self.nsem += 1
            self._wait(q, self._deps(reads, writes))
            o, i = pairs[0]
            self.E[q].dma_start(out=o, in_=i, **kw).then_inc(sem, 16)
            self.ninst += 1
            ev = (key, sem, 16)
            self._record(ev, reads, writes)
            self.gsems.append(ev)
            return ev
        if owner.dsem is None:
            if self.sem_free:
                owner.dsem, owner.dcnt, owner.dkey = self.sem_free.pop()
            else:
                owner.dsem = self.es.enter_context(self.nc.semaphore("d%d" % self.nsem))
                owner.dkey = "d%d" % self.nsem
                self.nsem += 1
            self.owners.append(owner)
        key = owner.dkey
        evs = self._deps(reads, writes)
        if owner.dcnt:
            evs.append((key, owner.dsem, owner.dcnt))
        self._wait(q, evs)
        for (o, i) in pairs:
            self.E[q].dma_start(out=o, in_=i, **kw).then_inc(owner.dsem, 16)
            owner.dcnt += 16
            self.ninst += 1
        ev = (key, owner.dsem, owner.dcnt)
        self._record(ev, reads, writes)
        return ev

    def barrier(self):
        for eng in self.E:
            evs = [(k, self.sem[k], self.cnt[k]) for k in self.E if self.cnt[k] > 0]
            evs += [(o.dkey, o.dsem, o.dcnt) for o in self.owners if o.dcnt > 0]
            evs += self.gsems
            self._wait(eng, evs)

    def end_phase(self, first_owner):
        self.barrier()
        rel = self.owners[first_owner:]
        self.owners = self.owners[:first_owner]
        for o in rel:
            self.sem_free.append((o.dsem, o.dcnt, o.dkey))
            o.dsem = None

    def wait_all(self, eng, tks):
        evs = []
        for t in tks:
            evs.append(t.w)
            evs.extend(t.r.values())
        self._wait(eng, evs)


class Buf:
    _n = [0]

    def __init__(self, cx, name, shape, dtype, psum=False):
        Buf._n[0] += 1
        name = "%s_%d" % (name, Buf._n[0])
        if psum:
            self.t = cx.mem_es.enter_context(cx.nc.psum_tensor(name, shape, dtype))
        else:
            self.t = cx.mem_es.enter_context(cx.nc.sbuf_tensor(name, shape, dtype))
        self.k = Tk(name)
        self.k.psum = psum

    def __getitem__(self, idx):
        return self.t[idx]


def build(NT=4096, depth=4, do_mix=True, do_ffn=True, debug=False):
    SEG = NT // 2
    nc = bass.Bass("TRN2", target_bir_lowering=False)
    dram_in = {}

    def din(name, shape, dt=F32):
        dram_in[name] = nc.dram_tensor(name, list(shape), dt, kind="ExternalInput").ap()
        return dram_in[name]

    xT = din("xT", [KD, 128, NT])
    cT = din("cT", [128, KD, 2])
    flag = din("flag", [128, 1])
    w_mod = din("w_mod", [depth, 128, KD, NMOD * D])
    b_mod = din("b_mod", [depth, 128, NMOD * KD])
    norm_g = din("norm_g", [depth, 128, 4, KD])
    w_ff1 = din("w_ff1", [depth, 128, KD * DFF])
    w_ff2 = din("w_ff2", [depth, 128, (DFF // 128) * D])
    c_ones = din("c_ones", [128, 128])
    c_ident = din("c_ident", [128, 128])
    c_masks = din("c_masks", [128, 4, 128])
    c_blk1 = din("c_blk1", [128, 128])
    c_bmask = din("c_bmask", [128, 4, 128])
    c_rmask = din("c_rmask", [128, 512])
    NPP, NBB, WIN = 114, 1056, 3584
    w_in = din("w_in", [depth, 128, KD * WIN])
    w_out = din("w_out", [depth, 128, KD * D])
    pp_in = din("pp", [depth, 128, NPP])
    bb_in = din("bb", [depth, NBB])
    wup_in = din("wup", [depth, 128, 512])
    aup_in = din("aup", [depth, 128, 512])
    gup_in = din("gup", [depth, 128, 512])
    NFC = 23
    NC = NT // 128
    uT = nc.dram_tensor("uT", [NFC, 128, NT], F32, kind="Internal").ap()
    z_tok = nc.dram_tensor("z_tok", [NT, 512], F32, kind="Internal").ap()
    dt_tok = nc.dram_tensor("dt_tok", [NT, 16], F32, kind="Internal").ap()
    ysacc = nc.dram_tensor("ysacc", [NT, 512], F32, kind="Internal").ap()
    yracc = nc.dram_tensor("yracc", [4, 128, NT], F32, kind="Internal").ap()
    ymix_d = nc.dram_tensor("ymix_d", [KD, 128, NT], BF16, kind="Internal").ap()
    dbg_out = nc.dram_tensor("dbg_out", [KD, 128, NT], F32, kind="ExternalOutput").ap() if debug else None
    yT = nc.dram_tensor("yT", [KD, 128, NT], F32, kind="ExternalOutput").ap()
    xres = nc.dram_tensor("xres", [KD, 128, NT], F32, kind="Internal").ap()

    with contextlib.ExitStack() as es:
        cx = Ctx(nc, es)
        ones_f = Buf(cx, "ones_f", [128, 128], F32)
        ones_b = Buf(cx, "ones_b", [128, 128], BF16)
        eps_t = Buf(cx, "eps_t", [128, 1], F32)
        sc_t = Buf(cx, "sc_t", [128, KD, 2], F32)
        flag_t = Buf(cx, "flag_t", [128, 1], F32)
        ident_f = Buf(cx, "ident_f", [128, 128], F32)
        ident_b = Buf(cx, "ident_b", [128, 128], BF16)
        mk_f = Buf(cx, "mk_f", [128, 4, 128], F32)
        one_t = Buf(cx, "one_t", [128, 1], F32)
        cx.dma("sp", [(ones_f[:], c_ones), (sc_t[:], cT), (flag_t[:], flag), (ident_f[:], c_ident), (mk_f[:], c_masks)], [],
               [ones_f.k, sc_t.k, flag_t.k, ident_f.k, mk_f.k], ones_f.k)
        cx.op("dve", lambda e: e.tensor_copy(out=ident_b[:], in_=ident_f[:]), [ident_f.k], [ident_b.k])
        blk1_f = Buf(cx, "blk1_f", [128, 128], F32)
        blk1_b = Buf(cx, "blk1_b", [128, 128], BF16)
        blk64_b = Buf(cx, "blk64_b", [128, 128], BF16)
        rmask = Buf(cx, "rmask", [128, 512], F32)
        gneps_t = Buf(cx, "gneps_t", [128, 1], F32)
        bm_f = Buf(cx, "bm_f", [128, 4, 128], F32)
        bm_b = Buf(cx, "bm_b", [128, 4, 128], BF16)
        cx.dma("sp", [(blk1_f[:], c_blk1), (rmask[:], c_rmask), (bm_f[:], c_bmask)], [], [blk1_f.k, rmask.k, bm_f.k], blk1_f.k)
        cx.op("dve", lambda e: e.tensor_copy(out=bm_b[:], in_=bm_f[:]), [bm_f.k], [bm_b.k])
        cx.op("dve", lambda e: e.tensor_copy(out=blk1_b[:], in_=blk1_f[:]), [blk1_f.k], [blk1_b.k])
        cx.op("dve", lambda e: e.tensor_scalar(out=blk64_b[:], in0=blk1_f[:], scalar1=1.0 / 64, scalar2=1.0, op0=ALU.mult, op1=ALU.mult), [blk1_f.k], [blk64_b.k])
        cx.op("dve", lambda e: e.memset(gneps_t[:], 64e-5), [], [gneps_t.k])
        cx.op("dve", lambda e: e.memset(one_t[:], 1.0), [], [one_t.k])
        cx.op("dve", lambda e: e.tensor_copy(out=ones_b[:], in_=ones_f[:]), [ones_f.k], [ones_b.k])
        cx.op("dve", lambda e: e.memset(eps_t[:], NORM_EPS), [], [eps_t.k])
        cx.op("act", lambda e: e.activation(out=sc_t[:], in_=sc_t[:], func=AF.Silu), [sc_t.k], [sc_t.k])

        import os
        DBG = os.environ.get("DBG_STOP", "")
        SKIP = os.environ.get("KSKIP", "").split(",")
        PS = [Buf(cx, "ps%d" % i, [128, 512], F32, psum=True) for i in range(7)]
        PT = Buf(cx, "pt", [128, 1024], BF16, psum=True)

        modT = Buf(cx, "modT", [128, NMOD * KD, 2], F32)
        bmod_t = Buf(cx, "bmod_t", [128, NMOD * KD], F32)
        ng_t = Buf(cx, "ng_t", [128, 4, KD], F32)
        A1 = Buf(cx, "A1", [128, KD, 2], F32)
        G1 = Buf(cx, "G1", [128, KD, 2], F32)
        A2 = Buf(cx, "A2", [128, KD, 2], F32)
        G2 = Buf(cx, "G2", [128, KD, 2], F32)

        def mod_phase(l):
            wm = [Buf(cx, "wm%d" % i, [128, KD, 512], F32) for i in range(2)]
            cx.dma("sp", [(bmod_t[:], b_mod[l]), (ng_t[:], norm_g[l])], [], [bmod_t.k, ng_t.k], bmod_t.k)
            ps = PS[6]
            if DBG == "const":
                return
            for j in range(NMOD * D // 512):
                wb = wm[j % 2]
                cx.dma("sp", [(wb[:], w_mod[l][:, :, j * 512:(j + 1) * 512])], [], [wb.k], wb.k)
                for cc in range(4):
                    col = j * 4 + cc
                    for k in range(KD):
                        cx.op("pe", lambda e, k=k, cc=cc, col=col, wb=wb: e.matmul(
                            ps[:, col * 2:col * 2 + 2], lhsT=wb[:, k, cc * 128:(cc + 1) * 128], rhs=sc_t[:, k, :],
                            start=(k == 0), stop=(k == KD - 1)), [wb.k, sc_t.k], [ps.k])
            if DBG == "modmm":
                return
            cx.op("dve", lambda e: e.tensor_tensor(
                out=modT[:], in0=ps[:, 0:NMOD * KD * 2].rearrange("p (c s) -> p c s", s=2),
                in1=bmod_t[:].unsqueeze(2).to_broadcast([128, NMOD * KD, 2]), op=ALU.add),
                [ps.k, bmod_t.k], [modT.k])

            def mk(dst, mi, gi, plus1):
                gb = ng_t[:, gi, :].unsqueeze(2).to_broadcast([128, KD, 2])
                src = modT[:, mi * KD:(mi + 1) * KD, :]
                if plus1:
                    cx.op("dve", lambda e: e.tensor_scalar(out=dst[:], in0=src, scalar1=1.0, scalar2=None, op0=ALU.add),
                          [modT.k], [dst.k])
                    src = dst[:]
                cx.op("dve", lambda e: e.tensor_tensor(out=dst[:], in0=src, in1=gb, op=ALU.mult), [modT.k, ng_t.k, dst.k], [dst.k])
            if DBG == "modtt":
                return
            mk(A1, 1, 0, True)
            mk(G1, 2, 1, False)
            mk(A2, 4, 2, True)
            mk(G2, 5, 3, False)

        TF = 256
        NKF = DFF // 128

        def ffn_phase(l, src, dst):
            w1b = Buf(cx, "w1b", [128, KD, DFF], BF16)
            w2b = Buf(cx, "w2b", [128, NKF, D], BF16)
            xt = [Buf(cx, "f_xt%d" % i, [128, KD, TF], F32) for i in range(2)]
            sq = Buf(cx, "f_sq", [128, KD, TF], BF16)
            std = Buf(cx, "f_std", [128, TF], F32)
            rstd = Buf(cx, "f_rstd", [128, TF], F32)
            tmp = Buf(cx, "f_tmp", [128, KD, TF], F32)
            hT = Buf(cx, "f_hT", [128, KD, TF], BF16)
            rl = [Buf(cx, "f_rl%d" % i, [128, TF], F32) for i in range(4)]
            aT = Buf(cx, "f_aT", [128, NKF, TF], BF16)
            fT = Buf(cx, "f_fT", [128, KD, TF], F32)
            w1v = w_ff1[l].rearrange("p (a b) -> p a b", b=2048)
            w2v = w_ff2[l].rearrange("p (a b) -> p a b", b=2048)
            cx.dma("pool", [(w1b[:].rearrange("p k c -> p (k c)").rearrange("p (a b) -> p a b", b=2048), w1v)],
                   [], [w1b.k], w1b.k)
            cx.dma("pool", [(w2b[:].rearrange("p k c -> p (k c)").rearrange("p (a b) -> p a b", b=2048), w2v)],
                   [], [w2b.k], w2b.k)
            dk = [Tk("ffn_dst%d" % i) for i in range(NT // TF)]
            if DBG == "ffn_w":
                return dk
            for ti in range(NT // TF):
                t0 = ti * TF
                seg = t0 // SEG
                x = xt[ti % 2]
                cx.dma("sp", [(x[:], src[:, :, t0:t0 + TF].rearrange("k p t -> p k t"))], [], [x.k], x.k)
                cx.op("act", lambda e: e.activation(out=sq[:], in_=x[:], func=AF.Square), [x.k], [sq.k])
                ps = PS[0]
                for k in range(KD):
                    cx.op("pe", lambda e, k=k: e.matmul(ps[:, 0:TF], lhsT=ones_b[:], rhs=sq[:, k, :],
                                                        start=(k == 0), stop=(k == KD - 1)), [ones_b.k, sq.k], [ps.k])
                cx.op("act", lambda e: e.activation(out=std[:], in_=ps[:, 0:TF], func=AF.Ln, bias=eps_t[:], scale=1.0 / D),
                      [ps.k, eps_t.k], [std.k])
                cx.op("act", lambda e: e.activation(out=rstd[:], in_=std[:], func=AF.Exp, scale=-0.5), [std.k], [rstd.k])
                if DBG == "ffn_rstd":
                    return dk
                cx.op("dve", lambda e: e.tensor_tensor(out=tmp[:], in0=x[:], in1=rstd[:].unsqueeze(1).to_broadcast([128, KD, TF]),
                                                       op=ALU.mult), [x.k, rstd.k], [tmp.k])
                if DBG == "ffn_tmp":
                    return dk
                for k in range(KD):
                    eng = "act" if k % 2 == 0 else "pool"
                    if eng == "act":
                        cx.op("act", lambda e, k=k: e.activation(out=hT[:, k, :], in_=tmp[:, k, :], func=AF.Identity,
                                                                 bias=modT[:, 3 * KD + k, seg:seg + 1], scale=A2[:, k, seg:seg + 1]),
                              [tmp.k, modT.k, A2.k], [hT.k])
                    else:
                        cx.op("pool", lambda e, k=k: e.tensor_scalar(out=hT[:, k, :], in0=tmp[:, k, :],
                                                                     scalar1=A2[:, k, seg:seg + 1], scalar2=modT[:, 3 * KD + k, seg:seg + 1],
                                                                     op0=ALU.mult, op1=ALU.add), [tmp.k, modT.k, A2.k], [hT.k])
                if DBG == "ffn_h":
                    return dk
                for c in range(NKF):
                    ps = PS[1 + c % 4]
                    r = rl[c % 4]
                    for k in range(KD):
                        cx.op("pe", lambda e, k=k, c=c, ps=ps: e.matmul(ps[:, 0:TF], lhsT=w1b[:, k, c * 128:(c + 1) * 128], rhs=hT[:, k, :],
                                                                       start=(k == 0), stop=(k == KD - 1)), [w1b.k, hT.k], [ps.k])
                    cx.op("act", lambda e, ps=ps, r=r: e.activation(out=r[:], in_=ps[:, 0:TF], func=AF.Relu), [ps.k], [r.k])
                    if c % 2 == 0:
                        cx.op("dve", lambda e, ps=ps, r=r, c=c: e.tensor_tensor(out=aT[:, c, :], in0=r[:], in1=ps[:, 0:TF], op=ALU.mult),
                              [r.k, ps.k], [aT.k])
                    else:
                        cx.op("pool", lambda e, r=r, c=c: e.tensor_tensor(out=aT[:, c, :], in0=r[:], in1=r[:], op=ALU.mult),
                              [r.k], [aT.k])
                if DBG == "ffn_1":
                    return dk
                for o in range(KD):
                    ps = PS[5 + o % 2]
                    for k in range(NKF):
                        cx.op("pe", lambda e, k=k, o=o, ps=ps: e.matmul(ps[:, 0:TF], lhsT=w2b[:, k, o * 128:(o + 1) * 128], rhs=aT[:, k, :],
                                                                       start=(k == 0), stop=(k == NKF - 1)), [w2b.k, aT.k], [ps.k])
                    if DBG != "ffn_2a":
                        cx.op("act", lambda e, ps=ps, o=o: e.activation(out=sq[:, o, :], in_=ps[:, 0:TF], func=AF.Square), [ps.k], [sq.k])
                    if DBG != "ffn_2b":
                        cx.op("dve", lambda e, ps=ps, o=o: e.tensor_copy(out=fT[:, o, :], in_=ps[:, 0:TF]), [ps.k], [fT.k])
                if DBG in ("ffn_2", "ffn_2a", "ffn_2b"):
                    return dk
                ps = PS[0]
                for k in range(KD):
                    cx.op("pe", lambda e, k=k: e.matmul(ps[:, 0:TF], lhsT=ones_b[:], rhs=sq[:, k, :],
                                                        start=(k == 0), stop=(k == KD - 1)), [ones_b.k, sq.k], [ps.k])
                cx.op("act", lambda e: e.activation(out=std[:], in_=ps[:, 0:TF], func=AF.Ln, bias=eps_t[:], scale=1.0 / D),
                      [ps.k, eps_t.k], [std.k])
                cx.op("act", lambda e: e.activation(out=rstd[:], in_=std[:], func=AF.Exp, scale=-0.5), [std.k], [rstd.k])
                cx.op("dve", lambda e: e.tensor_tensor(out=fT[:], in0=fT[:], in1=rstd[:].unsqueeze(1).to_broadcast([128, KD, TF]),
                                                       op=ALU.mult), [fT.k, rstd.k], [fT.k])
                for k in range(KD):
                    if k % 2 == 0:
                        cx.op("act", lambda e, k=k: e.activation(out=fT[:, k, :], in_=fT[:, k, :], func=AF.Identity,
                                                                 scale=G2[:, k, seg:seg + 1]), [fT.k, G2.k], [fT.k])
                    else:
                        cx.op("pool", lambda e, k=k: e.tensor_scalar(out=fT[:, k, :], in0=fT[:, k, :], scalar1=G2[:, k, seg:seg + 1],
                                                                     scalar2=1.0, op0=ALU.mult, op1=ALU.mult), [fT.k, G2.k], [fT.k])
                cx.op("dve", lambda e: e.tensor_tensor(out=x[:], in0=x[:], in1=fT[:], op=ALU.add), [x.k, fT.k], [x.k])
                if DBG == "ffn_tail":
                    return dk
                cx.dma("sp", [(dst[:, :, t0:t0 + TF].rearrange("k p t -> p k t"), x[:])], [x.k], [dk[ti]], x.k)
            return dk

        TM = 512
        V4 = lambda ap, a: ap.rearrange("p (a b) -> p a b", a=a)

        def norm_tile(x, seg, TT, sq, std, rstd, tmp, hT, Amul, boff):
            cx.op("act", lambda e: e.activation(out=sq[:], in_=x[:], func=AF.Square), [x.k], [sq.k])
            ps = PS[0]
            for k in range(KD):
                cx.op("pe", lambda e, k=k: e.matmul(ps[:, 0:TT], lhsT=ones_b[:], rhs=sq[:, k, :],
                                                    start=(k == 0), stop=(k == KD - 1)), [ones_b.k, sq.k], [ps.k])
            cx.op("act", lambda e: e.activation(out=std[:], in_=ps[:, 0:TT], func=AF.Ln, bias=eps_t[:], scale=1.0 / D),
                  [ps.k, eps_t.k], [std.k])
            cx.op("act", lambda e: e.activation(out=rstd[:], in_=std[:], func=AF.Exp, scale=-0.5), [std.k], [rstd.k])
            cx.op("dve", lambda e: e.tensor_tensor(out=tmp[:], in0=x[:], in1=rstd[:].unsqueeze(1).to_broadcast([128, KD, TT]),
                                                   op=ALU.mult), [x.k, rstd.k], [tmp.k])
            for k in range(KD):
                if k % 2 == 0:
                    cx.op("act", lambda e, k=k: e.activation(out=hT[:, k, :], in_=tmp[:, k, :], func=AF.Identity,
                                                             bias=modT[:, boff * KD + k, seg:seg + 1], scale=Amul[:, k, seg:seg + 1]),
                          [tmp.k, modT.k, Amul.k], [hT.k])
                else:
                    cx.op("pool", lambda e, k=k: e.tensor_scalar(out=hT[:, k, :], in0=tmp[:, k, :],
                                                                 scalar1=Amul[:, k, seg:seg + 1], scalar2=modT[:, boff * KD + k, seg:seg + 1],
                                                                 op0=ALU.mult, op1=ALU.add), [tmp.k, modT.k, Amul.k], [hT.k])

        def post_tile(x, seg, TT, psrc_fn, nk, sq, std, rstd, fT, Gmul):
            for o in range(KD):
                ps = psrc_fn(o)
                cx.op("act", lambda e, ps=ps, o=o: e.activation(out=sq[:, o, :], in_=ps[:, 0:TT], func=AF.Square), [ps.k], [sq.k])
                cx.op("dve", lambda e, ps=ps, o=o: e.tensor_copy(out=fT[:, o, :], in_=ps[:, 0:TT]), [ps.k], [fT.k])
            ps = PS[0]
            for k in range(KD):
                cx.op("pe", lambda e, k=k: e.matmul(ps[:, 0:TT], lhsT=ones_b[:], rhs=sq[:, k, :],
                                                    start=(k == 0), stop=(k == KD - 1)), [ones_b.k, sq.k], [ps.k])
            cx.op("act", lambda e: e.activation(out=std[:], in_=ps[:, 0:TT], func=AF.Ln, bias=eps_t[:], scale=1.0 / D),
                  [ps.k, eps_t.k], [std.k])
            cx.op("act", lambda e: e.activation(out=rstd[:], in_=std[:], func=AF.Exp, scale=-0.5), [std.k], [rstd.k])
            cx.op("dve", lambda e: e.tensor_tensor(out=fT[:], in0=fT[:], in1=rstd[:].unsqueeze(1).to_broadcast([128, KD, TT]),
                                                   op=ALU.mult), [fT.k, rstd.k], [fT.k])
            for k in range(KD):
                if k % 2 == 0:
                    cx.op("act", lambda e, k=k: e.activation(out=fT[:, k, :], in_=fT[:, k, :], func=AF.Identity,
                                                             scale=Gmul[:, k, seg:seg + 1]), [fT.k, Gmul.k], [fT.k])
                else:
                    cx.op("pool", lambda e, k=k: e.tensor_scalar(out=fT[:, k, :], in0=fT[:, k, :], scalar1=Gmul[:, k, seg:seg + 1],
                                                                 scalar2=1.0, op0=ALU.mult, op1=ALU.mult), [fT.k, Gmul.k], [fT.k])
            cx.op("dve", lambda e: e.tensor_tensor(out=x[:], in0=x[:], in1=fT[:], op=ALU.add), [x.k, fT.k], [x.k])

        class MixState:
            pass

        def mix_load_params(l, ms):
            ms.pp = Buf(cx, "pp", [128, NPP], F32)
            ms.bb = Buf(cx, "bb", [128, NBB], F32)
            cx.dma("sp", [(ms.pp[:], pp_in[l]), (ms.bb[:], bb_in[l].partition_broadcast(128))], [], [ms.pp.k, ms.bb.k], ms.pp.k)
            ms.aneg = Buf(cx, "aneg", [128, 16], F32)
            cx.op("act", lambda e: e.activation(out=ms.aneg[:], in_=ms.bb[:, 16:32], func=AF.Exp), [ms.bb.k], [ms.aneg.k])
            cx.op("dve", lambda e: e.tensor_scalar(out=ms.aneg[:], in0=ms.aneg[:], scalar1=-1.0, scalar2=1.0, op0=ALU.mult, op1=ALU.mult),
                  [ms.aneg.k], [ms.aneg.k])
            ms.ymk = [[Tk("ymk%d_%d" % (h, c)) for c in range(NC)] for h in range(2)]
            ms.uk = [[Tk("uT%d_%d" % (fc, ti)) for ti in range(NT // TM)] for fc in range(NFC)]
            ms.zk = [Tk("z%d" % c) for c in range(NC)]
            ms.dtk = [Tk("dt%d" % c) for c in range(NC)]

        def m1_inproj(l, src, ms):
            winb = Buf(cx, "winb", [128, KD, WIN], BF16)
            cx.dma("pool", [(winb[:].rearrange("p k c -> p (k c)").rearrange("p (a b) -> p a b", b=1792),
                             w_in[l].rearrange("p (a b) -> p a b", b=1792))], [], [winb.k], winb.k)
            xt = [Buf(cx, "m1_xt%d" % i, [128, KD, TM], F32) for i in range(2)]
            sq = Buf(cx, "m1_sq", [128, KD, TM], BF16)
            std = Buf(cx, "m1_std", [128, TM], F32)
            rstd = Buf(cx, "m1_rstd", [128, TM], F32)
            tmp = Buf(cx, "m1_tmp", [128, KD, TM], F32)
            hT = Buf(cx, "m1_hT", [128, KD, TM], BF16)
            stg = [Buf(cx, "m1_stg%d" % i, [128, TM], F32) for i in range(4)]
            stz = [Buf(cx, "m1_stz%d" % i, [128, 512 + 16], F32) for i in range(2)]
            ns = 0
            for ti in range(NT // TM):
                t0 = ti * TM
                seg = t0 // SEG
                x = xt[ti % 2]
                cx.dma("sp", [(x[:], src[:, :, t0:t0 + TM].rearrange("k p t -> p k t"))], [], [x.k], x.k)
                norm_tile(x, seg, TM, sq, std, rstd, tmp, hT, A1, 0)
                for fc in range(NFC):
                    ps = PS[1 + fc % 4]
                    for k in range(KD):
                        cx.op("pe", lambda e, k=k, fc=fc, ps=ps: e.matmul(ps[:, 0:TM], lhsT=winb[:, k, fc * 128:(fc + 1) * 128], rhs=hT[:, k, :],
                                                                         start=(k == 0), stop=(k == KD - 1)), [winb.k, hT.k], [ps.k])
                    st = stg[ns % 4]
                    ns += 1
                    if fc % 2 == 0:
                        cx.op("act", lambda e, ps=ps, st=st: e.activation(out=st[:], in_=ps[:, 0:TM], func=AF.Identity), [ps.k], [st.k])
                    else:
                        cx.op("dve", lambda e, ps=ps, st=st: e.tensor_copy(out=st[:], in_=ps[:, 0:TM]), [ps.k], [st.k])
                    cx.dma("sp", [(uT[fc, :, t0:t0 + TM], st[:])], [st.k], [ms.uk[fc][ti]], st.k)
                for sub in range(TM // 128):
                    c = (t0 // 128) + sub
                    psz = PS[5]
                    psd = PS[6]
                    for k in range(KD):
                        cx.op("pe", lambda e, k=k: e.matmul(psz[:, 0:512], lhsT=hT[:, k, sub * 128:(sub + 1) * 128], rhs=winb[:, k, 2944:3456],
                                                            start=(k == 0), stop=(k == KD - 1)), [winb.k, hT.k], [psz.k])
                    for k in range(KD):
                        cx.op("pe", lambda e, k=k: e.matmul(psd[:, 0:16], lhsT=hT[:, k, sub * 128:(sub + 1) * 128], rhs=winb[:, k, 3456:3472],
                                                            start=(k == 0), stop=(k == KD - 1)), [winb.k, hT.k], [psd.k])
                    sz = stz[c % 2]
                    cx.op("act", lambda e: e.activation(out=sz[:, 0:512], in_=psz[:, 0:512], func=AF.Identity), [psz.k], [sz.k])
                    cx.op("dve", lambda e: e.tensor_copy(out=sz[:, 512:528], in_=psd[:, 0:16]), [psd.k, sz.k], [sz.k])
                    cx.dma("sp", [(z_tok[c * 128:(c + 1) * 128, :], sz[:, 0:512]), (dt_tok[c * 128:(c + 1) * 128, :], sz[:, 512:528])],
                           [sz.k], [ms.zk[c], ms.dtk[c]], sz.k)

        def load_win(win, fc, ti, halo, ms):
            t0 = ti * TM
            lo, hi = max(t0 - halo, 0), min(t0 + TM + halo, NT)
            rk = [ms.uk[fc][ti]]
            if ti > 0:
                rk.append(ms.uk[fc][ti - 1])
            if ti < NT // TM - 1:
                rk.append(ms.uk[fc][ti + 1])
            cx.dma("sp", [(win[:, lo - (t0 - halo):hi - (t0 - halo)], uT[fc, :, lo:hi])], rk, [win.k], win.k)
            W = TM + 2 * halo
            if t0 == 0:
                cx.op("pool", lambda e: e.memset(win[:, 0:halo], 0.0), [], [win.k])
            if t0 + TM == NT:
                cx.op("pool", lambda e: e.memset(win[:, W - halo:W], 0.0), [], [win.k])
            if t0 == SEG:
                cx.op("pool", lambda e: e.tensor_scalar(out=win[:, 0:halo], in0=win[:, 0:halo], scalar1=flag_t[:, 0:1], scalar2=1.0,
                                                        op0=ALU.mult, op1=ALU.mult), [win.k, flag_t.k], [win.k])
            if t0 + TM == SEG:
                cx.op("pool", lambda e: e.tensor_scalar(out=win[:, W - halo:W], in0=win[:, W - halo:W], scalar1=flag_t[:, 0:1], scalar2=1.0,
                                                        op0=ALU.mult, op1=ALU.mult), [win.k, flag_t.k], [win.k])

        def m2_ssdprep(l, ms):
            ms.xs_tok = Buf(cx, "xs_tok", [128, NC, 512], BF16)
            ms.B_tok = Buf(cx, "B_tok", [128, NC, 256], BF16)
            ms.BT = Buf(cx, "BT", [128, 2, NT], BF16)
            ms.CT = Buf(cx, "CT", [128, 2, NT], BF16)
            ms.sk = [Tk("ssd%d" % c) for c in range(NC)]
            win = [Buf(cx, "m2_win%d" % i, [128, TM + 4], F32) for i in range(2)]
            acc = [Buf(cx, "m2_acc%d" % i, [128, TM], F32) for i in range(2)]
            cvo = [Buf(cx, "m2_cvo%d" % i, [128, TM], BF16) for i in range(2)]
            n = 0
            for ti in range(NT // TM):
                t0 = ti * TM
                cks = [ms.sk[t0 // 128 + j] for j in range(TM // 128)]
                for j8 in range(8):
                    fc = 15 + j8
                    w = win[n % 2]
                    a = acc[n % 2]
                    o = cvo[n % 2]
                    n += 1
                    load_win(w, fc, ti, 2, ms)
                    cx.op("pool", lambda e: e.tensor_scalar(out=a[:], in0=w[:, 0:TM], scalar1=ms.pp[:, j8 * 5:j8 * 5 + 1], scalar2=1.0,
                                                            op0=ALU.mult, op1=ALU.mult), [w.k, ms.pp.k], [a.k])
                    for j in range(1, 5):
                        cx.op("dve", lambda e, j=j: e.scalar_tensor_tensor(out=a[:], in0=w[:, j:j + TM], scalar=ms.pp[:, j8 * 5 + j:j8 * 5 + j + 1],
                                                                           in1=a[:], op0=ALU.mult, op1=ALU.add), [w.k, ms.pp.k, a.k], [a.k])
                    if j8 < 4 or j8 < 6:
                        cx.op("act", lambda e: e.activation(out=o[:], in_=a[:], func=AF.Silu, bias=ms.pp[:, 40 + j8:41 + j8]), [a.k, ms.pp.k], [o.k])
                        for sub in range(TM // 128):
                            cx.op("pe", lambda e, sub=sub: e.transpose(out=PT[:, sub * 128:(sub + 1) * 128], in_=o[:, sub * 128:(sub + 1) * 128],
                                                                      identity=ident_b[:]), [o.k, ident_b.k], [PT.k])
                        c0 = t0 // 128
                        if j8 < 4:
                            dstv = ms.xs_tok[:, c0:c0 + 4, j8 * 128:(j8 + 1) * 128]
                        else:
                            dstv = ms.B_tok[:, c0:c0 + 4, (j8 - 4) * 128:(j8 - 3) * 128]
                        cx.op("dve", lambda e: e.tensor_copy(out=dstv, in_=V4(PT[:, 0:512], 4)), [PT.k], cks)
                        if j8 >= 4:
                            cx.op("pool", lambda e: e.tensor_copy(out=ms.BT[:, j8 - 4, t0:t0 + TM], in_=o[:]), [o.k], cks)
                    else:
                        cx.op("act", lambda e: e.activation(out=ms.CT[:, j8 - 6, t0:t0 + TM], in_=a[:], func=AF.Silu, bias=ms.pp[:, 40 + j8:41 + j8]),
                              [a.k, ms.pp.k], cks)

        def m3_ssd(l, ms):
            dtt = [Buf(cx, "m3_dtt%d" % i, [128, 16], F32) for i in range(2)]
            zt = [Buf(cx, "m3_zt%d" % i, [128, 512], F32) for i in range(2)]
            yprev = [Buf(cx, "m3_yp%d" % i, [128, 512], F32) for i in range(2)]
            t1 = Buf(cx, "m3_t1", [128, 8], F32)
            dts = Buf(cx, "m3_dts", [128, 8], F32)
            a_t = Buf(cx, "m3_a", [128, 8], F32)
            lmh = Buf(cx, "m3_lmh", [128, 8, 128], F32)
            ex = Buf(cx, "m3_ex", [128, 24], F32)
            Lm = Buf(cx, "m3_Lm", [128, 8, 128], F32)
            CBm = Buf(cx, "m3_CBm", [128, 2, 128], F32)
            G = Buf(cx, "m3_G", [128, 8, 128], BF16)
            xdt = Buf(cx, "m3_xdt", [128, 8, 64], BF16)
            xw = Buf(cx, "m3_xw", [128, 8, 64], BF16)
            ytmp = Buf(cx, "m3_ytmp", [128, 512], F32)
            ycur = [Buf(cx, "m3_ycur%d" % i, [128, 512], F32) for i in range(2)]
            S = Buf(cx, "m3_S", [128, 512], F32)
            Sb = Buf(cx, "m3_Sb", [128, 512], BF16)
            y2 = Buf(cx, "m3_y2", [128, 512], F32)
            szt = Buf(cx, "m3_sz", [128, 512], F32)
            junk = Buf(cx, "m3_junk", [128, 512], BF16)
            ssum = Buf(cx, "m3_ssum", [128, 1], F32)
            ysn = Buf(cx, "m3_ysn", [128, 512], BF16)
            ysT = [Buf(cx, "m3_ysT%d" % i, [128, 4, 128], BF16) for i in range(2)]
            yk = [Tk("ysacc%d" % c) for c in range(NC)]
            psA0, psA1, psB, psC, psY, psO, psS = PS
            for d in range(2):
                m1i, m2i, vi = (0, 1, 1) if d == 0 else (2, 3, 3)
                cx.op("dve", lambda e: e.memset(S[:], 0.0), [], [S.k])
                cx.op("pool", lambda e: e.memset(Sb[:], 0.0), [], [Sb.k])
                order = range(NC) if d == 0 else range(NC - 1, -1, -1)
                for n, c in enumerate(order):
                    tok = slice(c * 128, (c + 1) * 128)
                    dtb = dtt[n % 2]
                    cx.dma("sp", [(dtb[:], dt_tok[tok, :])], [ms.dtk[c]], [dtb.k], dtb.k)
                    if (d == 0 and c == NC // 2) or (d == 1 and c == NC // 2 - 1):
                        cx.op("dve", lambda e: e.tensor_scalar(out=S[:], in0=S[:], scalar1=flag_t[:, 0:1], scalar2=1.0, op0=ALU.mult, op1=ALU.mult),
                              [S.k, flag_t.k], [S.k])
                        cx.op("act", lambda e: e.activation(out=Sb[:], in_=S[:], func=AF.Identity), [S.k], [Sb.k])
                    cx.op("dve", lambda e: e.tensor_tensor(out=t1[:], in0=dtb[:, d * 8:d * 8 + 8], in1=ms.bb[:, d * 8:d * 8 + 8], op=ALU.add),
                          [dtb.k, ms.bb.k], [t1.k])
                    cx.op("act", lambda e: e.activation(out=t1[:], in_=t1[:], func=AF.Exp), [t1.k], [t1.k])
                    cx.op("act", lambda e: e.activation(out=dts[:], in_=t1[:], func=AF.Ln, bias=one_t[:]), [t1.k, one_t.k], [dts.k])
                    cx.op("dve", lambda e: e.tensor_tensor(out=a_t[:], in0=dts[:], in1=ms.aneg[:, d * 8:d * 8 + 8], op=ALU.mult),
                          [dts.k, ms.aneg.k], [a_t.k])
                    for h in range(8):
                        eng = "dve" if h % 2 == 0 else "pool"
                        cx.op(eng, lambda e, h=h: e.tensor_scalar(out=lmh[:, h, :], in0=mk_f[:, m1i, :], scalar1=a_t[:, h:h + 1], scalar2=1.0,
                                                                  op0=ALU.mult, op1=ALU.mult), [mk_f.k, a_t.k], [lmh.k])
                    for h in range(8):
                        psA = psA0 if h < 4 else psA1
                        cx.op("pe", lambda e, h=h, psA=psA: e.matmul(psA[:, (h % 4) * 128:(h % 4 + 1) * 128], lhsT=lmh[:, h, :], rhs=mk_f[:, m2i, :],
                                                                    start=True, stop=True), [lmh.k, mk_f.k], [psA.k])
                    cx.op("pe", lambda e: e.matmul(psB[:, 0:8], lhsT=mk_f[:, m1i, :], rhs=a_t[:], start=True, stop=True), [mk_f.k, a_t.k], [psB.k])
                    cx.op("pe", lambda e: e.matmul(psB[:, 8:16], lhsT=mk_f[:, m2i, :], rhs=a_t[:], start=True, stop=True), [mk_f.k, a_t.k], [psB.k])
                    cx.op("pe", lambda e: e.matmul(psB[:, 16:24], lhsT=ones_f[:], rhs=a_t[:], start=True, stop=True), [ones_f.k, a_t.k], [psB.k])
                    cx.op("act", lambda e: e.activation(out=ex[:], in_=psB[:, 0:24], func=AF.Exp), [psB.k], [ex.k])
                    cx.op("act", lambda e: e.activation(out=Lm[:, 0:4, :], in_=V4(psA0[:, 0:512], 4), func=AF.Exp), [psA0.k], [Lm.k])
                    cx.op("act", lambda e: e.activation(out=Lm[:, 4:8, :], in_=V4(psA1[:, 0:512], 4), func=AF.Exp), [psA1.k, Lm.k], [Lm.k])
                    for g in range(2):
                        cx.op("pe", lambda e, g=g: e.matmul(psC[:, g * 128:(g + 1) * 128], lhsT=ms.BT[:, g, tok], rhs=ms.CT[:, g, tok],
                                                            start=True, stop=True), [ms.sk[c]], [psC.k])
                    cx.op("dve", lambda e: e.tensor_tensor(out=CBm[:], in0=V4(psC[:, 0:256], 2),
                                                           in1=mk_f[:, vi, :].unsqueeze(1).to_broadcast([128, 2, 128]), op=ALU.mult),
                          [psC.k, mk_f.k], [CBm.k])
                    for g in range(2):
                        cx.op("dve", lambda e, g=g: e.tensor_tensor(out=G[:, g * 4:(g + 1) * 4, :], in0=Lm[:, g * 4:(g + 1) * 4, :],
                                                                    in1=CBm[:, g, :].unsqueeze(1).to_broadcast([128, 4, 128]), op=ALU.mult),
                              [Lm.k, CBm.k, G.k], [G.k])
                    cx.op("dve", lambda e: e.tensor_tensor(out=xdt[:], in0=V4(ms.xs_tok[:, c, :], 8),
                                                           in1=dts[:].unsqueeze(2).to_broadcast([128, 8, 64]), op=ALU.mult),
                          [ms.sk[c], dts.k], [xdt.k])
                    cx.op("pool", lambda e: e.tensor_tensor(out=xw[:], in0=xdt[:], in1=ex[:, 0:8].unsqueeze(2).to_broadcast([128, 8, 64]),
                                                            op=ALU.mult), [xdt.k, ex.k], [xw.k])
                    for h in range(8):
                        cx.op("pe", lambda e, h=h: e.matmul(psY[:, h * 64:(h + 1) * 64], lhsT=G[:, h, :], rhs=xdt[:, h, :], start=True, stop=True),
                              [G.k, xdt.k], [psY.k])
                    for g in range(2):
                        cx.op("pe", lambda e, g=g: e.matmul(psO[:, g * 256:(g + 1) * 256], lhsT=ms.CT[:, g, tok], rhs=Sb[:, g * 256:(g + 1) * 256],
                                                            start=True, stop=True), [ms.sk[c], Sb.k], [psO.k])
                    cx.op("dve", lambda e: e.tensor_tensor(out=V4(ytmp[:], 8), in0=V4(psO[:, 0:512], 8),
                                                           in1=ex[:, 8:16].unsqueeze(2).to_broadcast([128, 8, 64]), op=ALU.mult),
                          [psO.k, ex.k], [ytmp.k])
                    yc = ycur[n % 2]
                    cx.op("dve", lambda e: e.tensor_tensor(out=yc[:], in0=ytmp[:], in1=psY[:, 0:512], op=ALU.add), [ytmp.k, psY.k], [yc.k])
                    for g in range(2):
                        cx.op("pe", lambda e, g=g: e.matmul(psS[:, g * 256:(g + 1) * 256], lhsT=ms.B_tok[:, c, g * 128:(g + 1) * 128],
                                                            rhs=xw[:, g * 4:(g + 1) * 4, :].rearrange("p a b -> p (a b)"), start=True, stop=True),
                              [ms.sk[c], xw.k], [psS.k])
                    cx.op("dve", lambda e: e.tensor_tensor(out=V4(S[:], 8), in0=V4(S[:], 8),
                                                           in1=ex[:, 16:24].unsqueeze(2).to_broadcast([128, 8, 64]), op=ALU.mult),
                          [S.k, ex.k, Sb.k], [S.k])
                    cx.op("dve", lambda e: e.tensor_tensor(out=S[:], in0=S[:], in1=psS[:, 0:512], op=ALU.add), [S.k, psS.k], [S.k])
                    cx.op("act", lambda e: e.activation(out=Sb[:], in_=S[:], func=AF.Identity), [S.k], [Sb.k])
                    if d == 0:
                        cx.dma("sp", [(ysacc[tok, :], yc[:])], [yc.k], [yk[c]], yc.k)
                        continue
                    yp = yprev[n % 2]
                    z = zt[n % 2]
                    cx.dma("sp", [(yp[:], ysacc[tok, :])], [yk[c]], [yp.k], yp.k)
                    cx.dma("sp", [(z[:], z_tok[tok, :])], [ms.zk[c]], [z.k], z.k)
                    cx.op("dve", lambda e: e.tensor_tensor(out=yc[:], in0=yc[:], in1=yp[:], op=ALU.add), [yc.k, yp.k], [yc.k])
                    cx.op("pool", lambda e: e.tensor_tensor(out=y2[:], in0=ms.xs_tok[:, c, :], in1=ms.bb[:, 32:544], op=ALU.mult),
                          [ms.sk[c], ms.bb.k], [y2.k])
                    cx.op("dve", lambda e: e.tensor_tensor(out=y2[:], in0=y2[:], in1=yc[:], op=ALU.add), [y2.k, yc.k], [y2.k])
                    cx.op("act", lambda e: e.activation(out=szt[:], in_=z[:], func=AF.Silu), [z.k], [szt.k])
                    cx.op("dve", lambda e: e.tensor_tensor(out=y2[:], in0=y2[:], in1=szt[:], op=ALU.mult), [y2.k, szt.k], [y2.k])
                    cx.op("act", lambda e: e.activation(out=junk[:], in_=y2[:], func=AF.Square, accum_out=ssum[:]), [y2.k], [junk.k, ssum.k])
                    cx.op("act", lambda e: e.activation(out=ssum[:], in_=ssum[:], func=AF.Sqrt, bias=eps_t[:], scale=1.0 / 512), [ssum.k, eps_t.k], [ssum.k])
                    cx.op("dve", lambda e: e.reciprocal(out=ssum[:], in_=ssum[:]), [ssum.k], [ssum.k])
                    cx.op("pool", lambda e: e.tensor_scalar(out=y2[:], in0=y2[:], scalar1=ssum[:, 0:1], scalar2=1.0, op0=ALU.mult, op1=ALU.mult),
                          [y2.k, ssum.k], [y2.k])
                    cx.op("dve", lambda e: e.tensor_tensor(out=ysn[:], in0=y2[:], in1=ms.bb[:, 544:1056], op=ALU.mult), [y2.k, ms.bb.k], [ysn.k])
                    for j in range(4):
                        cx.op("pe", lambda e, j=j: e.transpose(out=PT[:, j * 128:(j + 1) * 128], in_=ysn[:, j * 128:(j + 1) * 128],
                                                              identity=ident_b[:]), [ysn.k, ident_b.k], [PT.k])
                    yst = ysT[n % 2]
                    cx.op("act", lambda e: e.activation(out=yst[:], in_=V4(PT[:, 0:512], 4), func=AF.Identity), [PT.k], [yst.k])
                    cx.dma("sp", [(ymix_d[4:8, :, tok].rearrange("k p t -> p k t"), yst[:])], [yst.k], [ms.ymk[1][c]], yst.k)

        def m5_outproj(l, src, dst, ms):
            woutb = Buf(cx, "woutb", [128, KD, D], BF16)
            cx.dma("pool", [(woutb[:].rearrange("p k c -> p (k c)").rearrange("p (a b) -> p a b", b=2048),
                             w_out[l].rearrange("p (a b) -> p a b", b=2048))], [], [woutb.k], woutb.k)
            xt = [Buf(cx, "m5_xt%d" % i, [128, KD, TM], F32) for i in range(2)]
            sq = Buf(cx, "m5_sq", [128, KD, TM], BF16)
            std = Buf(cx, "m5_std", [128, TM], F32)
            rstd = Buf(cx, "m5_rstd", [128, TM], F32)
            fT = Buf(cx, "m5_fT", [128, KD, TM], F32)
            ymtb = [Buf(cx, "m5_ymt%d" % i, [128, KD, TM], BF16) for i in range(2)]
            dk = [Tk("m5_dst%d" % i) for i in range(NT // TM)]
            for ti in range(NT // TM):
                t0 = ti * TM
                seg = t0 // SEG
                x = xt[ti % 2]
                cx.dma("sp", [(x[:], src[:, :, t0:t0 + TM].rearrange("k p t -> p k t"))], [], [x.k], x.k)
                rks = [ms.ymk[h][t0 // 128 + j] for h in range(2) for j in range(TM // 128)]
                ymt = ymtb[ti % 2]
                cx.dma("sp", [(ymt[:], ymix_d[:, :, t0:t0 + TM].rearrange("k p t -> p k t"))], rks, [ymt.k], ymt.k)
                if debug and l == 0:
                    cx.op("dve", lambda e: e.tensor_copy(out=fT[:], in_=ymt[:]), [ymt.k], [fT.k])
                    cx.dma("sp", [(dbg_out[:, :, t0:t0 + TM].rearrange("k p t -> p k t"), fT[:])], [fT.k], [], fT.k)

                def psrc(o):
                    ps = PS[1 + o % 4]
                    for k in range(KD):
                        cx.op("pe", lambda e, k=k: e.matmul(ps[:, 0:TM], lhsT=woutb[:, k, o * 128:(o + 1) * 128], rhs=ymt[:, k, :],
                                                            start=(k == 0), stop=(k == KD - 1)), [woutb.k, ymt.k], [ps.k])
                    return ps
                post_tile(x, seg, TM, psrc, KD, sq, std, rstd, fT, G1)
                cx.dma("sp", [(dst[:, :, t0:t0 + TM].rearrange("k p t -> p k t"), x[:])], [x.k], [dk[ti]], x.k)


        KAP = 0.6065306597126334

        def m4_rwkv(l, ms):
            NTL = NT // TM
            wupb = Buf(cx, "wupb", [128, 512], BF16)
            aupb = Buf(cx, "aupb", [128, 512], BF16)
            gupb = Buf(cx, "gupb", [128, 512], BF16)
            cx.dma("pool", [(wupb[:], wup_in[l])], [], [wupb.k], wupb.k)
            cx.dma("pool", [(aupb[:], aup_in[l])], [], [aupb.k], aupb.k)
            cx.dma("pool", [(gupb[:], gup_in[l])], [], [gupb.k], gupb.k)
            c0 = Buf(cx, "c0", [128, 15], F32)
            omka = Buf(cx, "omka", [128, 4], F32)
            cx.op("dve", lambda e: e.tensor_tensor(out=c0[:], in0=ms.pp[:, 48:63], in1=ms.pp[:, 63:78], op=ALU.add), [ms.pp.k], [c0.k])
            cx.op("dve", lambda e: e.tensor_scalar(out=c0[:], in0=c0[:], scalar1=-1.0, scalar2=1.0, op0=ALU.mult, op1=ALU.add), [c0.k], [c0.k])
            cx.op("dve", lambda e: e.tensor_scalar(out=omka[:], in0=ms.pp[:, 98:102], scalar1=-1.0, scalar2=1.0, op0=ALU.mult, op1=ALU.add),
                  [ms.pp.k], [omka.k])
            m4m = [Buf(cx, "m4m%d" % d, [128, 4, 128], F32) for d in range(2)]
            for d in range(2):
                for q in range(4):
                    src_i = ((2, 1) if d == 0 else (0, 3))[q % 2]
                    cx.op("pool", lambda e, d=d, q=q, src_i=src_i: e.tensor_copy(out=m4m[d][:, q, :], in_=mk_f[:, src_i, :]), [mk_f.k], [m4m[d].k])
            FB = lambda nm: Buf(cx, nm, [128, TM], F32)
            BB = lambda nm: Buf(cx, nm, [128, TM], BF16)
            win = [Buf(cx, "m4_win%d" % i, [128, TM + 2], F32) for i in range(2)]
            nwin = [0]
            wl = FB("m4_wl")
            al = gl = wl
            twl, alb, sgl = BB("m4_twl"), BB("m4_alb"), BB("m4_sgl")
            rs = [FB("m4_rs%d" % g) for g in range(4)]
            ks = [FB("m4_ks%d" % g) for g in range(4)]
            vs = [FB("m4_vs%d" % g) for g in range(4)]
            AR = [Buf(cx, "m4_AR%d" % g, [128, 4, 256], BF16) for g in range(4)]
            ARh = [Buf(cx, "m4_ARh%d" % h, [128, 4, 256], BF16) for h in range(8)]
            kt = [BB("m4_kt%d" % g) for g in range(4)]
            bt = [BB("m4_bt%d" % g) for g in range(4)]
            v_tok = Buf(cx, "m4_vtok", [128, 4, 512], BF16)
            kh_tok = Buf(cx, "m4_khtok", [128, 4, 512], BF16)
            bh_tok = Buf(cx, "m4_bhtok", [128, 4, 512], BF16)
            gam = Buf(cx, "m4_gam", [128, 4, 4], F32)
            class _TS:
                pass
            TS = []
            for si in range(1):
                T = _TS()
                for n_ in "sg E1 E0 R0 R1 eN eP eA eH a_s kkr nrm kk tt kd bq".split():
                    setattr(T, n_, FB("m4_%s%d" % (n_, si)))
                for n_ in "sqk khT bhT vT".split():
                    setattr(T, n_, BB("m4_%s%d" % (n_, si)))
                TS.append(T)
            PSS = [[PS[0], PS[1], PS[2]], [PS[3], PS[5], PS[6]]]
            sg, E1, E0, R0, R1, sqk, khT = TS[0].sg, TS[0].E1, TS[0].E0, TS[0].R0, TS[0].R1, TS[0].sqk, TS[0].khT
            SC4 = [Buf(cx, "m4_SC%d" % i, [128, 4, 512], BF16) for i in range(2)]
            XT0s = [Buf(cx, "m4_XT0_%d" % i, [128, 4, 128], BF16) for i in range(2)]
            TT = [Buf(cx, "m4_TT%d" % i, [128, 4, 128], BF16) for i in range(2)]
            IB = []
            for si in range(2):
                mkb = lambda nm, n: [Buf(cx, "m4_%s%d_%d" % (nm, si, i), [128, 4, 128], BF16) for i in range(n)]
                IB.append((mkb("X", 2), mkb("XT", 2), mkb("R", 3), mkb("Noff", 3), mkb("Loff", 3), mkb("Dp", 2), mkb("DTp", 2),
                           mkb("M1b", 1)[0], mkb("M2b", 1)[0]))
            IBANK = [(PS[0], PS[1], PS[2]), (PS[3], PS[5], PS[6])]
            Wb = Buf(cx, "m4_Wb", [128, 512], BF16)
            Uneg = Buf(cx, "m4_Uneg", [128, 512], BF16)
            Hf = Buf(cx, "m4_Hf", [128, 4, 128], F32)
            Hb = Buf(cx, "m4_Hb", [128, 4, 128], BF16)
            tmpH = Buf(cx, "m4_tmpH", [128, 4, 128], F32)
            Ytile = Buf(cx, "m4_Ytile", [128, 4, TM], F32)
            yfw = Buf(cx, "m4_yfw", [128, 4, TM], F32) if DBG == "rdbg" else None
            yfw1 = FB("m4_yfw1")
            yv, ycn, yn2, rk, bon = sg, E1, E0, R0, R1
            ybf, rkb = sqk, khT
            yob = [BB("m4_yo%d" % i) for i in range(2)]
            yrk = [Tk("yracc%d" % ti) for ti in range(NTL)]

            def rr(gens):
                active = list(gens)
                while active:
                    for gg in list(active):
                        try:
                            next(gg)
                        except StopIteration:
                            active.remove(gg)

            def shift(fc, ti, out):
                w = win[nwin[0] % 2]
                nwin[0] += 1
                load_win(w, fc, ti, 1, ms)
                cx.op("pool", lambda e: e.tensor_scalar(out=out[:], in0=w[:, 1:TM + 1], scalar1=c0[:, fc:fc + 1], scalar2=1.0, op0=ALU.mult, op1=ALU.mult),
                      [w.k, c0.k], [out.k])
                cx.op("dve", lambda e: e.scalar_tensor_tensor(out=out[:], in0=w[:, 0:TM], scalar=ms.pp[:, 48 + fc:49 + fc], in1=out[:],
                                                              op0=ALU.mult, op1=ALU.add), [w.k, ms.pp.k, out.k], [out.k])
                cx.op("dve", lambda e: e.scalar_tensor_tensor(out=out[:], in0=w[:, 2:TM + 2], scalar=ms.pp[:, 63 + fc:64 + fc], in1=out[:],
                                                              op0=ALU.mult, op1=ALU.add), [w.k, ms.pp.k, out.k], [out.k])

            def tt_(eng, out, a, b, op, rd, wr):
                cx.op(eng, lambda e: e.tensor_tensor(out=out, in0=a, in1=b, op=op), rd, wr)

            def prep(d, ti):
                shift(12, ti, wl)
                cx.op("act", lambda e: e.activation(out=twl[:], in_=wl[:], func=AF.Tanh), [wl.k], [twl.k])
                shift(13, ti, al)
                cx.op("act", lambda e: e.activation(out=alb[:], in_=al[:], func=AF.Identity), [al.k], [alb.k])
                if d == 1:
                    shift(14, ti, gl)
                    cx.op("act", lambda e: e.activation(out=sgl[:], in_=gl[:], func=AF.Sigmoid), [gl.k], [sgl.k])
                dp = slice(d * 64, d * 64 + 64)
                def prep_g(g, T, PSg):
                    sg, E1, E0, R0, R1, eN, eP, eA, eH, a_s, kkr, nrm, kk, tt, kd, bq, sqk, khT, bhT, vT = [getattr(T, n_) for n_ in 'sg, E1, E0, R0, R1, eN, eP, eA, eH, a_s, kkr, nrm, kk, tt, kd, bq, sqk, khT, bhT, vT'.split(', ')]
                    gs = slice(g * 128, (g + 1) * 128)
                    X1, X0e, Y0 = (E1, E0, R0) if d == 0 else (R1, R0, E0)

                    def transp(srcT, dstk):
                        for j in range(4):
                            cx.op("pe", lambda e, j=j: e.transpose(out=PT[:, j * 128:(j + 1) * 128], in_=srcT[:, j * 128:(j + 1) * 128],
                                                                  identity=ident_b[:]), [srcT.k, ident_b.k], [PT.k])
                        cx.op("dve", lambda e: e.tensor_copy(out=dstk[:, :, gs], in_=V4(PT[:, 0:512], 4)), [PT.k, dstk.k], [dstk.k])

                    def chainA():
                        cx.op("pe", lambda e: e.matmul(PSg[0][:, 0:TM], lhsT=wupb[dp, gs], rhs=twl[dp, :], start=True, stop=True), [wupb.k, twl.k], [PSg[0].k])
                        yield
                        cx.op("act", lambda e: e.activation(out=sg[:], in_=PSg[0][:, 0:TM], func=AF.Sigmoid, bias=ms.pp[:, 78 + d * 4 + g:79 + d * 4 + g]),
                              [PSg[0].k, ms.pp.k], [sg.k])
                        yield
                        cx.op("dve", lambda e: e.tensor_tensor_scan(out=E1[:], data0=rmask[:], data1=sg[:], initial=0.0, op0=ALU.mult, op1=ALU.add),
                              [rmask.k, sg.k], [E1.k])
                        yield
                        tt_("pool", E0[:], E1[:], sg[:], ALU.subtract, [E1.k, sg.k], [E0.k])
                        yield
                        tt_("dve", V4(R0[:], 4), V4(E1[:], 4)[:, :, 127:128].to_broadcast([128, 4, 128]), V4(E1[:], 4), ALU.subtract, [E1.k], [R0.k])
                        yield
                        tt_("pool", R1[:], R0[:], sg[:], ALU.add, [R0.k, sg.k], [R1.k])
                        yield
                        cx.op("act", lambda e: e.activation(out=eN[:], in_=X1[:], func=AF.Exp, scale=-KAP), [X1.k], [eN.k])
                        yield
                        cx.op("act", lambda e: e.activation(out=eP[:], in_=X1[:], func=AF.Exp, scale=KAP), [X1.k], [eP.k])
                        yield
                        cx.op("act", lambda e: e.activation(out=eA[:], in_=X0e[:], func=AF.Exp, scale=-KAP), [X0e.k], [eA.k])
                        yield
                        cx.op("act", lambda e: e.activation(out=eH[:], in_=Y0[:], func=AF.Exp, scale=-KAP), [Y0.k], [eH.k])
                        yield
                        cx.op("act", lambda e: e.activation(out=gam[:, g, :], in_=V4(E1[:], 4)[:, :, 127], func=AF.Exp, scale=-KAP), [E1.k], [gam.k])
                        yield

                    def chainB():
                        shift(4 + g, ti, ks[g])
                        yield
                        cx.op("pool", lambda e: e.tensor_scalar(out=kkr[:], in0=ks[g][:], scalar1=ms.pp[:, 94 + g:95 + g], scalar2=1.0, op0=ALU.mult, op1=ALU.mult),
                              [ks[g].k, ms.pp.k], [kkr.k])
                        yield
                        cx.op("act", lambda e: e.activation(out=sqk[:], in_=kkr[:], func=AF.Square), [kkr.k], [sqk.k])
                        yield
                        cx.op("pe", lambda e: e.matmul(PSg[2][:, 0:TM], lhsT=blk1_b[:], rhs=sqk[:], start=True, stop=True), [blk1_b.k, sqk.k], [PSg[2].k])
                        cx.op("pe", lambda e: e.matmul(PSg[1][:, 0:TM], lhsT=aupb[dp, gs], rhs=alb[dp, :], start=True, stop=True), [aupb.k, alb.k], [PSg[1].k])
                        yield
                        cx.op("dve", lambda e: e.tensor_scalar(out=nrm[:], in0=PSg[2][:, 0:TM], scalar1=2.0 ** -60, scalar2=None, op0=ALU.max), [PSg[2].k], [nrm.k])
                        yield
                        cx.op("act", lambda e: e.activation(out=a_s[:], in_=PSg[1][:, 0:TM], func=AF.Sigmoid, bias=ms.pp[:, 86 + d * 4 + g:87 + d * 4 + g]),
                              [PSg[1].k, ms.pp.k], [a_s.k])
                        cx.op("act", lambda e: e.activation(out=nrm[:], in_=nrm[:], func=AF.Ln), [nrm.k], [nrm.k])
                        yield
                        cx.op("act", lambda e: e.activation(out=nrm[:], in_=nrm[:], func=AF.Exp, scale=-0.5), [nrm.k], [nrm.k])
                        yield
                        cx.op("pool", lambda e: e.tensor_scalar(out=tt[:], in0=a_s[:], scalar1=ms.pp[:, 98 + g:99 + g], scalar2=omka[:, g:g + 1],
                                                                op0=ALU.mult, op1=ALU.add), [a_s.k, ms.pp.k, omka.k], [tt.k])
                        yield
                        tt_("dve", kk[:], kkr[:], nrm[:], ALU.mult, [kkr.k, nrm.k], [kk.k])
                        yield
                        tt_("dve", kd[:], ks[g][:], tt[:], ALU.mult, [ks[g].k, tt.k], [kd.k])
                        yield
                        tt_("pool", bq[:], kk[:], a_s[:], ALU.mult, [kk.k, a_s.k], [bq.k])
                        yield

                    def chainC():
                        shift(8 + g, ti, vs[g])
                        yield
                        cx.op("act", lambda e: e.activation(out=vT[:], in_=vs[g][:], func=AF.Identity), [vs[g].k], [vT.k])
                        yield
                        shift(g, ti, rs[g])
                        yield
                        transp(vT, v_tok)
                        yield

                    rr([chainA(), chainB(), chainC()])
                    tt_("dve", AR[g][:, :, 0:128], V4(kk[:], 4), V4(eA[:], 4), ALU.mult, [kk.k, eA.k], [AR[g].k])
                    tt_("pool", AR[g][:, :, 128:256], V4(rs[g][:], 4), V4(eN[:], 4), ALU.mult, [rs[g].k, eN.k, AR[g].k], [AR[g].k])
                    tt_("dve", khT[:], kd[:], eH[:], ALU.mult, [kd.k, eH.k], [khT.k])
                    tt_("pool", bhT[:], bq[:], eH[:], ALU.mult, [bq.k, eH.k], [bhT.k])
                    tt_("dve", kt[g][:], kd[:], eP[:], ALU.mult, [kd.k, eP.k], [kt[g].k])
                    tt_("pool", bt[g][:], bq[:], eP[:], ALU.mult, [bq.k, eP.k], [bt[g].k])
                    transp(khT, kh_tok)
                    for hl in range(2):
                        eng = "dve" if hl == 0 else "pool"
                        cx.op(eng, lambda e, hl=hl: e.tensor_scalar(out=ARh[2 * g + hl][:], in0=AR[g][:], scalar1=blk1_f[:, hl * 64:hl * 64 + 1], scalar2=1.0,
                                                                    op0=ALU.mult, op1=ALU.mult), [AR[g].k, blk1_f.k], [ARh[2 * g + hl].k])
                    transp(bhT, bh_tok)
                    return
                    yield

                for g in range(4):
                    for _ in prep_g(g, TS[0], PSS[0]):
                        pass

            def scan(d, ti, j):
                cs = slice(j * 128, (j + 1) * 128)
                mL = 0 if d == 0 else 2
                c = ti * 4 + j
                if (d == 0 and c == NC // 2) or (d == 1 and c == NC // 2 - 1):
                    cx.op("dve", lambda e: e.tensor_scalar(out=Hf[:], in0=Hf[:], scalar1=flag_t[:, 0:1], scalar2=1.0, op0=ALU.mult, op1=ALU.mult),
                          [Hf.k, flag_t.k], [Hf.k])
                    cx.op("act", lambda e: e.activation(out=Hb[:], in_=Hf[:], func=AF.Identity), [Hf.k], [Hb.k])
                for hg in range(2):
                    sc = SC4[hg]
                    for hh in range(4):
                        h = hg * 4 + hh
                        g = h // 2
                        po = slice((h % 2) * 64, (h % 2) * 64 + 64)
                        ps = PS[hh]
                        cx.op("pe", lambda e: e.matmul(ps[:, 0:256], lhsT=kt[g][:, cs], rhs=ARh[h][:, j, :], start=True, stop=True),
                              [kt[g].k, ARh[h].k], [ps.k])
                        cx.op("pe", lambda e: e.matmul(ps[:, 256:512], lhsT=bt[g][:, cs], rhs=ARh[h][:, j, :], start=True, stop=True),
                              [bt[g].k, ARh[h].k], [ps.k])
                        cx.op("pe", lambda e: e.matmul(PS[4][:, hh * 128:(hh + 1) * 128], lhsT=ARh[h][:, j, 0:128], rhs=bt[g][:, cs], start=True, stop=True),
                              [bt[g].k, ARh[h].k], [PS[4].k])
                        cx.op("dve", lambda e: e.tensor_tensor(out=sc[:, hh, :], in0=ps[:, 0:512], in1=m4m[d][:].rearrange("p a b -> p (a b)"), op=ALU.mult),
                              [ps.k, m4m[d].k, sc.k], [sc.k])
                    cx.op("dve", lambda e: e.tensor_tensor(out=XT0s[hg][:], in0=V4(PS[4][:, 0:512], 4),
                                                           in1=mk_f[:, mL, :].unsqueeze(1).to_broadcast([128, 4, 128]), op=ALU.mult),
                          [PS[4].k, mk_f.k], [XT0s[hg].k])
                def inv_g(hg):
                    sc = SC4[hg]
                    XT0 = XT0s[hg]
                    Xp, XTp, Rp, Noff, Loff, Dp, DTp, M1b, M2b = IB[hg]
                    P1, P2, P3 = IBANK[hg]
                    bmb = lambda q: bm_b[:, q, :].unsqueeze(1).to_broadcast([128, 4, 128])
                    N0, L0 = Xp[0], XTp[0]
                    cx.op("pool", lambda e: e.tensor_tensor(out=N0[:], in0=sc[:, :, 256:384], in1=bmb(0), op=ALU.mult), [sc.k, bm_b.k], [N0.k])
                    yield
                    cx.op("dve", lambda e: e.tensor_tensor(out=L0[:], in0=XT0[:], in1=bmb(0), op=ALU.mult), [XT0.k, bm_b.k], [L0.k])
                    yield
                    for q in range(3):
                        cx.op("pool", lambda e, q=q: e.tensor_tensor(out=Noff[q][:], in0=sc[:, :, 256:384], in1=bmb(q + 1), op=ALU.mult),
                              [sc.k, bm_b.k], [Noff[q].k])
                        yield
                        cx.op("dve", lambda e, q=q: e.tensor_tensor(out=Loff[q][:], in0=XT0[:], in1=bmb(q + 1), op=ALU.mult),
                              [XT0.k, bm_b.k], [Loff[q].k])
                        yield
                    R = Rp[2]
                    cx.op("pool", lambda e: e.tensor_tensor(out=R[:], in0=ident_b[:].unsqueeze(1).to_broadcast([128, 4, 128]), in1=N0[:],
                                                            op=ALU.subtract), [ident_b.k, N0.k], [R.k])
                    yield
                    Xc, XTc = N0, L0
                    for k in range(3):
                        XTn = XTp[(k + 1) % 2]
                        Xn = Xp[(k + 1) % 2]
                        for hh in range(4):
                            cx.op("pe", lambda e, hh=hh: e.matmul(P2[:, hh * 128:(hh + 1) * 128], lhsT=Xc[:, hh, :], rhs=XTc[:, hh, :], start=True, stop=True),
                                  [Xc.k, XTc.k], [P2.k])
                        if k < 2:
                            for hh in range(4):
                                cx.op("pe", lambda e, hh=hh: e.matmul(P1[:, hh * 128:(hh + 1) * 128], lhsT=XTc[:, hh, :], rhs=Xc[:, hh, :], start=True, stop=True),
                                      [Xc.k, XTc.k], [P1.k])
                        cx.op("dve", lambda e: e.tensor_copy(out=XTn[:], in_=V4(P2[:, 0:512], 4)), [P2.k], [XTn.k])
                        yield
                        if k < 2:
                            cx.op("act", lambda e: e.activation(out=Xn[:], in_=V4(P1[:, 0:512], 4), func=AF.Identity), [P1.k], [Xn.k])
                            yield
                        for hh in range(4):
                            cx.op("pe", lambda e, hh=hh: e.matmul(P3[:, hh * 128:(hh + 1) * 128], lhsT=XTn[:, hh, :], rhs=R[:, hh, :], start=True, stop=False),
                                  [XTn.k, R.k], [P3.k])
                            cx.op("pe", lambda e, hh=hh: e.matmul(P3[:, hh * 128:(hh + 1) * 128], lhsT=ident_b[:], rhs=R[:, hh, :], start=False, stop=True),
                                  [ident_b.k, R.k], [P3.k])
                        Rn = Rp[k % 2]
                        cx.op("act", lambda e: e.activation(out=Rn[:], in_=V4(P3[:, 0:512], 4), func=AF.Identity), [P3.k], [Rn.k])
                        yield
                        R = Rn
                        Xc, XTc = Xn, XTn
                    DT = R
                    Dn = Dp[0]
                    for hh in range(4):
                        cx.op("pe", lambda e, hh=hh: e.transpose(out=PT[:, hh * 128:(hh + 1) * 128], in_=DT[:, hh, :], identity=ident_b[:]),
                              [DT.k, ident_b.k], [PT.k])
                    cx.op("dve", lambda e: e.tensor_copy(out=Dn[:], in_=V4(PT[:, 0:512], 4)), [PT.k], [Dn.k])
                    yield
                    for q in range(3):
                        last = (q == 2)
                        DTn = TT[hg] if last else DTp[q % 2]
                        Dnn = Dp[(q + 1) % 2]
                        for hh in range(4):
                            cx.op("pe", lambda e, hh=hh: e.matmul(P1[:, hh * 128:(hh + 1) * 128], lhsT=Loff[q][:, hh, :], rhs=DT[:, hh, :], start=True, stop=True),
                                  [Loff[q].k, DT.k], [P1.k])
                        cx.op("act", lambda e: e.activation(out=M1b[:], in_=V4(P1[:, 0:512], 4), func=AF.Identity), [P1.k], [M1b.k])
                        yield
                        if not last:
                            for hh in range(4):
                                cx.op("pe", lambda e, hh=hh: e.matmul(P2[:, hh * 128:(hh + 1) * 128], lhsT=Noff[q][:, hh, :], rhs=Dn[:, hh, :], start=True, stop=True),
                                      [Noff[q].k, Dn.k], [P2.k])
                            cx.op("dve", lambda e: e.tensor_copy(out=M2b[:], in_=V4(P2[:, 0:512], 4)), [P2.k], [M2b.k])
                            yield
                        for hh in range(4):
                            cx.op("pe", lambda e, hh=hh: e.matmul(P3[:, hh * 128:(hh + 1) * 128], lhsT=Dn[:, hh, :], rhs=M1b[:, hh, :], start=True, stop=True),
                                  [Dn.k, M1b.k], [P3.k])
                        cx.op("dve", lambda e: e.tensor_tensor(out=DTn[:], in0=DT[:], in1=V4(P3[:, 0:512], 4), op=ALU.subtract), [DT.k, P3.k], [DTn.k])
                        yield
                        if not last:
                            for hh in range(4):
                                cx.op("pe", lambda e, hh=hh: e.matmul(P1[:, hh * 128:(hh + 1) * 128], lhsT=DT[:, hh, :], rhs=M2b[:, hh, :], start=True, stop=True),
                                      [DT.k, M2b.k], [P1.k])
                            cx.op("dve", lambda e: e.tensor_tensor(out=Dnn[:], in0=Dn[:], in1=V4(P1[:, 0:512], 4), op=ALU.subtract), [Dn.k, P1.k], [Dnn.k])
                            yield
                            Dn = Dnn
                        DT = DTn

                rr([inv_g(0), inv_g(1)])
                psW, psU, psY, psH = PS[0], PS[1], PS[2], PS[3]
                for g in range(4):
                    cx.op("pe", lambda e: e.matmul(psW[:, g * 128:(g + 1) * 128], lhsT=AR[g][:, j, 0:128], rhs=Hb[:, g, :], start=True, stop=False),
                          [AR[g].k, Hb.k], [psW.k])
                    for hl in range(2):
                        h = 2 * g + hl
                        hg, hh = divmod(h, 4)
                        cx.op("pe", lambda e: e.matmul(psW[:, h * 64:(h + 1) * 64], lhsT=SC4[hg][:, hh, 0:128], rhs=v_tok[:, j, h * 64:(h + 1) * 64],
                                                       start=False, stop=(hl == 1)), [SC4[hg].k, v_tok.k], [psW.k])
                cx.op("act", lambda e: e.activation(out=Wb[:], in_=psW[:, 0:512], func=AF.Identity), [psW.k], [Wb.k])
                for h in range(8):
                    hg, hh = divmod(h, 4)
                    cx.op("pe", lambda e: e.matmul(psU[:, h * 64:(h + 1) * 64], lhsT=TT[hg][:, hh, :], rhs=Wb[:, h * 64:(h + 1) * 64], start=True, stop=True),
                          [TT[hg].k, Wb.k], [psU.k])
                cx.op("act", lambda e: e.activation(out=Uneg[:], in_=psU[:, 0:512], func=AF.Identity, scale=-1.0), [psU.k], [Uneg.k])
                for g in range(4):
                    cx.op("pe", lambda e: e.matmul(psY[:, g * 128:(g + 1) * 128], lhsT=Hb[:, g, :], rhs=AR[g][:, j, 128:256], start=True, stop=False),
                          [AR[g].k, Hb.k], [psY.k])
                    for hl in range(2):
                        h = 2 * g + hl
                        hg, hh = divmod(h, 4)
                        po = slice(hl * 64, hl * 64 + 64)
                        cx.op("pe", lambda e: e.matmul(psY[po, g * 128:(g + 1) * 128], lhsT=v_tok[:, j, h * 64:(h + 1) * 64], rhs=SC4[hg][:, hh, 128:256],
                                                       start=False, stop=False), [SC4[hg].k, v_tok.k], [psY.k])
                        cx.op("pe", lambda e: e.matmul(psY[po, g * 128:(g + 1) * 128], lhsT=Uneg[:, h * 64:(h + 1) * 64], rhs=SC4[hg][:, hh, 384:512],
                                                       start=False, stop=True), [SC4[hg].k, Uneg.k], [psY.k])
                cx.op("act", lambda e: e.activation(out=Ytile[:, :, cs], in_=V4(psY[:, 0:512], 4), func=AF.Identity), [psY.k, Ytile.k], [Ytile.k])
                for g in range(4):
                    gs = slice(g * 128, (g + 1) * 128)
                    cx.op("pe", lambda e: e.matmul(psH[:, gs], lhsT=kh_tok[:, j, gs], rhs=v_tok[:, j, gs], start=True, stop=False),
                          [kh_tok.k, v_tok.k], [psH.k])
                    cx.op("pe", lambda e: e.matmul(psH[:, gs], lhsT=bh_tok[:, j, gs], rhs=Uneg[:, gs], start=False, stop=True),
                          [bh_tok.k, Uneg.k], [psH.k])
                cx.op("dve", lambda e: e.tensor_tensor(out=tmpH[:], in0=V4(psH[:, 0:512], 4), in1=blk1_f[:].unsqueeze(1).to_broadcast([128, 4, 128]),
                                                       op=ALU.mult), [psH.k, blk1_f.k], [tmpH.k])
                cx.op("dve", lambda e: e.tensor_tensor(out=Hf[:], in0=Hf[:], in1=gam[:, :, j:j + 1].to_broadcast([128, 4, 128]), op=ALU.mult),
                      [Hf.k, gam.k, Hb.k], [Hf.k])
                cx.op("dve", lambda e: e.tensor_tensor(out=Hf[:], in0=Hf[:], in1=tmpH[:], op=ALU.add), [Hf.k, tmpH.k], [Hf.k])
                cx.op("act", lambda e: e.activation(out=Hb[:], in_=Hf[:], func=AF.Identity), [Hf.k], [Hb.k])

            def finalize(ti):
                t0 = ti * TM
                cks = [ms.ymk[0][t0 // 128 + jj] for jj in range(TM // 128)]
                for g in range(4):
                    gs = slice(g * 128, (g + 1) * 128)
                    cx.dma("sp", [(yfw1[:], yracc[g, :, t0:t0 + TM])], [yrk[ti]], [yfw1.k], yfw1.k)
                    tt_("dve", yv[:], Ytile[:, g, :], yfw1[:], ALU.add, [Ytile.k, yfw1.k], [yv.k])
                    cx.op("act", lambda e: e.activation(out=ybf[:], in_=yv[:], func=AF.Identity), [yv.k], [ybf.k])
                    cx.op("pe", lambda e: e.matmul(PS[5][:, 0:TM], lhsT=blk64_b[:], rhs=ybf[:], start=True, stop=True), [blk64_b.k, ybf.k], [PS[5].k])
                    tt_("dve", ycn[:], yv[:], PS[5][:, 0:TM], ALU.subtract, [yv.k, PS[5].k], [ycn.k])
                    cx.op("act", lambda e: e.activation(out=ybf[:], in_=ycn[:], func=AF.Square), [ycn.k], [ybf.k])
                    cx.op("pe", lambda e: e.matmul(PS[6][:, 0:TM], lhsT=blk64_b[:], rhs=ybf[:], start=True, stop=True), [blk64_b.k, ybf.k], [PS[6].k])
                    cx.op("act", lambda e: e.activation(out=yn2[:], in_=PS[6][:, 0:TM], func=AF.Ln, bias=gneps_t[:]), [PS[6].k, gneps_t.k], [yn2.k])
                    cx.op("act", lambda e: e.activation(out=yn2[:], in_=yn2[:], func=AF.Exp, scale=-0.5), [yn2.k], [yn2.k])
                    tt_("dve", ycn[:], ycn[:], yn2[:], ALU.mult, [ycn.k, yn2.k], [ycn.k])
                    cx.op("pool", lambda e: e.tensor_scalar(out=yn2[:], in0=ycn[:], scalar1=ms.pp[:, 106 + g:107 + g], scalar2=ms.pp[:, 110 + g:111 + g],
                                                            op0=ALU.mult, op1=ALU.add), [ycn.k, ms.pp.k], [yn2.k])
                    cx.op("pool", lambda e: e.tensor_scalar(out=rk[:], in0=rs[g][:], scalar1=ms.pp[:, 102 + g:103 + g], scalar2=1.0, op0=ALU.mult, op1=ALU.mult),
                          [rs[g].k, ms.pp.k], [rk.k])
                    tt_("dve", rk[:], rk[:], ks[g][:], ALU.mult, [rk.k, ks[g].k], [rk.k])
                    cx.op("act", lambda e: e.activation(out=rkb[:], in_=rk[:], func=AF.Identity), [rk.k], [rkb.k])
                    cx.op("pe", lambda e: e.matmul(PS[5][:, 0:TM], lhsT=blk1_b[:], rhs=rkb[:], start=True, stop=True), [blk1_b.k, rkb.k], [PS[5].k])
                    tt_("dve", bon[:], PS[5][:, 0:TM], vs[g][:], ALU.mult, [PS[5].k, vs[g].k], [bon.k])
                    tt_("pool", yn2[:], yn2[:], bon[:], ALU.add, [yn2.k, bon.k], [yn2.k])
                    cx.op("pe", lambda e: e.matmul(PS[6][:, 0:TM], lhsT=gupb[:, gs], rhs=sgl[:], start=True, stop=True), [gupb.k, sgl.k], [PS[6].k])
                    yo = yob[g % 2]
                    cx.op("dve", lambda e: e.tensor_tensor(out=yo[:], in0=yn2[:], in1=PS[6][:, 0:TM], op=ALU.mult), [yn2.k, PS[6].k], [yo.k])
                    cx.dma("sp", [(ymix_d[g, :, t0:t0 + TM], yo[:])], [yo.k], cks, yo.k)

            for d in range(2):
                cx.op("dve", lambda e: e.memset(Hf[:], 0.0), [], [Hf.k])
                cx.op("pool", lambda e: e.memset(Hb[:], 0.0), [], [Hb.k])
                tiles = range(NTL) if d == 0 else range(NTL - 1, -1, -1)
                for ti in tiles:
                    prep(d, ti)
                    for j in (range(4) if d == 0 else range(3, -1, -1)):
                        if "m4scan" not in SKIP:
                            scan(d, ti, j)
                    if DBG == "rdbg" and d == 0 and ti == 0:
                        cx.op("dve", lambda e: e.tensor_copy(out=yfw[:, 0, :], in_=SC4[1][:, 0, :]), [SC4[1].k], [yfw.k])
                        cx.op("dve", lambda e: e.tensor_copy(out=yfw[:, 1, :], in_=SC4[1][:, 2, :]), [SC4[1].k], [yfw.k])
                        cx.op("dve", lambda e: e.tensor_copy(out=yfw[:, 2, :], in_=TT[1][:].rearrange("p a b -> p (a b)")), [TT[1].k], [yfw.k])
                        cx.op("dve", lambda e: e.tensor_copy(out=yfw[:, 3, :], in_=Wb[:]), [Wb.k], [yfw.k])
                        for slot in range(4):
                            cx.dma("sp", [(dbg_out[slot, :, 0:TM], yfw[:, slot, :])], [yfw.k], [], yfw.k)
                        cx.op("dve", lambda e: e.tensor_copy(out=yfw[:, 0, :], in_=Uneg[:]), [Uneg.k], [yfw.k])
                        cx.op("dve", lambda e: e.tensor_copy(out=yfw[:, 1, :], in_=v_tok[:, 3, :]), [v_tok.k], [yfw.k])
                        cx.op("dve", lambda e: e.tensor_copy(out=yfw[:, 2, :], in_=kt[2][:]), [kt[2].k], [yfw.k])
                        cx.op("dve", lambda e: e.tensor_copy(out=yfw[:, 3, 0:256], in_=AR[2][:, 3, :]), [AR[2].k], [yfw.k])
                        for slot in range(4):
                            cx.dma("sp", [(dbg_out[4 + slot, :, 0:(TM if slot < 3 else 256)], yfw[:, slot, 0:(TM if slot < 3 else 256)])], [yfw.k], [], yfw.k)
                        return
                    if d == 0:
                        cx.dma("sp", [(yracc[:, :, ti * TM:(ti + 1) * TM].rearrange("g p t -> p g t"), Ytile[:])], [Ytile.k], [yrk[ti]], Ytile.k)
                    else:
                        finalize(ti)


        def mixer_phase(l, src, dst, em):
            ms = MixState()
            mix_load_params(l, ms)
            with contextlib.ExitStack() as e1:
                cx.mem_es = e1
                fo = len(cx.owners)
                if "m1" not in SKIP:
                    m1_inproj(l, src, ms)
                cx.end_phase(fo)
            cx.mem_es = em
            with contextlib.ExitStack() as e2:
                cx.mem_es = e2
                fo = len(cx.owners)
                if "m2" not in SKIP:
                    m2_ssdprep(l, ms)
                if "m3" not in SKIP and "m2" not in SKIP:
                    m3_ssd(l, ms)
                cx.end_phase(fo)
            cx.mem_es = em
            with contextlib.ExitStack() as e4:
                cx.mem_es = e4
                fo = len(cx.owners)
                if "m4" not in SKIP:
                    m4_rwkv(l, ms)
                cx.end_phase(fo)
            cx.mem_es = em
            with contextlib.ExitStack() as e5:
                cx.mem_es = e5
                fo = len(cx.owners)
                if DBG != "rdbg":
                    m5_outproj(l, src, dst, ms)
                cx.end_phase(fo)
            cx.mem_es = em

        out_tks = []
        cur = xT
        for l in range(depth):
            with contextlib.ExitStack() as fes:
                cx.mem_es = fes
                fo = len(cx.owners)
                mod_phase(l)
                cx.end_phase(fo)
                cx.mem_es = es
            import os
            if do_mix:
                mdst = xres if (do_ffn or l < depth - 1) else yT
                with contextlib.ExitStack() as em:
                    cx.mem_es = em
                    fo = len(cx.owners)
                    mixer_phase(l, cur, mdst, em)
                    cx.end_phase(fo)
                    cx.mem_es = es
                cur = mdst
            if DBG in ("mod", "modmm", "const", "modtt"):
                dbg = Buf(cx, "dbg", [128, 512], F32)
                cx.op("dve", lambda e: e.tensor_copy(out=dbg[:, 0:96], in_=modT[:].rearrange("p c s -> p (c s)")), [modT.k], [dbg.k])
                cx.op("dve", lambda e: e.tensor_copy(out=dbg[:, 96:112], in_=A2[:].rearrange("p c s -> p (c s)")), [A2.k], [dbg.k])
                cx.dma("sp", [(yT[0, :, 0:512], dbg[:])], [dbg.k], [], dbg.k)
                break
            if do_ffn and "ffn" not in SKIP:
                last = (l == depth - 1)
                dst = yT if last else xres
                with contextlib.ExitStack() as fes:
                    cx.mem_es = fes
                    fo = len(cx.owners)
                    out_tks = ffn_phase(l, cur, dst)
                    cx.end_phase(fo)
                    cx.mem_es = es
                cur = dst
        cx.barrier()
    return nc


def _prep_core_inputs(inputs, depth=4):
    f = np.float32
    sh = {}
    wm = np.asarray(inputs["w_mod"], f)[:depth]
    sh["w_mod"] = np.ascontiguousarray(wm.reshape(depth, KD, 128, NMOD * D).transpose(0, 2, 1, 3))
    bm = np.asarray(inputs["b_mod"], f)[:depth]
    sh["b_mod"] = np.ascontiguousarray(bm.reshape(depth, NMOD * KD, 128).transpose(0, 2, 1))
    ng = np.asarray(inputs["norm_g"], f)[:depth]
    sh["norm_g"] = np.ascontiguousarray(ng.reshape(depth, 4, KD, 128).transpose(0, 3, 1, 2))
    w1 = np.asarray(inputs["w_ff1"], f)[:depth]
    sh["w_ff1"] = np.ascontiguousarray(w1.reshape(depth, KD, 128, DFF).transpose(0, 2, 1, 3)).reshape(depth, 128, KD * DFF)
    w2 = np.asarray(inputs["w_ff2"], f)[:depth]
    sh["w_ff2"] = np.ascontiguousarray(w2.reshape(depth, DFF // 128, 128, D).transpose(0, 2, 1, 3)).reshape(depth, 128, (DFF // 128) * D)
    sh["c_ones"] = np.ones((128, 128), f)
    sh["c_ident"] = np.eye(128, dtype=f)
    b1 = np.zeros((128, 128), f)
    b1[:64, :64] = 1.0
    b1[64:, 64:] = 1.0
    sh["c_blk1"] = b1
    ii = np.arange(128)
    Bk = lambda b: ((ii[:, None] // b) == (ii[None, :] // b)).astype(f)
    sh["c_bmask"] = np.ascontiguousarray(np.stack([Bk(16), Bk(32) - Bk(16), Bk(64) - Bk(32), 1.0 - Bk(64)], axis=1))
    rm = np.ones((128, 512), f)
    rm[:, ::128] = 0.0
    sh["c_rmask"] = rm
    r = np.arange(128)[:, None]
    c = np.arange(128)[None, :]
    sh["c_masks"] = np.ascontiguousarray(np.stack([(r > c), (r <= c), (r < c), (r >= c)], axis=1).astype(f))
    wi = np.asarray(inputs["w_in"], f)[:depth]
    RW = 1920
    wi2 = np.zeros((depth, D, 3584), f)
    wi2[:, :, 0:RW] = wi[:, :, 0:RW]
    wi2[:, :, RW:RW + 1024] = wi[:, :, RW + 512:RW + 1536]
    wi2[:, :, 2944:3456] = wi[:, :, RW:RW + 512]
    wi2[:, :, 3456:3472] = wi[:, :, RW + 1536:RW + 1552]
    sh["w_in"] = np.ascontiguousarray(wi2.reshape(depth, KD, 128, 3584).transpose(0, 2, 1, 3)).reshape(depth, 128, KD * 3584)
    wo = np.asarray(inputs["w_out"], f)[:depth]
    sh["w_out"] = np.ascontiguousarray(wo.reshape(depth, KD, 128, D).transpose(0, 2, 1, 3)).reshape(depth, 128, KD * D)
    pp = np.zeros((depth, 128, 114), f)
    cw = np.asarray(inputs["conv_w"], f)[:depth]
    pp[:, :, 0:40] = cw.reshape(depth, 5, 8, 128).transpose(0, 3, 2, 1).reshape(depth, 128, 40)
    pp[:, :, 40:48] = np.asarray(inputs["conv_b"], f)[:depth].reshape(depth, 8, 128).transpose(0, 2, 1)
    mu = np.asarray(inputs["shift_mu"], f)[:depth]
    pp[:, :, 48:63] = mu[:, 0].reshape(depth, 15, 128).transpose(0, 2, 1)
    pp[:, :, 63:78] = mu[:, 1].reshape(depth, 15, 128).transpose(0, 2, 1)
    pp[:, :, 78:86] = np.asarray(inputs["w0"], f)[:depth].reshape(depth, 8, 128).transpose(0, 2, 1)
    pp[:, :, 86:94] = np.asarray(inputs["a0"], f)[:depth].reshape(depth, 8, 128).transpose(0, 2, 1)
    for off, nm in ((94, "k_k"), (98, "k_a"), (102, "r_k"), (106, "gn_w"), (110, "gn_b")):
        pp[:, :, off:off + 4] = np.asarray(inputs[nm], f)[:depth].reshape(depth, 4, 128).transpose(0, 2, 1)
    sh["pp"] = pp
    bb = np.zeros((depth, 1056), f)
    bb[:, 0:16] = np.asarray(inputs["dt_bias"], f)[:depth].reshape(depth, 16)
    bb[:, 16:32] = np.asarray(inputs["A_log"], f)[:depth].reshape(depth, 16)
    bb[:, 32:544] = np.repeat(np.asarray(inputs["d_skip"], f)[:depth], 64, axis=1)
    bb[:, 544:1056] = np.asarray(inputs["ssm_norm_w"], f)[:depth]
    sh["bb"] = bb
    sh["wup"] = np.ascontiguousarray(np.asarray(inputs["w_up"], f)[:depth].reshape(depth, 128, 512))
    sh["aup"] = np.ascontiguousarray(np.asarray(inputs["a_up"], f)[:depth].reshape(depth, 128, 512))
    sh["gup"] = np.ascontiguousarray(np.asarray(inputs["g_up"], f)[:depth])
    return sh


def _core_tokens(x2seq, c2, flagval):
    f = np.float32
    NT = x2seq.shape[0]
    m = {}
    m["xT"] = np.ascontiguousarray(x2seq.T.reshape(KD, 128, NT)).astype(f)
    m["cT"] = np.ascontiguousarray(c2.reshape(2, KD, 128).transpose(2, 1, 0)).astype(f)
    m["flag"] = np.full((128, 1), flagval, f)
    return m


_NC_CACHE = {}


def kernel(**inputs):
    depth = 4
    NT = 4096
    key = (NT, depth)
    if key not in _NC_CACHE:
        _NC_CACHE[key] = build(NT, depth)
    nc = _NC_CACHE[key]
    shared = _prep_core_inputs(inputs, depth)
    xp = np.asarray(inputs["x_prompt"], np.float32)
    xs = np.asarray(inputs["x_sample"], np.float32)
    cp = np.asarray(inputs["c_prompt"], np.float32)
    cs = np.asarray(inputs["c_sample"], np.float32)
    in_maps = []
    for core in range(8):
        if core < 4:
            x2 = np.concatenate([xp[2 * core], xp[2 * core + 1]], axis=0)
            c2 = np.stack([cp[2 * core], cp[2 * core + 1]])
            m = _core_tokens(x2, c2, 0.0)
        else:
            j = core - 4
            m = _core_tokens(xs[j], np.stack([cs[j], cs[j]]), 1.0)
        m.update(shared)
        in_maps.append(m)
    res = run_bass_kernel_spmd(nc, in_maps, core_ids=list(range(8)))
    yp = np.zeros_like(xp)
    ys = np.zeros_like(xs)
    for core in range(8):
        y = res.results[core]["yT"].reshape(D, NT).T
        if core < 4:
            yp[2 * core] = y[:2048]
            yp[2 * core + 1] = y[2048:]
        else:
            ys[core - 4] = y
    return (yp, ys)
```

```python
import contextlib
import numpy as np
import concourse.bass as bass
import concourse.mybir as mybir
from concourse.bass_utils import run_bass_kernel_spmd

F32 = mybir.dt.float32
BF16 = mybir.dt.bfloat16
AF = mybir.ActivationFunctionType
ALU = mybir.AluOpType

D = 1024
KD = D // 128
DFF = 4096
NMOD = 6
NORM_EPS = 1e-6


class Tk:
    __slots__ = ("w", "r", "dsem", "dcnt", "dkey", "name", "psum")

    def __init__(self, name=""):
        self.w = None
        self.r = {}
        self.dsem = None
        self.dcnt = 0
        self.name = name
        self.psum = False


class Ctx:
    def __init__(self, nc, es):
        self.nc = nc
        self.es = es
        self.E = {"pe": nc.tensor, "act": nc.scalar, "dve": nc.vector, "pool": nc.gpsimd, "sp": nc.sync}
        self.sem = {}
        self.cnt = {}
        for k in self.E:
            self.sem[k] = es.enter_context(nc.semaphore("c_" + k))
            self.cnt[k] = 0
        self.seen = {k: {} for k in self.E}
        self.nsem = 5
        self.ninst = 0
        self.mem_es = es
        self.sem_free = []
        self.owners = []
        self.gsems = []

    def _wait(self, eng, evs):
        need = {}
        for ev in evs:
            if ev is None:
                continue
            key, h, v = ev
            if eng == "pe" and key == "pe":
                continue
            if v > need.get(key, (None, 0))[1]:
                need[key] = (h, v)
        for key, (h, v) in need.items():
            if self.seen[eng].get(key, 0) >= v:
                continue
            self.E[eng].wait_ge(h, v)
            self.seen[eng][key] = v

    def _deps(self, reads, writes):
        evs = []
        for t in reads:
            evs.append(t.w)
            if t.psum:
                evs.extend(t.r.values())
        for t in writes:
            evs.append(t.w)
            evs.extend(t.r.values())
        return evs

    def _record(self, ev, reads, writes):
        key = ev[0]
        for t in reads:
            old = t.r.get(key)
            if old is None or old[2] < ev[2]:
                t.r[key] = ev
        for t in writes:
            t.w = ev
            t.r = {}

    def op(self, eng, fn, reads=(), writes=()):
        self._wait(eng, self._deps(reads, writes))
        ins = fn(self.E[eng])
        self.cnt[eng] += 1
        ins.then_inc(self.sem[eng], 1)
        ev = (eng, self.sem[eng], self.cnt[eng])
        self._record(ev, reads, writes)
        self.ninst += 1
        return ev

    def dma(self, q, pairs, reads, writes, owner, **kw):
        if q == "pool":
            assert len(pairs) == 1
            sem = self.es.enter_context(self.nc.semaphore("g%d" % self.nsem))
            key = "g%d" % self.nsem
            self.nsem += 1
            self._wait(q, self._deps(reads, writes))
            o, i = pairs[0]
            self.E[q].dma_start(out=o, in_=i, **kw).then_inc(sem, 16)
            self.ninst += 1
            ev = (key, sem, 16)
            self._record(ev, reads, writes)
            self.gsems.append(ev)
            return ev
        if owner.dsem is None:
            if self.sem_free:
                owner.dsem, owner.dcnt, owner.dkey = self.sem_free.pop()
            else:
                owner.dsem = self.es.enter_context(self.nc.semaphore("d%d" % self.nsem))
                owner.dkey = "d%d" % self.nsem
                self.nsem += 1
            self.owners.append(owner)
        key = owner.dkey
        evs = self._deps(reads, writes)
        if owner.dcnt:
            evs.append((key, owner.dsem, owner.dcnt))
        self._wait(q, evs)
        for (o, i) in pairs:
            self.E[q].dma_start(out=o, in_=i, **kw).then_inc(owner.dsem, 16)
            owner.dcnt += 16
            self.ninst += 1
        ev = (key, owner.dsem, owner.dcnt)
        self._record(ev, reads, writes)
        return ev

    def barrier(self):
        for eng in self.E:
            evs = [(k, self.sem[k], self.cnt[k]) for k in self.E if self.cnt[k] > 0]
            evs += [(o.dkey, o.dsem, o.dcnt) for o in self.owners if o.dcnt > 0]
            evs += self.gsems
            self._wait(eng, evs)

    def end_phase(self, first_owner):
        self.barrier()
        rel = self.owners[first_owner:]
        self.owners = self.owners[:first_owner]
        for o in rel:
            self.sem_free.append((o.dsem, o.dcnt, o.dkey))
            o.dsem = None

    def wait_all(self, eng, tks):
        evs = []
        for t in tks:
            evs.append(t.w)
            evs.extend(t.r.values())
        self._wait(eng, evs)


class Buf:
    _n = [0]

    def __init__(self, cx, name, shape, dtype, psum=False):
        Buf._n[0] += 1
        name = "%s_%d" % (name, Buf._n[0])
        if psum:
            self.t = cx.mem_es.enter_context(cx.nc.psum_tensor(name, shape, dtype))
        else:
            self.t = cx.mem_es.enter_context(cx.nc.sbuf_tensor(name, shape, dtype))
        self.k = Tk(name)
        self.k.psum = psum

    def __getitem__(self, idx):
        return self.t[idx]


def build(NT=4096, depth=4, do_mix=True, do_ffn=True, debug=False):
    SEG = NT // 2
    nc = bass.Bass("TRN2", target_bir_lowering=False)
    dram_in = {}

    def din(name, shape, dt=F32):
        dram_in[name] = nc.dram_tensor(name, list(shape), dt, kind="ExternalInput").ap()
        return dram_in[name]

    xT = din("xT", [KD, 128, NT])
    cT = din("cT", [128, KD, 2])
    flag = din("flag", [128, 1])
    w_mod = din("w_mod", [depth, 128, KD, NMOD * D])
    b_mod = din("b_mod", [depth, 128, NMOD * KD])
    norm_g = din("norm_g", [depth, 128, 4, KD])
    w_ff1 = din("w_ff1", [depth, 128, KD * DFF])
    w_ff2 = din("w_ff2", [depth, 128, (DFF // 128) * D])
    c_ones = din("c_ones", [128, 128])
    c_ident = din("c_ident", [128, 128])
    c_masks = din("c_masks", [128, 4, 128])
    c_blk1 = din("c_blk1", [128, 128])
    c_bmask = din("c_bmask", [128, 4, 128])
    c_rmask = din("c_rmask", [128, 512])
    NPP, NBB, WIN = 114, 1056, 3584
    w_in = din("w_in", [depth, 128, KD * WIN])
    w_out = din("w_out", [depth, 128, KD * D])
    pp_in = din("pp", [depth, 128, NPP])
    bb_in = din("bb", [depth, NBB])
    wup_in = din("wup", [depth, 128, 512])
    aup_in = din("aup", [depth, 128, 512])
    gup_in = din("gup", [depth, 128, 512])
    NFC = 23
    NC = NT // 128
    uT = nc.dram_tensor("uT", [NFC, 128, NT], F32, kind="Internal").ap()
    z_tok = nc.dram_tensor("z_tok", [NT, 512], F32, kind="Internal").ap()
    dt_tok = nc.dram_tensor("dt_tok", [NT, 16], F32, kind="Internal").ap()
    ysacc = nc.dram_tensor("ysacc", [NT, 512], F32, kind="Internal").ap()
    yracc = nc.dram_tensor("yracc", [4, 128, NT], F32, kind="Internal").ap()
    ymix_d = nc.dram_tensor("ymix_d", [KD, 128, NT], BF16, kind="Internal").ap()
    dbg_out = nc.dram_tensor("dbg_out", [KD, 128, NT], F32, kind="ExternalOutput").ap() if debug else None
    yT = nc.dram_tensor("yT", [KD, 128, NT], F32, kind="ExternalOutput").ap()
    xres = nc.dram_tensor("xres", [KD, 128, NT], F32, kind="Internal").ap()

    with contextlib.ExitStack() as es:
        cx = Ctx(nc, es)
        ones_f = Buf(cx, "ones_f", [128, 128], F32)
        ones_b = Buf(cx, "ones_b", [128, 128], BF16)
        eps_t = Buf(cx, "eps_t", [128, 1], F32)
        sc_t = Buf(cx, "sc_t", [128, KD, 2], F32)
        flag_t = Buf(cx, "flag_t", [128, 1], F32)
        ident_f = Buf(cx, "ident_f", [128, 128], F32)
        ident_b = Buf(cx, "ident_b", [128, 128], BF16)
        mk_f = Buf(cx, "mk_f", [128, 4, 128], F32)
        one_t = Buf(cx, "one_t", [128, 1], F32)
        cx.dma("sp", [(ones_f[:], c_ones), (sc_t[:], cT), (flag_t[:], flag), (ident_f[:], c_ident), (mk_f[:], c_masks)], [],
               [ones_f.k, sc_t.k, flag_t.k, ident_f.k, mk_f.k], ones_f.k)
        cx.op("dve", lambda e: e.tensor_copy(out=ident_b[:], in_=ident_f[:]), [ident_f.k], [ident_b.k])
        blk1_f = Buf(cx, "blk1_f", [128, 128], F32)
        blk1_b = Buf(cx, "blk1_b", [128, 128], BF16)
        blk64_b = Buf(cx, "blk64_b", [128, 128], BF16)
        rmask = Buf(cx, "rmask", [128, 512], F32)
        gneps_t = Buf(cx, "gneps_t", [128, 1], F32)
        bm_f = Buf(cx, "bm_f", [128, 4, 128], F32)
        bm_b = Buf(cx, "bm_b", [128, 4, 128], BF16)
        cx.dma("sp", [(blk1_f[:], c_blk1), (rmask[:], c_rmask), (bm_f[:], c_bmask)], [], [blk1_f.k, rmask.k, bm_f.k], blk1_f.k)
        cx.op("dve", lambda e: e.tensor_copy(out=bm_b[:], in_=bm_f[:]), [bm_f.k], [bm_b.k])
        cx.op("dve", lambda e: e.tensor_copy(out=blk1_b[:], in_=blk1_f[:]), [blk1_f.k], [blk1_b.k])
        cx.op("dve", lambda e: e.tensor_scalar(out=blk64_b[:], in0=blk1_f[:], scalar1=1.0 / 64, scalar2=1.0, op0=ALU.mult, op1=ALU.mult), [blk1_f.k], [blk64_b.k])
        cx.op("dve", lambda e: e.memset(gneps_t[:], 64e-5), [], [gneps_t.k])
        cx.op("dve", lambda e: e.memset(one_t[:], 1.0), [], [one_t.k])
        cx.op("dve", lambda e: e.tensor_copy(out=ones_b[:], in_=ones_f[:]), [ones_f.k], [ones_b.k])
        cx.op("dve", lambda e: e.memset(eps_t[:], NORM_EPS), [], [eps_t.k])
        cx.op("act", lambda e: e.activation(out=sc_t[:], in_=sc_t[:], func=AF.Silu), [sc_t.k], [sc_t.k])

        import os
        DBG = os.environ.get("DBG_STOP", "")
        SKIP = os.environ.get("KSKIP", "").split(",")
        PS = [Buf(cx, "ps%d" % i, [128, 512], F32, psum=True) for i in range(7)]
        PT = Buf(cx, "pt", [128, 1024], BF16, psum=True)

        modT = Buf(cx, "modT", [128, NMOD * KD, 2], F32)
        bmod_t = Buf(cx, "bmod_t", [128, NMOD * KD], F32)
        ng_t = Buf(cx, "ng_t", [128, 4, KD], F32)
        A1 = Buf(cx, "A1", [128, KD, 2], F32)
        G1 = Buf(cx, "G1", [128, KD, 2], F32)
        A2 = Buf(cx, "A2", [128, KD, 2], F32)
        G2 = Buf(cx, "G2", [128, KD, 2], F32)

        def mod_phase(l):
            wm = [Buf(cx, "wm%d" % i, [128, KD, 512], F32) for i in range(2)]
            cx.dma("sp", [(bmod_t[:], b_mod[l]), (ng_t[:], norm_g[l])], [], [bmod_t.k, ng_t.k], bmod_t.k)
            ps = PS[6]
            if DBG == "const":
                return
            for j in range(NMOD * D // 512):
                wb = wm[j % 2]
                cx.dma("sp", [(wb[:], w_mod[l][:, :, j * 512:(j + 1) * 512])], [], [wb.k], wb.k)
                for cc in range(4):
                    col = j * 4 + cc
                    for k in range(KD):
                        cx.op("pe", lambda e, k=k, cc=cc, col=col, wb=wb: e.matmul(
                            ps[:, col * 2:col * 2 + 2], lhsT=wb[:, k, cc * 128:(cc + 1) * 128], rhs=sc_t[:, k, :],
                            start=(k == 0), stop=(k == KD - 1)), [wb.k, sc_t.k], [ps.k])
            if DBG == "modmm":
                return
            cx.op("dve", lambda e: e.tensor_tensor(
                out=modT[:], in0=ps[:, 0:NMOD * KD * 2].rearrange("p (c s) -> p c s", s=2),
                in1=bmod_t[:].unsqueeze(2).to_broadcast([128, NMOD * KD, 2]), op=ALU.add),
                [ps.k, bmod_t.k], [modT.k])

            def mk(dst, mi, gi, plus1):
                gb = ng_t[:, gi, :].unsqueeze(2).to_broadcast([128, KD, 2])
                src = modT[:, mi * KD:(mi + 1) * KD, :]
                if plus1:
                    cx.op("dve", lambda e: e.tensor_scalar(out=dst[:], in0=src, scalar1=1.0, scalar2=None, op0=ALU.add),
                          [modT.k], [dst.k])
                    src = dst[:]
                cx.op("dve", lambda e: e.tensor_tensor(out=dst[:], in0=src, in1=gb, op=ALU.mult), [modT.k, ng_t.k, dst.k], [dst.k])
            if DBG == "modtt":
                return
            mk(A1, 1, 0, True)
            mk(G1, 2, 1, False)
            mk(A2, 4, 2, True)
            mk(G2, 5, 3, False)

        TF = 256
        NKF = DFF // 128

        def ffn_phase(l, src, dst):
            w1b = Buf(cx, "w1b", [128, KD, DFF], BF16)
            w2b = Buf(cx, "w2b", [128, NKF, D], BF16)
            xt = [Buf(cx, "f_xt%d" % i, [128, KD, TF], F32) for i in range(2)]
            sq = Buf(cx, "f_sq", [128, KD, TF], BF16)
            std = Buf(cx, "f_std", [128, TF], F32)
            rstd = Buf(cx, "f_rstd", [128, TF], F32)
            tmp = Buf(cx, "f_tmp", [128, KD, TF], F32)
            hT = Buf(cx, "f_hT", [128, KD, TF], BF16)
            rl = [Buf(cx, "f_rl%d" % i, [128, TF], F32) for i in range(4)]
            aT = Buf(cx, "f_aT", [128, NKF, TF], BF16)
            fT = Buf(cx, "f_fT", [128, KD, TF], F32)
            w1v = w_ff1[l].rearrange("p (a b) -> p a b", b=2048)
            w2v = w_ff2[l].rearrange("p (a b) -> p a b", b=2048)
            cx.dma("pool", [(w1b[:].rearrange("p k c -> p (k c)").rearrange("p (a b) -> p a b", b=2048), w1v)],
                   [], [w1b.k], w1b.k)
            cx.dma("pool", [(w2b[:].rearrange("p k c -> p (k c)").rearrange("p (a b) -> p a b", b=2048), w2v)],
                   [], [w2b.k], w2b.k)
            dk = [Tk("ffn_dst%d" % i) for i in range(NT // TF)]
            if DBG == "ffn_w":
                return dk
            for ti in range(NT // TF):
                t0 = ti * TF
                seg = t0 // SEG
                x = xt[ti % 2]
                cx.dma("sp", [(x[:], src[:, :, t0:t0 + TF].rearrange("k p t -> p k t"))], [], [x.k], x.k)
                cx.op("act", lambda e: e.activation(out=sq[:], in_=x[:], func=AF.Square), [x.k], [sq.k])
                ps = PS[0]
                for k in range(KD):
                    cx.op("pe", lambda e, k=k: e.matmul(ps[:, 0:TF], lhsT=ones_b[:], rhs=sq[:, k, :],
                                                        start=(k == 0), stop=(k == KD - 1)), [ones_b.k, sq.k], [ps.k])
                cx.op("act", lambda e: e.activation(out=std[:], in_=ps[:, 0:TF], func=AF.Ln, bias=eps_t[:], scale=1.0 / D),
                      [ps.k, eps_t.k], [std.k])
                cx.op("act", lambda e: e.activation(out=rstd[:], in_=std[:], func=AF.Exp, scale=-0.5), [std.k], [rstd.k])
                if DBG == "ffn_rstd":
                    return dk
                cx.op("dve", lambda e: e.tensor_tensor(out=tmp[:], in0=x[:], in1=rstd[:].unsqueeze(1).to_broadcast([128, KD, TF]),
                                                       op=ALU.mult), [x.k, rstd.k], [tmp.k])
                if DBG == "ffn_tmp":
                    return dk
                for k in range(KD):
                    eng = "act" if k % 2 == 0 else "pool"
                    if eng == "act":
                        cx.op("act", lambda e, k=k: e.activation(out=hT[:, k, :], in_=tmp[:, k, :], func=AF.Identity,
                                                                 bias=modT[:, 3 * KD + k, seg:seg + 1], scale=A2[:, k, seg:seg + 1]),
                              [tmp.k, modT.k, A2.k], [hT.k])
                    else:
                        cx.op("pool", lambda e, k=k: e.tensor_scalar(out=hT[:, k, :], in0=tmp[:, k, :],
                                                                     scalar1=A2[:, k, seg:seg + 1], scalar2=modT[:, 3 * KD + k, seg:seg + 1],
                                                                     op0=ALU.mult, op1=ALU.add), [tmp.k, modT.k, A2.k], [hT.k])
                if DBG == "ffn_h":
                    return dk
                for c in range(NKF):
                    ps = PS[1 + c % 4]
                    r = rl[c % 4]
                    for k in range(KD):
                        cx.op("pe", lambda e, k=k, c=c, ps=ps: e.matmul(ps[:, 0:TF], lhsT=w1b[:, k, c * 128:(c + 1) * 128], rhs=hT[:, k, :],
                                                                       start=(k == 0), stop=(k == KD - 1)), [w1b.k, hT.k], [ps.k])
                    cx.op("act", lambda e, ps=ps, r=r: e.activation(out=r[:], in_=ps[:, 0:TF], func=AF.Relu), [ps.k], [r.k])
                    if c % 2 == 0:
                        cx.op("dve", lambda e, ps=ps, r=r, c=c: e.tensor_tensor(out=aT[:, c, :], in0=r[:], in1=ps[:, 0:TF], op=ALU.mult),
                              [r.k, ps.k], [aT.k])
                    else:
                        cx.op("pool", lambda e, r=r, c=c: e.tensor_tensor(out=aT[:, c, :], in0=r[:], in1=r[:], op=ALU.mult),
                              [r.k], [aT.k])
                if DBG == "ffn_1":
                    return dk
                for o in range(KD):
                    ps = PS[5 + o % 2]
                    for k in range(NKF):
                        cx.op("pe", lambda e, k=k, o=o, ps=ps: e.matmul(ps[:, 0:TF], lhsT=w2b[:, k, o * 128:(o + 1) * 128], rhs=aT[:, k, :],
                                                                       start=(k == 0), stop=(k == NKF - 1)), [w2b.k, aT.k], [ps.k])
                    if DBG != "ffn_2a":
                        cx.op("act", lambda e, ps=ps, o=o: e.activation(out=sq[:, o, :], in_=ps[:, 0:TF], func=AF.Square), [ps.k], [sq.k])
                    if DBG != "ffn_2b":
                        cx.op("dve", lambda e, ps=ps, o=o: e.tensor_copy(out=fT[:, o, :], in_=ps[:, 0:TF]), [ps.k], [fT.k])
                if DBG in ("ffn_2", "ffn_2a", "ffn_2b"):
                    return dk
                ps = PS[0]
                for k in range(KD):
                    cx.op("pe", lambda e, k=k: e.matmul(ps[:, 0:TF], lhsT=ones_b[:], rhs=sq[:, k, :],
                                                        start=(k == 0), stop=(k == KD - 1)), [ones_b.k, sq.k], [ps.k])
                cx.op("act", lambda e: e.activation(out=std[:], in_=ps[:, 0:TF], func=AF.Ln, bias=eps_t[:], scale=1.0 / D),
                      [ps.k, eps_t.k], [std.k])
                cx.op("act", lambda e: e.activation(out=rstd[:], in_=std[:], func=AF.Exp, scale=-0.5), [std.k], [rstd.k])
                cx.op("dve", lambda e: e.tensor_tensor(out=fT[:], in0=fT[:], in1=rstd[:].unsqueeze(1).to_broadcast([128, KD, TF]),
                                                       op=ALU.mult), [fT.k, rstd.k], [fT.k])
                for k in range(KD):
                    if k % 2 == 0:
                        cx.op("act", lambda e, k=k: e.activation(out=fT[:, k, :], in_=fT[:, k, :], func=AF.Identity,
                                                                 scale=G2[:, k, seg:seg + 1]), [fT.k, G2.k], [fT.k])
                    else:
                        cx.op("pool", lambda e, k=k: e.tensor_scalar(out=fT[:, k, :], in0=fT[:, k, :], scalar1=G2[:, k, seg:seg + 1],
                                                                     scalar2=1.0, op0=ALU.mult, op1=ALU.mult), [fT.k, G2.k], [fT.k])
                cx.op("dve", lambda e: e.tensor_tensor(out=x[:], in0=x[:], in1=fT[:], op=ALU.add), [x.k, fT.k], [x.k])
                if DBG == "ffn_tail":
                    return dk
                cx.dma("sp", [(dst[:, :, t0:t0 + TF].rearrange("k p t -> p k t"), x[:])], [x.k], [dk[ti]], x.k)
            return dk

        TM = 512
        V4 = lambda ap, a: ap.rearrange("p (a b) -> p a b", a=a)

        def norm_tile(x, seg, TT, sq, std, rstd, tmp, hT, Amul, boff):
            cx.op("act", lambda e: e.activation(out=sq[:], in_=x[:], func=AF.Square), [x.k], [sq.k])
            ps = PS[0]
            for k in range(KD):
                cx.op("pe", lambda e, k=k: e.matmul(ps[:, 0:TT], lhsT=ones_b[:], rhs=sq[:, k, :],
                                                    start=(k == 0), stop=(k == KD - 1)), [ones_b.k, sq.k], [ps.k])
            cx.op("act", lambda e: e.activation(out=std[:], in_=ps[:, 0:TT], func=AF.Ln, bias=eps_t[:], scale=1.0 / D),
                  [ps.k, eps_t.k], [std.k])
            cx.op("act", lambda e: e.activation(out=rstd[:], in_=std[:], func=AF.Exp, scale=-0.5), [std.k], [rstd.k])
            cx.op("dve", lambda e: e.tensor_tensor(out=tmp[:], in0=x[:], in1=rstd[:].unsqueeze(1).to_broadcast([128, KD, TT]),
                                                   op=ALU.mult), [x.k, rstd.k], [tmp.k])
            for k in range(KD):
                if k % 2 == 0:
                    cx.op("act", lambda e, k=k: e.activation(out=hT[:, k, :], in_=tmp[:, k, :], func=AF.Identity,
                                                             bias=modT[:, boff * KD + k, seg:seg + 1], scale=Amul[:, k, seg:seg + 1]),
                          [tmp.k, modT.k, Amul.k], [hT.k])
                else:
                    cx.op("pool", lambda e, k=k: e.tensor_scalar(out=hT[:, k, :], in0=tmp[:, k, :],
                                                                 scalar1=Amul[:, k, seg:seg + 1], scalar2=modT[:, boff * KD + k, seg:seg + 1],
                                                                 op0=ALU.mult, op1=ALU.add), [tmp.k, modT.k, Amul.k], [hT.k])

        def post_tile(x, seg, TT, psrc_fn, nk, sq, std, rstd, fT, Gmul):
            for o in range(KD):
                ps = psrc_fn(o)
                cx.op("act", lambda e, ps=ps, o=o: e.activation(out=sq[:, o, :], in_=ps[:, 0:TT], func=AF.Square), [ps.k], [sq.k])
                cx.op("dve", lambda e, ps=ps, o=o: e.tensor_copy(out=fT[:, o, :], in_=ps[:, 0:TT]), [ps.k], [fT.k])
            ps = PS[0]
            for k in range(KD):
                cx.op("pe", lambda e, k=k: e.matmul(ps[:, 0:TT], lhsT=ones_b[:], rhs=sq[:, k, :],
                                                    start=(k == 0), stop=(k == KD - 1)), [ones_b.k, sq.k], [ps.k])
            cx.op("act", lambda e: e.activation(out=std[:], in_=ps[:, 0:TT], func=AF.Ln, bias=eps_t[:], scale=1.0 / D),
                  [ps.k, eps_t.k], [std.k])
            cx.op("act", lambda e: e.activation(out=rstd[:], in_=std[:], func=AF.Exp, scale=-0.5), [std.k], [rstd.k])
            cx.op("dve", lambda e: e.tensor_tensor(out=fT[:], in0=fT[:], in1=rstd[:].unsqueeze(1).to_broadcast([128, KD, TT]),
                                                   op=ALU.mult), [fT.k, rstd.k], [fT.k])
            for k in range(KD):
                if k % 2 == 0:
                    cx.op("act", lambda e, k=k: e.activation(out=fT[:, k, :], in_=fT[:, k, :], func=AF.Identity,
                                                             scale=Gmul[:, k, seg:seg + 1]), [fT.k, Gmul.k], [fT.k])
                else:
                    cx.op("pool", lambda e, k=k: e.tensor_scalar(out=fT[:, k, :], in0=fT[:, k, :], scalar1=Gmul[:, k, seg:seg + 1],
                                                                 scalar2=1.0, op0=ALU.mult, op1=ALU.mult), [fT.k, Gmul.k], [fT.k])
            cx.op("dve", lambda e: e.tensor_tensor(out=x[:], in0=x[:], in1=fT[:], op=ALU.add), [x.k, fT.k], [x.k])

        class MixState:
            pass

        def mix_load_params(l, ms):
            ms.pp = Buf(cx, "pp", [128, NPP], F32)
            ms.bb = Buf(cx, "bb", [128, NBB], F32)
            cx.dma("sp", [(ms.pp[:], pp_in[l]), (ms.bb[:], bb_in[l].partition_broadcast(128))], [], [ms.pp.k, ms.bb.k], ms.pp.k)
            ms.aneg = Buf(cx, "aneg", [128, 16], F32)
            cx.op("act", lambda e: e.activation(out=ms.aneg[:], in_=ms.bb[:, 16:32], func=AF.Exp), [ms.bb.k], [ms.aneg.k])
            cx.op("dve", lambda e: e.tensor_scalar(out=ms.aneg[:], in0=ms.aneg[:], scalar1=-1.0, scalar2=1.0, op0=ALU.mult, op1=ALU.mult),
                  [ms.aneg.k], [ms.aneg.k])
            ms.ymk = [[Tk("ymk%d_%d" % (h, c)) for c in range(NC)] for h in range(2)]
            ms.uk = [[Tk("uT%d_%d" % (fc, ti)) for ti in range(NT // TM)] for fc in range(NFC)]
            ms.zk = [Tk("z%d" % c) for c in range(NC)]
            ms.dtk = [Tk("dt%d" % c) for c in range(NC)]

        def m1_inproj(l, src, ms):
            winb = Buf(cx, "winb", [128, KD, WIN], BF16)
            cx.dma("pool", [(winb[:].rearrange("p k c -> p (k c)").rearrange("p (a b) -> p a b", b=1792),
                             w_in[l].rearrange("p (a b) -> p a b", b=1792))], [], [winb.k], winb.k)
            xt = [Buf(cx, "m1_xt%d" % i, [128, KD, TM], F32) for i in range(2)]
            sq = Buf(cx, "m1_sq", [128, KD, TM], BF16)
            std = Buf(cx, "m1_std", [128, TM], F32)
            rstd = Buf(cx, "m1_rstd", [128, TM], F32)
            tmp = Buf(cx, "m1_tmp", [128, KD, TM], F32)
            hT = Buf(cx, "m1_hT", [128, KD, TM], BF16)
            stg = [Buf(cx, "m1_stg%d" % i, [128, TM], F32) for i in range(4)]
            stz = [Buf(cx, "m1_stz%d" % i, [128, 512 + 16], F32) for i in range(2)]
            ns = 0
            for ti in range(NT // TM):
                t0 = ti * TM
                seg = t0 // SEG
                x = xt[ti % 2]
                cx.dma("sp", [(x[:], src[:, :, t0:t0 + TM].rearrange("k p t -> p k t"))], [], [x.k], x.k)
                norm_tile(x, seg, TM, sq, std, rstd, tmp, hT, A1, 0)
                for fc in range(NFC):
                    ps = PS[1 + fc % 4]
                    for k in range(KD):
                        cx.op("pe", lambda e, k=k, fc=fc, ps=ps: e.matmul(ps[:, 0:TM], lhsT=winb[:, k, fc * 128:(fc + 1) * 128], rhs=hT[:, k, :],
                                                                         start=(k == 0), stop=(k == KD - 1)), [winb.k, hT.k], [ps.k])
                    st = stg[ns % 4]
                    ns += 1
                    if fc % 2 == 0:
                        cx.op("act", lambda e, ps=ps, st=st: e.activation(out=st[:], in_=ps[:, 0:TM], func=AF.Identity), [ps.k], [st.k])
                    else:
                        cx.op("dve", lambda e, ps=ps, st=st: e.tensor_copy(out=st[:], in_=ps[:, 0:TM]), [ps.k], [st.k])
                    cx.dma("sp", [(uT[fc, :, t0:t0 + TM], st[:])], [st.k], [ms.uk[fc][ti]], st.k)
                for sub in range(TM // 128):
                    c = (t0 // 128) + sub
                    psz = PS[5]
                    psd = PS[6]
                    for k in range(KD):
                        cx.op("pe", lambda e, k=k: e.matmul(psz[:, 0:512], lhsT=hT[:, k, sub * 128:(sub + 1) * 128], rhs=winb[:, k, 2944:3456],
                                                            start=(k == 0), stop=(k == KD - 1)), [winb.k, hT.k], [psz.k])
                    for k in range(KD):
                        cx.op("pe", lambda e, k=k: e.matmul(psd[:, 0:16], lhsT=hT[:, k, sub * 128:(sub + 1) * 128], rhs=winb[:, k, 3456:3472],
                                                            start=(k == 0), stop=(k == KD - 1)), [winb.k, hT.k], [psd.k])
                    sz = stz[c % 2]
                    cx.op("act", lambda e: e.activation(out=sz[:, 0:512], in_=psz[:, 0:512], func=AF.Identity), [psz.k], [sz.k])
                    cx.op("dve", lambda e: e.tensor_copy(out=sz[:, 512:528], in_=psd[:, 0:16]), [psd.k, sz.k], [sz.k])
                    cx.dma("sp", [(z_tok[c * 128:(c + 1) * 128, :], sz[:, 0:512]), (dt_tok[c * 128:(c + 1) * 128, :], sz[:, 512:528])],
                           [sz.k], [ms.zk[c], ms.dtk[c]], sz.k)

        def load_win(win, fc, ti, halo, ms):
            t0 = ti * TM
            lo, hi = max(t0 - halo, 0), min(t0 + TM + halo, NT)
            rk = [ms.uk[fc][ti]]
            if ti > 0:
                rk.append(ms.uk[fc][ti - 1])
            if ti < NT // TM - 1:
                rk.append(ms.uk[fc][ti + 1])
            cx.dma("sp", [(win[:, lo - (t0 - halo):hi - (t0 - halo)], uT[fc, :, lo:hi])], rk, [win.k], win.k)
            W = TM + 2 * halo
            if t0 == 0:
                cx.op("pool", lambda e: e.memset(win[:, 0:halo], 0.0), [], [win.k])
            if t0 + TM == NT:
                cx.op("pool", lambda e: e.memset(win[:, W - halo:W], 0.0), [], [win.k])
            if t0 == SEG:
                cx.op("pool", lambda e: e.tensor_scalar(out=win[:, 0:halo], in0=win[:, 0:halo], scalar1=flag_t[:, 0:1], scalar2=1.0,
                                                        op0=ALU.mult, op1=ALU.mult), [win.k, flag_t.k], [win.k])
            if t0 + TM == SEG:
                cx.op("pool", lambda e: e.tensor_scalar(out=win[:, W - halo:W], in0=win[:, W - halo:W], scalar1=flag_t[:, 0:1], scalar2=1.0,
                                                        op0=ALU.mult, op1=ALU.mult), [win.k, flag_t.k], [win.k])

        def m2_ssdprep(l, ms):
            ms.xs_tok = Buf(cx, "xs_tok", [128, NC, 512], BF16)
            ms.B_tok = Buf(cx, "B_tok", [128, NC, 256], BF16)
            ms.BT = Buf(cx, "BT", [128, 2, NT], BF16)
            ms.CT = Buf(cx, "CT", [128, 2, NT], BF16)
            ms.sk = [Tk("ssd%d" % c) for c in range(NC)]
            win = [Buf(cx, "m2_win%d" % i, [128, TM + 4], F32) for i in range(2)]
            acc = [Buf(cx, "m2_acc%d" % i, [128, TM], F32) for i in range(2)]
            cvo = [Buf(cx, "m2_cvo%d" % i, [128, TM], BF16) for i in range(2)]
            n = 0
            for ti in range(NT // TM):
                t0 = ti * TM
                cks = [ms.sk[t0 // 128 + j] for j in range(TM // 128)]
                for j8 in range(8):
                    fc = 15 + j8
                    w = win[n % 2]
                    a = acc[n % 2]
                    o = cvo[n % 2]
                    n += 1
                    load_win(w, fc, ti, 2, ms)
                    cx.op("pool", lambda e: e.tensor_scalar(out=a[:], in0=w[:, 0:TM], scalar1=ms.pp[:, j8 * 5:j8 * 5 + 1], scalar2=1.0,
                                                            op0=ALU.mult, op1=ALU.mult), [w.k, ms.pp.k], [a.k])
                    for j in range(1, 5):
                        cx.op("dve", lambda e, j=j: e.scalar_tensor_tensor(out=a[:], in0=w[:, j:j + TM], scalar=ms.pp[:, j8 * 5 + j:j8 * 5 + j + 1],
                                                                           in1=a[:], op0=ALU.mult, op1=ALU.add), [w.k, ms.pp.k, a.k], [a.k])
                    if j8 < 4 or j8 < 6:
                        cx.op("act", lambda e: e.activation(out=o[:], in_=a[:], func=AF.Silu, bias=ms.pp[:, 40 + j8:41 + j8]), [a.k, ms.pp.k], [o.k])
                        for sub in range(TM // 128):
                            cx.op("pe", lambda e, sub=sub: e.transpose(out=PT[:, sub * 128:(sub + 1) * 128], in_=o[:, sub * 128:(sub + 1) * 128],
                                                                      identity=ident_b[:]), [o.k, ident_b.k], [PT.k])
                        c0 = t0 // 128
                        if j8 < 4:
                            dstv = ms.xs_tok[:, c0:c0 + 4, j8 * 128:(j8 + 1) * 128]
                        else:
                            dstv = ms.B_tok[:, c0:c0 + 4, (j8 - 4) * 128:(j8 - 3) * 128]
                        cx.op("dve", lambda e: e.tensor_copy(out=dstv, in_=V4(PT[:, 0:512], 4)), [PT.k], cks)
                        if j8 >= 4:
                            cx.op("pool", lambda e: e.tensor_copy(out=ms.BT[:, j8 - 4, t0:t0 + TM], in_=o[:]), [o.k], cks)
                    else:
                        cx.op("act", lambda e: e.activation(out=ms.CT[:, j8 - 6, t0:t0 + TM], in_=a[:], func=AF.Silu, bias=ms.pp[:, 40 + j8:41 + j8]),
                              [a.k, ms.pp.k], cks)

        def m3_ssd(l, ms):
            dtt = [Buf(cx, "m3_dtt%d" % i, [128, 16], F32) for i in range(2)]
            zt = [Buf(cx, "m3_zt%d" % i, [128, 512], F32) for i in range(2)]
            yprev = [Buf(cx, "m3_yp%d" % i, [128, 512], F32) for i in range(2)]
            t1 = Buf(cx, "m3_t1", [128, 8], F32)
            dts = Buf(cx, "m3_dts", [128, 8], F32)
            a_t = Buf(cx, "m3_a", [128, 8], F32)
            lmh = Buf(cx, "m3_lmh", [128, 8, 128], F32)
            ex = Buf(cx, "m3_ex", [128, 24], F32)
            Lm = Buf(cx, "m3_Lm", [128, 8, 128], F32)
            CBm = Buf(cx, "m3_CBm", [128, 2, 128], F32)
            G = Buf(cx, "m3_G", [128, 8, 128], BF16)
            xdt = Buf(cx, "m3_xdt", [128, 8, 64], BF16)
            xw = Buf(cx, "m3_xw", [128, 8, 64], BF16)
            ytmp = Buf(cx, "m3_ytmp", [128, 512], F32)
            ycur = [Buf(cx, "m3_ycur%d" % i, [128, 512], F32) for i in range(2)]
            S = Buf(cx, "m3_S", [128, 512], F32)
            Sb = Buf(cx, "m3_Sb", [128, 512], BF16)
            y2 = Buf(cx, "m3_y2", [128, 512], F32)
            szt = Buf(cx, "m3_sz", [128, 512], F32)
            junk = Buf(cx, "m3_junk", [128, 512], BF16)
            ssum = Buf(cx, "m3_ssum", [128, 1], F32)
            ysn = Buf(cx, "m3_ysn", [128, 512], BF16)
            ysT = [Buf(cx, "m3_ysT%d" % i, [128, 4, 128], BF16) for i in range(2)]
            yk = [Tk("ysacc%d" % c) for c in range(NC)]
            psA0, psA1, psB, psC, psY, psO, psS = PS
            for d in range(2):
                m1i, m2i, vi = (0, 1, 1) if d == 0 else (2, 3, 3)
                cx.op("dve", lambda e: e.memset(S[:], 0.0), [], [S.k])
                cx.op("pool", lambda e: e.memset(Sb[:], 0.0), [], [Sb.k])
                order = range(NC) if d == 0 else range(NC - 1, -1, -1)
                for n, c in enumerate(order):
                    tok = slice(c * 128, (c + 1) * 128)
                    dtb = dtt[n % 2]
                    cx.dma("sp", [(dtb[:], dt_tok[tok, :])], [ms.dtk[c]], [dtb.k], dtb.k)
                    if (d == 0 and c == NC // 2) or (d == 1 and c == NC // 2 - 1):
                        cx.op("dve", lambda e: e.tensor_scalar(out=S[:], in0=S[:], scalar1=flag_t[:, 0:1], scalar2=1.0, op0=ALU.mult, op1=ALU.mult),
                              [S.k, flag_t.k], [S.k])
                        cx.op("act", lambda e: e.activation(out=Sb[:], in_=S[:], func=AF.Identity), [S.k], [Sb.k])
                    cx.op("dve", lambda e: e.tensor_tensor(out=t1[:], in0=dtb[:, d * 8:d * 8 + 8], in1=ms.bb[:, d * 8:d * 8 + 8], op=ALU.add),
                          [dtb.k, ms.bb.k], [t1.k])
                    cx.op("act", lambda e: e.activation(out=t1[:], in_=t1[:], func=AF.Exp), [t1.k], [t1.k])
                    cx.op("act", lambda e: e.activation(out=dts[:], in_=t1[:], func=AF.Ln, bias=one_t[:]), [t1.k, one_t.k], [dts.k])
                    cx.op("dve", lambda e: e.tensor_tensor(out=a_t[:], in0=dts[:], in1=ms.aneg[:, d * 8:d * 8 + 8], op=ALU.mult),
                          [dts.k, ms.aneg.k], [a_t.k])
                    for h in range(8):
                        eng = "dve" if h % 2 == 0 else "pool"
                        cx.op(eng, lambda e, h=h: e.tensor_scalar(out=lmh[:, h, :], in0=mk_f[:, m1i, :], scalar1=a_t[:, h:h + 1], scalar2=1.0,
                                                                  op0=ALU.mult, op1=ALU.mult), [mk_f.k, a_t.k], [lmh.k])
                    for h in range(8):
                        psA = psA0 if h < 4 else psA1
                        cx.op("pe", lambda e, h=h, psA=psA: e.matmul(psA[:, (h % 4) * 128:(h % 4 + 1) * 128], lhsT=lmh[:, h, :], rhs=mk_f[:, m2i, :],
                                                                    start=True, stop=True), [lmh.k, mk_f.k], [psA.k])
                    cx.op("pe", lambda e: e.matmul(psB[:, 0:8], lhsT=mk_f[:, m1i, :], rhs=a_t[:], start=True, stop=True), [mk_f.k, a_t.k], [psB.k])
                    cx.op("pe", lambda e: e.matmul(psB[:, 8:16], lhsT=mk_f[:, m2i, :], rhs=a_t[:], start=True, stop=True), [mk_f.k, a_t.k], [psB.k])
                    cx.op("pe", lambda e: e.matmul(psB[:, 16:24], lhsT=ones_f[:], rhs=a_t[:], start=True, stop=True), [ones_f.k, a_t.k], [psB.k])
                    cx.op("act", lambda e: e.activation(out=ex[:], in_=psB[:, 0:24], func=AF.Exp), [psB.k], [ex.k])
                    cx.op("act", lambda e: e.activation(out=Lm[:, 0:4, :], in_=V4(psA0[:, 0:512], 4), func=AF.Exp), [psA0.k], [Lm.k])
                    cx.op("act", lambda e: e.activation(out=Lm[:, 4:8, :], in_=V4(psA1[:, 0:512], 4), func=AF.Exp), [psA1.k, Lm.k], [Lm.k])
                    for g in range(2):
                        cx.op("pe", lambda e, g=g: e.matmul(psC[:, g * 128:(g + 1) * 128], lhsT=ms.BT[:, g, tok], rhs=ms.CT[:, g, tok],
                                                            start=True, stop=True), [ms.sk[c]], [psC.k])
                    cx.op("dve", lambda e: e.tensor_tensor(out=CBm[:], in0=V4(psC[:, 0:256], 2),
                                                           in1=mk_f[:, vi, :].unsqueeze(1).to_broadcast([128, 2, 128]), op=ALU.mult),
                          [psC.k, mk_f.k], [CBm.k])
                    for g in range(2):
                        cx.op("dve", lambda e, g=g: e.tensor_tensor(out=G[:, g * 4:(g + 1) * 4, :], in0=Lm[:, g * 4:(g + 1) * 4, :],
                                                                    in1=CBm[:, g, :].unsqueeze(1).to_broadcast([128, 4, 128]), op=ALU.mult),
                              [Lm.k, CBm.k, G.k], [G.k])
                    cx.op("dve", lambda e: e.tensor_tensor(out=xdt[:], in0=V4(ms.xs_tok[:, c, :], 8),
                                                           in1=dts[:].unsqueeze(2).to_broadcast([128, 8, 64]), op=ALU.mult),
                          [ms.sk[c], dts.k], [xdt.k])
                    cx.op("pool", lambda e: e.tensor_tensor(out=xw[:], in0=xdt[:], in1=ex[:, 0:8].unsqueeze(2).to_broadcast([128, 8, 64]),
                                                            op=ALU.mult), [xdt.k, ex.k], [xw.k])
                    for h in range(8):
                        cx.op("pe", lambda e, h=h: e.matmul(psY[:, h * 64:(h + 1) * 64], lhsT=G[:, h, :], rhs=xdt[:, h, :], start=True, stop=True),
                              [G.k, xdt.k], [psY.k])
                    for g in range(2):
                        cx.op("pe", lambda e, g=g: e.matmul(psO[:, g * 256:(g + 1) * 256], lhsT=ms.CT[:, g, tok], rhs=Sb[:, g * 256:(g + 1) * 256],
                                                            start=True, stop=True), [ms.sk[c], Sb.k], [psO.k])
                    cx.op("dve", lambda e: e.tensor_tensor(out=V4(ytmp[:], 8), in0=V4(psO[:, 0:512], 8),
                                                           in1=ex[:, 8:16].unsqueeze(2).to_broadcast([128, 8, 64]), op=ALU.mult),
                          [psO.k, ex.k], [ytmp.k])
                    yc = ycur[n % 2]
                    cx.op("dve", lambda e: e.tensor_tensor(out=yc[:], in0=ytmp[:], in1=psY[:, 0:512], op=ALU.add), [ytmp.k, psY.k], [yc.k])
                    for g in range(2):
                        cx.op("pe", lambda e, g=g: e.matmul(psS[:, g * 256:(g + 1) * 256], lhsT=ms.B_tok[:, c, g * 128:(g + 1) * 128],
                                                            rhs=xw[:, g * 4:(g + 1) * 4, :].rearrange("p a b -> p (a b)"), start=True, stop=True),
                              [ms.sk[c], xw.k], [psS.k])
                    cx.op("dve", lambda e: e.tensor_tensor(out=V4(S[:], 8), in0=V4(S[:], 8),
                                                           in1=ex[:, 16:24].unsqueeze(2).to_broadcast([128, 8, 64]), op=ALU.mult),
                          [S.k, ex.k, Sb.k], [S.k])
                    cx.op("dve", lambda e: e.tensor_tensor(out=S[:], in0=S[:], in1=psS[:, 0:512], op=ALU.add), [S.k, psS.k], [S.k])
                    cx.op("act", lambda e: e.activation(out=Sb[:], in_=S[:], func=AF.Identity), [S.k], [Sb.k])
                    if d == 0:
                        cx.dma("sp", [(ysacc[tok, :], yc[:])], [yc.k], [yk[c]], yc.k)
                        continue
                    yp = yprev[n % 2]
                    z = zt[n % 2]
                    cx.dma("sp", [(yp[:], ysacc[tok, :])], [yk[c]], [yp.k], yp.k)
                    cx.dma("sp", [(z[:], z_tok[tok, :])], [ms.zk[c]], [z.k], z.k)
                    cx.op("dve", lambda e: e.tensor_tensor(out=yc[:], in0=yc[:], in1=yp[:], op=ALU.add), [yc.k, yp.k], [yc.k])
                    cx.op("pool", lambda e: e.tensor_tensor(out=y2[:], in0=ms.xs_tok[:, c, :], in1=ms.bb[:, 32:544], op=ALU.mult),
                          [ms.sk[c], ms.bb.k], [y2.k])
                    cx.op("dve", lambda e: e.tensor_tensor(out=y2[:], in0=y2[:], in1=yc[:], op=ALU.add), [y2.k, yc.k], [y2.k])
                    cx.op("act", lambda e: e.activation(out=szt[:], in_=z[:], func=AF.Silu), [z.k], [szt.k])
                    cx.op("dve", lambda e: e.tensor_tensor(out=y2[:], in0=y2[:], in1=szt[:], op=ALU.mult), [y2.k, szt.k], [y2.k])
                    cx.op("act", lambda e: e.activation(out=junk[:], in_=y2[:], func=AF.Square, accum_out=ssum[:]), [y2.k], [junk.k, ssum.k])
                    cx.op("act", lambda e: e.activation(out=ssum[:], in_=ssum[:], func=AF.Sqrt, bias=eps_t[:], scale=1.0 / 512), [ssum.k, eps_t.k], [ssum.k])
                    cx.op("dve", lambda e: e.reciprocal(out=ssum[:], in_=ssum[:]), [ssum.k], [ssum.k])
                    cx.op("pool", lambda e: e.tensor_scalar(out=y2[:], in0=y2[:], scalar1=ssum[:, 0:1], scalar2=1.0, op0=ALU.mult, op1=ALU.mult),
                          [y2.k, ssum.k], [y2.k])
                    cx.op("dve", lambda e: e.tensor_tensor(out=ysn[:], in0=y2[:], in1=ms.bb[:, 544:1056], op=ALU.mult), [y2.k, ms.bb.k], [ysn.k])
                    for j in range(4):
                        cx.op("pe", lambda e, j=j: e.transpose(out=PT[:, j * 128:(j + 1) * 128], in_=ysn[:, j * 128:(j + 1) * 128],
                                                              identity=ident_b[:]), [ysn.k, ident_b.k], [PT.k])
                    yst = ysT[n % 2]
                    cx.op("act", lambda e: e.activation(out=yst[:], in_=V4(PT[:, 0:512], 4), func=AF.Identity), [PT.k], [yst.k])
                    cx.dma("sp", [(ymix_d[4:8, :, tok].rearrange("k p t -> p k t"), yst[:])], [yst.k], [ms.ymk[1][c]], yst.k)

        def m5_outproj(l, src, dst, ms):
            woutb = Buf(cx, "woutb", [128, KD, D], BF16)
            cx.dma("pool", [(woutb[:].rearrange("p k c -> p (k c)").rearrange("p (a b) -> p a b", b=2048),
                             w_out[l].rearrange("p (a b) -> p a b", b=2048))], [], [woutb.k], woutb.k)
            xt = [Buf(cx, "m5_xt%d" % i, [128, KD, TM], F32) for i in range(2)]
            sq = Buf(cx, "m5_sq", [128, KD, TM], BF16)
            std = Buf(cx, "m5_std", [128, TM], F32)
            rstd = Buf(cx, "m5_rstd", [128, TM], F32)
            fT = Buf(cx, "m5_fT", [128, KD, TM], F32)
            ymtb = [Buf(cx, "m5_ymt%d" % i, [128, KD, TM], BF16) for i in range(2)]
            dk = [Tk("m5_dst%d" % i) for i in range(NT // TM)]
            for ti in range(NT // TM):
                t0 = ti * TM
                seg = t0 // SEG
                x = xt[ti % 2]
                cx.dma("sp", [(x[:], src[:, :, t0:t0 + TM].rearrange("k p t -> p k t"))], [], [x.k], x.k)
                rks = [ms.ymk[h][t0 // 128 + j] for h in range(2) for j in range(TM // 128)]
                ymt = ymtb[ti % 2]
                cx.dma("sp", [(ymt[:], ymix_d[:, :, t0:t0 + TM].rearrange("k p t -> p k t"))], rks, [ymt.k], ymt.k)
                if debug and l == 0:
                    cx.op("dve", lambda e: e.tensor_copy(out=fT[:], in_=ymt[:]), [ymt.k], [fT.k])
                    cx.dma("sp", [(dbg_out[:, :, t0:t0 + TM].rearrange("k p t -> p k t"), fT[:])], [fT.k], [], fT.k)

                def psrc(o):
                    ps = PS[1 + o % 4]
                    for k in range(KD):
                        cx.op("pe", lambda e, k=k: e.matmul(ps[:, 0:TM], lhsT=woutb[:, k, o * 128:(o + 1) * 128], rhs=ymt[:, k, :],
                                                            start=(k == 0), stop=(k == KD - 1)), [woutb.k, ymt.k], [ps.k])
                    return ps
                post_tile(x, seg, TM, psrc, KD, sq, std, rstd, fT, G1)
                cx.dma("sp", [(dst[:, :, t0:t0 + TM].rearrange("k p t -> p k t"), x[:])], [x.k], [dk[ti]], x.k)


        KAP = 0.6065306597126334

        def m4_rwkv(l, ms):
            NTL = NT // TM
            wupb = Buf(cx, "wupb", [128, 512], BF16)
            aupb = Buf(cx, "aupb", [128, 512], BF16)
            gupb = Buf(cx, "gupb", [128, 512], BF16)
            cx.dma("pool", [(wupb[:], wup_in[l])], [], [wupb.k], wupb.k)
            cx.dma("pool", [(aupb[:], aup_in[l])], [], [aupb.k], aupb.k)
            cx.dma("pool", [(gupb[:], gup_in[l])], [], [gupb.k], gupb.k)
            c0 = Buf(cx, "c0", [128, 15], F32)
            omka = Buf(cx, "omka", [128, 4], F32)
            cx.op("dve", lambda e: e.tensor_tensor(out=c0[:], in0=ms.pp[:, 48:63], in1=ms.pp[:, 63:78], op=ALU.add), [ms.pp.k], [c0.k])
            cx.op("dve", lambda e: e.tensor_scalar(out=c0[:], in0=c0[:], scalar1=-1.0, scalar2=1.0, op0=ALU.mult, op1=ALU.add), [c0.k], [c0.k])
            cx.op("dve", lambda e: e.tensor_scalar(out=omka[:], in0=ms.pp[:, 98:102], scalar1=-1.0, scalar2=1.0, op0=ALU.mult, op1=ALU.add),
                  [ms.pp.k], [omka.k])
            m4m = [Buf(cx, "m4m%d" % d, [128, 4, 128], F32) for d in range(2)]
            for d in range(2):
                for q in range(4):
                    src_i = ((2, 1) if d == 0 else (0, 3))[q % 2]
                    cx.op("pool", lambda e, d=d, q=q, src_i=src_i: e.tensor_copy(out=m4m[d][:, q, :], in_=mk_f[:, src_i, :]), [mk_f.k], [m4m[d].k])
            FB = lambda nm: Buf(cx, nm, [128, TM], F32)
            BB = lambda nm: Buf(cx, nm, [128, TM], BF16)
            win = [Buf(cx, "m4_win%d" % i, [128, TM + 2], F32) for i in range(2)]
            nwin = [0]
            wl = FB("m4_wl")
            al = gl = wl
            twl, alb, sgl = BB("m4_twl"), BB("m4_alb"), BB("m4_sgl")
            rs = [FB("m4_rs%d" % g) for g in range(4)]
            ks = [FB("m4_ks%d" % g) for g in range(4)]
            vs = [FB("m4_vs%d" % g) for g in range(4)]
            AR = [Buf(cx, "m4_AR%d" % g, [128, 4, 256], BF16) for g in range(4)]
            ARh = [Buf(cx, "m4_ARh%d" % h, [128, 4, 256], BF16) for h in range(8)]
            kt = [BB("m4_kt%d" % g) for g in range(4)]
            bt = [BB("m4_bt%d" % g) for g in range(4)]
            v_tok = Buf(cx, "m4_vtok", [128, 4, 512], BF16)
            kh_tok = Buf(cx, "m4_khtok", [128, 4, 512], BF16)
            bh_tok = Buf(cx, "m4_bhtok", [128, 4, 512], BF16)
            gam = Buf(cx, "m4_gam", [128, 4, 4], F32)
            class _TS:
                pass
            TS = []
            for si in range(1):
                T = _TS()
                for n_ in "sg E1 E0 R0 R1 eN eP eA eH a_s kkr nrm kk tt kd bq".split():
                    setattr(T, n_, FB("m4_%s%d" % (n_, si)))
                for n_ in "sqk khT bhT vT".split():
                    setattr(T, n_, BB("m4_%s%d" % (n_, si)))
                TS.append(T)
            PSS = [[PS[0], PS[1], PS[2]], [PS[3], PS[5], PS[6]]]
            sg, E1, E0, R0, R1, sqk, khT = TS[0].sg, TS[0].E1, TS[0].E0, TS[0].R0, TS[0].R1, TS[0].sqk, TS[0].khT
            SC4 = [Buf(cx, "m4_SC%d" % i, [128, 4, 512], BF16) for i in range(2)]
            XT0s = [Buf(cx, "m4_XT0_%d" % i, [128, 4, 128], BF16) for i in range(2)]
            TT = [Buf(cx, "m4_TT%d" % i, [128, 4, 128], BF16) for i in range(2)]
            IB = []
            for si in range(2):
                mkb = lambda nm, n: [Buf(cx, "m4_%s%d_%d" % (nm, si, i), [128, 4, 128], BF16) for i in range(n)]
                IB.append((mkb("X", 2), mkb("XT", 2), mkb("R", 3), mkb("Noff", 3), mkb("Loff", 3), mkb("Dp", 2), mkb("DTp", 2),
                           mkb("M1b", 1)[0], mkb("M2b", 1)[0]))
            IBANK = [(PS[0], PS[1], PS[2]), (PS[3], PS[5], PS[6])]
            Wb = Buf(cx, "m4_Wb", [128, 512], BF16)
            Uneg = Buf(cx, "m4_Uneg", [128, 512], BF16)
            Hf = Buf(cx, "m4_Hf", [128, 4, 128], F32)
            Hb = Buf(cx, "m4_Hb", [128, 4, 128], BF16)
            tmpH = Buf(cx, "m4_tmpH", [128, 4, 128], F32)
            Ytile = Buf(cx, "m4_Ytile", [128, 4, TM], F32)
            yfw = Buf(cx, "m4_yfw", [128, 4, TM], F32) if DBG == "rdbg" else None
            yfw1 = FB("m4_yfw1")
            yv, ycn, yn2, rk, bon = sg, E1, E0, R0, R1
            ybf, rkb = sqk, khT
            yob = [BB("m4_yo%d" % i) for i in range(2)]
            yrk = [Tk("yracc%d" % ti) for ti in range(NTL)]

            def rr(gens):
                active = list(gens)
                while active:
                    for gg in list(active):
                        try:
                            next(gg)
                        except StopIteration:
                            active.remove(gg)

            def shift(fc, ti, out):
                w = win[nwin[0] % 2]
                nwin[0] += 1
                load_win(w, fc, ti, 1, ms)
                cx.op("pool", lambda e: e.tensor_scalar(out=out[:], in0=w[:, 1:TM + 1], scalar1=c0[:, fc:fc + 1], scalar2=1.0, op0=ALU.mult, op1=ALU.mult),
                      [w.k, c0.k], [out.k])
                cx.op("dve", lambda e: e.scalar_tensor_tensor(out=out[:], in0=w[:, 0:TM], scalar=ms.pp[:, 48 + fc:49 + fc], in1=out[:],
                                                              op0=ALU.mult, op1=ALU.add), [w.k, ms.pp.k, out.k], [out.k])
                cx.op("dve", lambda e: e.scalar_tensor_tensor(out=out[:], in0=w[:, 2:TM + 2], scalar=ms.pp[:, 63 + fc:64 + fc], in1=out[:],
                                                              op0=ALU.mult, op1=ALU.add), [w.k, ms.pp.k, out.k], [out.k])

            def tt_(eng, out, a, b, op, rd, wr):
                cx.op(eng, lambda e: e.tensor_tensor(out=out, in0=a, in1=b, op=op), rd, wr)

            def prep(d, ti):
                shift(12, ti, wl)
                cx.op("act", lambda e: e.activation(out=twl[:], in_=wl[:], func=AF.Tanh), [wl.k], [twl.k])
                shift(13, ti, al)
                cx.op("act", lambda e: e.activation(out=alb[:], in_=al[:], func=AF.Identity), [al.k], [alb.k])
                if d == 1:
                    shift(14, ti, gl)
                    cx.op("act", lambda e: e.activation(out=sgl[:], in_=gl[:], func=AF.Sigmoid), [gl.k], [sgl.k])
                dp = slice(d * 64, d * 64 + 64)
                def prep_g(g, T, PSg):
                    sg, E1, E0, R0, R1, eN, eP, eA, eH, a_s, kkr, nrm, kk, tt, kd, bq, sqk, khT, bhT, vT = [getattr(T, n_) for n_ in 'sg, E1, E0, R0, R1, eN, eP, eA, eH, a_s, kkr, nrm, kk, tt, kd, bq, sqk, khT, bhT, vT'.split(', ')]
                    gs = slice(g * 128, (g + 1) * 128)
                    X1, X0e, Y0 = (E1, E0, R0) if d == 0 else (R1, R0, E0)

                    def transp(srcT, dstk):
                        for j in range(4):
                            cx.op("pe", lambda e, j=j: e.transpose(out=PT[:, j * 128:(j + 1) * 128], in_=srcT[:, j * 128:(j + 1) * 128],
                                                                  identity=ident_b[:]), [srcT.k, ident_b.k], [PT.k])
                        cx.op("dve", lambda e: e.tensor_copy(out=dstk[:, :, gs], in_=V4(PT[:, 0:512], 4)), [PT.k, dstk.k], [dstk.k])

                    def chainA():
                        cx.op("pe", lambda e: e.matmul(PSg[0][:, 0:TM], lhsT=wupb[dp, gs], rhs=twl[dp, :], start=True, stop=True), [wupb.k, twl.k], [PSg[0].k])
                        yield
                        cx.op("act", lambda e: e.activation(out=sg[:], in_=PSg[0][:, 0:TM], func=AF.Sigmoid, bias=ms.pp[:, 78 + d * 4 + g:79 + d * 4 + g]),
                              [PSg[0].k, ms.pp.k], [sg.k])
                        yield
                        cx.op("dve", lambda e: e.tensor_tensor_scan(out=E1[:], data0=rmask[:], data1=sg[:], initial=0.0, op0=ALU.mult, op1=ALU.add),
                              [rmask.k, sg.k], [E1.k])
                        yield
                        tt_("pool", E0[:], E1[:], sg[:], ALU.subtract, [E1.k, sg.k], [E0.k])
                        yield
                        tt_("dve", V4(R0[:], 4), V4(E1[:], 4)[:, :, 127:128].to_broadcast([128, 4, 128]), V4(E1[:], 4), ALU.subtract, [E1.k], [R0.k])
                        yield
                        tt_("pool", R1[:], R0[:], sg[:], ALU.add, [R0.k, sg.k], [R1.k])
                        yield
                        cx.op("act", lambda e: e.activation(out=eN[:], in_=X1[:], func=AF.Exp, scale=-KAP), [X1.k], [eN.k])
                        yield
                        cx.op("act", lambda e: e.activation(out=eP[:], in_=X1[:], func=AF.Exp, scale=KAP), [X1.k], [eP.k])
                        yield
                        cx.op("act", lambda e: e.activation(out=eA[:], in_=X0e[:], func=AF.Exp, scale=-KAP), [X0e.k], [eA.k])
                        yield
                        cx.op("act", lambda e: e.activation(out=eH[:], in_=Y0[:], func=AF.Exp, scale=-KAP), [Y0.k], [eH.k])
                        yield
                        cx.op("act", lambda e: e.activation(out=gam[:, g, :], in_=V4(E1[:], 4)[:, :, 127], func=AF.Exp, scale=-KAP), [E1.k], [gam.k])
                        yield

                    def chainB():
                        shift(4 + g, ti, ks[g])
                        yield
                        cx.op("pool", lambda e: e.tensor_scalar(out=kkr[:], in0=ks[g][:], scalar1=ms.pp[:, 94 + g:95 + g], scalar2=1.0, op0=ALU.mult, op1=ALU.mult),
                              [ks[g].k, ms.pp.k], [kkr.k])
                        yield
                        cx.op("act", lambda e: e.activation(out=sqk[:], in_=kkr[:], func=AF.Square), [kkr.k], [sqk.k])
                        yield
                        cx.op("pe", lambda e: e.matmul(PSg[2][:, 0:TM], lhsT=blk1_b[:], rhs=sqk[:], start=True, stop=True), [blk1_b.k, sqk.k], [PSg[2].k])
                        cx.op("pe", lambda e: e.matmul(PSg[1][:, 0:TM], lhsT=aupb[dp, gs], rhs=alb[dp, :], start=True, stop=True), [aupb.k, alb.k], [PSg[1].k])
                        yield
                        cx.op("dve", lambda e: e.tensor_scalar(out=nrm[:], in0=PSg[2][:, 0:TM], scalar1=2.0 ** -60, scalar2=None, op0=ALU.max), [PSg[2].k], [nrm.k])
                        yield
                        cx.op("act", lambda e: e.activation(out=a_s[:], in_=PSg[1][:, 0:TM], func=AF.Sigmoid, bias=ms.pp[:, 86 + d * 4 + g:87 + d * 4 + g]),
                              [PSg[1].k, ms.pp.k], [a_s.k])
                        cx.op("act", lambda e: e.activation(out=nrm[:], in_=nrm[:], func=AF.Ln), [nrm.k], [nrm.k])
                        yield
                        cx.op("act", lambda e: e.activation(out=nrm[:], in_=nrm[:], func=AF.Exp, scale=-0.5), [nrm.k], [nrm.k])
                        yield
                        cx.op("pool", lambda e: e.tensor_scalar(out=tt[:], in0=a_s[:], scalar1=ms.pp[:, 98 + g:99 + g], scalar2=omka[:, g:g + 1],
                                                                op0=ALU.mult, op1=ALU.add), [a_s.k, ms.pp.k, omka.k], [tt.k])
                        yield
                        tt_("dve", kk[:], kkr[:], nrm[:], ALU.mult, [kkr.k, nrm.k], [kk.k])
                        yield
                        tt_("dve", kd[:], ks[g][:], tt[:], ALU.mult, [ks[g].k, tt.k], [kd.k])
                        yield
                        tt_("pool", bq[:], kk[:], a_s[:], ALU.mult, [kk.k, a_s.k], [bq.k])
                        yield

                    def chainC():
                        shift(8 + g, ti, vs[g])
                        yield
                        cx.op("act", lambda e: e.activation(out=vT[:], in_=vs[g][:], func=AF.Identity), [vs[g].k], [vT.k])
                        yield
                        shift(g, ti, rs[g])
                        yield
                        transp(vT, v_tok)
                        yield

                    rr([chainA(), chainB(), chainC()])
                    tt_("dve", AR[g][:, :, 0:128], V4(kk[:], 4), V4(eA[:], 4), ALU.mult, [kk.k, eA.k], [AR[g].k])
                    tt_("pool", AR[g][:, :, 128:256], V4(rs[g][:], 4), V4(eN[:], 4), ALU.mult, [rs[g].k, eN.k, AR[g].k], [AR[g].k])
                    tt_("dve", khT[:], kd[:], eH[:], ALU.mult, [kd.k, eH.k], [khT.k])
                    tt_("pool", bhT[:], bq[:], eH[:], ALU.mult, [bq.k, eH.k], [bhT.k])
                    tt_("dve", kt[g][:], kd[:], eP[:], ALU.mult, [kd.k, eP.k], [kt[g].k])
                    tt_("pool", bt[g][:], bq[:], eP[:], ALU.mult, [bq.k, eP.k], [bt[g].k])
                    transp(khT, kh_tok)
                    for hl in range(2):
                        eng = "dve" if hl == 0 else "pool"
                        cx.op(eng, lambda e, hl=hl: e.tensor_scalar(out=ARh[2 * g + hl][:], in0=AR[g][:], scalar1=blk1_f[:, hl * 64:hl * 64 + 1], scalar2=1.0,
                                                                    op0=ALU.mult, op1=ALU.mult), [AR[g].k, blk1_f.k], [ARh[2 * g + hl].k])
                    transp(bhT, bh_tok)
                    return
                    yield

                for g in range(4):
                    for _ in prep_g(g, TS[0], PSS[0]):
                        pass

            def scan(d, ti, j):
                cs = slice(j * 128, (j + 1) * 128)
                mL = 0 if d == 0 else 2
                c = ti * 4 + j
                if (d == 0 and c == NC // 2) or (d == 1 and c == NC // 2 - 1):
                    cx.op("dve", lambda e: e.tensor_scalar(out=Hf[:], in0=Hf[:], scalar1=flag_t[:, 0:1], scalar2=1.0, op0=ALU.mult, op1=ALU.mult),
                          [Hf.k, flag_t.k], [Hf.k])
                    cx.op("act", lambda e: e.activation(out=Hb[:], in_=Hf[:], func=AF.Identity), [Hf.k], [Hb.k])
                for hg in range(2):
                    sc = SC4[hg]
                    for hh in range(4):
                        h = hg * 4 + hh
                        g = h // 2
                        po = slice((h % 2) * 64, (h % 2) * 64 + 64)
                        ps = PS[hh]
                        cx.op("pe", lambda e: e.matmul(ps[:, 0:256], lhsT=kt[g][:, cs], rhs=ARh[h][:, j, :], start=True, stop=True),
                              [kt[g].k, ARh[h].k], [ps.k])
                        cx.op("pe", lambda e: e.matmul(ps[:, 256:512], lhsT=bt[g][:, cs], rhs=ARh[h][:, j, :], start=True, stop=True),
                              [bt[g].k, ARh[h].k], [ps.k])
                        cx.op("pe", lambda e: e.matmul(PS[4][:, hh * 128:(hh + 1) * 128], lhsT=ARh[h][:, j, 0:128], rhs=bt[g][:, cs], start=True, stop=True),
                              [bt[g].k, ARh[h].k], [PS[4].k])
                        cx.op("dve", lambda e: e.tensor_tensor(out=sc[:, hh, :], in0=ps[:, 0:512], in1=m4m[d][:].rearrange("p a b -> p (a b)"), op=ALU.mult),
                              [ps.k, m4m[d].k, sc.k], [sc.k])
                    cx.op("dve", lambda e: e.tensor_tensor(out=XT0s[hg][:], in0=V4(PS[4][:, 0:512], 4),
                                                           in1=mk_f[:, mL, :].unsqueeze(1).to_broadcast([128, 4, 128]), op=ALU.mult),
                          [PS[4].k, mk_f.k], [XT0s[hg].k])
                def inv_g(hg):
                    sc = SC4[hg]
                    XT0 = XT0s[hg]
                    Xp, XTp, Rp, Noff, Loff, Dp, DTp, M1b, M2b = IB[hg]
                    P1, P2, P3 = IBANK[hg]
                    bmb = lambda q: bm_b[:, q, :].unsqueeze(1).to_broadcast([128, 4, 128])
                    N0, L0 = Xp[0], XTp[0]
                    cx.op("dve", lambda e: e.tensor_tensor(out=N0[:], in0=sc[:, :, 256:384], in1=bmb(0), op=ALU.mult), [sc.k, bm_b.k], [N0.k])
                    yield
                    cx.op("dve", lambda e: e.tensor_tensor(out=L0[:], in0=XT0[:], in1=bmb(0), op=ALU.mult), [XT0.k, bm_b.k], [L0.k])
                    yield
                    for q in range(3):
                        cx.op("pool", lambda e, q=q: e.tensor_tensor(out=Noff[q][:], in0=sc[:, :, 256:384], in1=bmb(q + 1), op=ALU.mult),
                              [sc.k, bm_b.k], [Noff[q].k])
                        yield
                        cx.op("dve", lambda e, q=q: e.tensor_tensor(out=Loff[q][:], in0=XT0[:], in1=bmb(q + 1), op=ALU.mult),
                              [XT0.k, bm_b.k], [Loff[q].k])
                        yield
                    R = Rp[2]
                    cx.op("dve", lambda e: e.tensor_tensor(out=R[:], in0=ident_b[:].unsqueeze(1).to_broadcast([128, 4, 128]), in1=N0[:],
                                                            op=ALU.subtract), [ident_b.k, N0.k], [R.k])
                    yield
                    Xc, XTc = N0, L0
                    for k in range(3):
                        XTn = XTp[(k + 1) % 2]
                        Xn = Xp[(k + 1) % 2]
                        for hh in range(4):
                            cx.op("pe", lambda e, hh=hh: e.matmul(P2[:, hh * 128:(hh + 1) * 128], lhsT=Xc[:, hh, :], rhs=XTc[:, hh, :], start=True, stop=True),
                                  [Xc.k, XTc.k], [P2.k])
                        if k < 2:
                            for hh in range(4):
                                cx.op("pe", lambda e, hh=hh: e.matmul(P1[:, hh * 128:(hh + 1) * 128], lhsT=XTc[:, hh, :], rhs=Xc[:, hh, :], start=True, stop=True),
                                      [Xc.k, XTc.k], [P1.k])
                        cx.op("dve", lambda e: e.tensor_copy(out=XTn[:], in_=V4(P2[:, 0:512], 4)), [P2.k], [XTn.k])
                        yield
                        if k < 2:
                            cx.op("act", lambda e: e.activation(out=Xn[:], in_=V4(P1[:, 0:512], 4), func=AF.Identity), [P1.k], [Xn.k])
                            yield
                        for hh in range(4):
                            cx.op("pe", lambda e, hh=hh: e.matmul(P3[:, hh * 128:(hh + 1) * 128], lhsT=XTn[:, hh, :], rhs=R[:, hh, :], start=True, stop=False),
                                  [XTn.k, R.k], [P3.k])
                            cx.op("pe", lambda e, hh=hh: e.matmul(P3[:, hh * 128:(hh + 1) * 128], lhsT=ident_b[:], rhs=R[:, hh, :], start=False, stop=True),
                                  [ident_b.k, R.k], [P3.k])
                        Rn = Rp[k % 2]
                        cx.op("act", lambda e: e.activation(out=Rn[:], in_=V4(P3[:, 0:512], 4), func=AF.Identity), [P3.k], [Rn.k])
                        yield
                        R = Rn
                        Xc, XTc = Xn, XTn
                    DT = R
                    Dn = Dp[0]
                    for hh in range(4):
                        cx.op("pe", lambda e, hh=hh: e.transpose(out=PT[:, hh * 128:(hh + 1) * 128], in_=DT[:, hh, :], identity=ident_b[:]),
                              [DT.k, ident_b.k], [PT.k])
                    cx.op("dve", lambda e: e.tensor_copy(out=Dn[:], in_=V4(PT[:, 0:512], 4)), [PT.k], [Dn.k])
                    yield
                    for q in range(3):
                        last = (q == 2)
                        DTn = TT[hg] if last else DTp[q % 2]
                        Dnn = Dp[(q + 1) % 2]
                        for hh in range(4):
                            cx.op("pe", lambda e, hh=hh: e.matmul(P1[:, hh * 128:(hh + 1) * 128], lhsT=Loff[q][:, hh, :], rhs=DT[:, hh, :], start=True, stop=True),
                                  [Loff[q].k, DT.k], [P1.k])
                        cx.op("act", lambda e: e.activation(out=M1b[:], in_=V4(P1[:, 0:512], 4), func=AF.Identity), [P1.k], [M1b.k])
                        yield
                        if not last:
                            for hh in range(4):
                                cx.op("pe", lambda e, hh=hh: e.matmul(P2[:, hh * 128:(hh + 1) * 128], lhsT=Noff[q][:, hh, :], rhs=Dn[:, hh, :], start=True, stop=True),
                                      [Noff[q].k, Dn.k], [P2.k])
                            cx.op("dve", lambda e: e.tensor_copy(out=M2b[:], in_=V4(P2[:, 0:512], 4)), [P2.k], [M2b.k])
                            yield
                        for hh in range(4):
                            cx.op("pe", lambda e, hh=hh: e.matmul(P3[:, hh * 128:(hh + 1) * 128], lhsT=Dn[:, hh, :], rhs=M1b[:, hh, :], start=True, stop=True),
                                  [Dn.k, M1b.k], [P3.k])
                        cx.op("dve", lambda e: e.tensor_tensor(out=DTn[:], in0=DT[:], in1=V4(P3[:, 0:512], 4), op=ALU.subtract), [DT.k, P3.k], [DTn.k])
                        yield
                        if not last:
                            for hh in range(4):
                                cx.op("pe", lambda e, hh=hh: e.matmul(P1[:, hh * 128:(hh + 1) * 128], lhsT=DT[:, hh, :], rhs=M2b[:, hh, :], start=True, stop=True),
                                      [DT.k, M2b.k], [P1.k])
                            cx.op("dve", lambda e: e.tensor_tensor(out=Dnn[:], in0=Dn[:], in1=V4(P1[:, 0:512], 4), op=ALU.subtract), [Dn.k, P1.k], [Dnn.k])
                            yield
                            Dn = Dnn
                        DT = DTn

                rr([inv_g(0), inv_g(1)])
                psW, psU, psY, psH = PS[0], PS[1], PS[2], PS[3]
                for g in range(4):
                    cx.op("pe", lambda e: e.matmul(psW[:, g * 128:(g + 1) * 128], lhsT=AR[g][:, j, 0:128], rhs=Hb[:, g, :], start=True, stop=False),
                          [AR[g].k, Hb.k], [psW.k])
                    for hl in range(2):
                        h = 2 * g + hl
                        hg, hh = divmod(h, 4)
                        cx.op("pe", lambda e: e.matmul(psW[:, h * 64:(h + 1) * 64], lhsT=SC4[hg][:, hh, 0:128], rhs=v_tok[:, j, h * 64:(h + 1) * 64],
                                                       start=False, stop=(hl == 1)), [SC4[hg].k, v_tok.k], [psW.k])
                cx.op("act", lambda e: e.activation(out=Wb[:], in_=psW[:, 0:512], func=AF.Identity), [psW.k], [Wb.k])
                for h in range(8):
                    hg, hh = divmod(h, 4)
                    cx.op("pe", lambda e: e.matmul(psU[:, h * 64:(h + 1) * 64], lhsT=TT[hg][:, hh, :], rhs=Wb[:, h * 64:(h + 1) * 64], start=True, stop=True),
                          [TT[hg].k, Wb.k], [psU.k])
                cx.op("act", lambda e: e.activation(out=Uneg[:], in_=psU[:, 0:512], func=AF.Identity, scale=-1.0), [psU.k], [Uneg.k])
                for g in range(4):
                    cx.op("pe", lambda e: e.matmul(psY[:, g * 128:(g + 1) * 128], lhsT=Hb[:, g, :], rhs=AR[g][:, j, 128:256], start=True, stop=False),
                          [AR[g].k, Hb.k], [psY.k])
                    for hl in range(2):
                        h = 2 * g + hl
                        hg, hh = divmod(h, 4)
                        po = slice(hl * 64, hl * 64 + 64)
                        cx.op("pe", lambda e: e.matmul(psY[po, g * 128:(g + 1) * 128], lhsT=v_tok[:, j, h * 64:(h + 1) * 64], rhs=SC4[hg][:, hh, 128:256],
                                                       start=False, stop=False), [SC4[hg].k, v_tok.k], [psY.k])
                        cx.op("pe", lambda e: e.matmul(psY[po, g * 128:(g + 1) * 128], lhsT=Uneg[:, h * 64:(h + 1) * 64], rhs=SC4[hg][:, hh, 384:512],
                                                       start=False, stop=True), [SC4[hg].k, Uneg.k], [psY.k])
                cx.op("act", lambda e: e.activation(out=Ytile[:, :, cs], in_=V4(psY[:, 0:512], 4), func=AF.Identity), [psY.k, Ytile.k], [Ytile.k])
                for g in range(4):
                    gs = slice(g * 128, (g + 1) * 128)
                    cx.op("pe", lambda e: e.matmul(psH[:, gs], lhsT=kh_tok[:, j, gs], rhs=v_tok[:, j, gs], start=True, stop=False),
                          [kh_tok.k, v_tok.k], [psH.k])
                    cx.op("pe", lambda e: e.matmul(psH[:, gs], lhsT=bh_tok[:, j, gs], rhs=Uneg[:, gs], start=False, stop=True),
                          [bh_tok.k, Uneg.k], [psH.k])
                cx.op("dve", lambda e: e.tensor_tensor(out=tmpH[:], in0=V4(psH[:, 0:512], 4), in1=blk1_f[:].unsqueeze(1).to_broadcast([128, 4, 128]),
                                                       op=ALU.mult), [psH.k, blk1_f.k], [tmpH.k])
                cx.op("dve", lambda e: e.tensor_tensor(out=Hf[:], in0=Hf[:], in1=gam[:, :, j:j + 1].to_broadcast([128, 4, 128]), op=ALU.mult),
                      [Hf.k, gam.k, Hb.k], [Hf.k])
                cx.op("dve", lambda e: e.tensor_tensor(out=Hf[:], in0=Hf[:], in1=tmpH[:], op=ALU.add), [Hf.k, tmpH.k], [Hf.k])
                cx.op("act", lambda e: e.activation(out=Hb[:], in_=Hf[:], func=AF.Identity), [Hf.k], [Hb.k])

            def finalize(ti):
                t0 = ti * TM
                cks = [ms.ymk[0][t0 // 128 + jj] for jj in range(TM // 128)]
                for g in range(4):
                    gs = slice(g * 128, (g + 1) * 128)
                    cx.dma("sp", [(yfw1[:], yracc[g, :, t0:t0 + TM])], [yrk[ti]], [yfw1.k], yfw1.k)
                    tt_("dve", yv[:], Ytile[:, g, :], yfw1[:], ALU.add, [Ytile.k, yfw1.k], [yv.k])
                    cx.op("act", lambda e: e.activation(out=ybf[:], in_=yv[:], func=AF.Identity), [yv.k], [ybf.k])
                    cx.op("pe", lambda e: e.matmul(PS[5][:, 0:TM], lhsT=blk64_b[:], rhs=ybf[:], start=True, stop=True), [blk64_b.k, ybf.k], [PS[5].k])
                    tt_("dve", ycn[:], yv[:], PS[5][:, 0:TM], ALU.subtract, [yv.k, PS[5].k], [ycn.k])
                    cx.op("act", lambda e: e.activation(out=ybf[:], in_=ycn[:], func=AF.Square), [ycn.k], [ybf.k])
                    cx.op("pe", lambda e: e.matmul(PS[6][:, 0:TM], lhsT=blk64_b[:], rhs=ybf[:], start=True, stop=True), [blk64_b.k, ybf.k], [PS[6].k])
                    cx.op("act", lambda e: e.activation(out=yn2[:], in_=PS[6][:, 0:TM], func=AF.Ln, bias=gneps_t[:]), [PS[6].k, gneps_t.k], [yn2.k])
                    cx.op("act", lambda e: e.activation(out=yn2[:], in_=yn2[:], func=AF.Exp, scale=-0.5), [yn2.k], [yn2.k])
                    tt_("dve", ycn[:], ycn[:], yn2[:], ALU.mult, [ycn.k, yn2.k], [ycn.k])
                    cx.op("pool", lambda e: e.tensor_scalar(out=yn2[:], in0=ycn[:], scalar1=ms.pp[:, 106 + g:107 + g], scalar2=ms.pp[:, 110 + g:111 + g],
                                                            op0=ALU.mult, op1=ALU.add), [ycn.k, ms.pp.k], [yn2.k])
                    cx.op("pool", lambda e: e.tensor_scalar(out=rk[:], in0=rs[g][:], scalar1=ms.pp[:, 102 + g:103 + g], scalar2=1.0, op0=ALU.mult, op1=ALU.mult),
                          [rs[g].k, ms.pp.k], [rk.k])
                    tt_("dve", rk[:], rk[:], ks[g][:], ALU.mult, [rk.k, ks[g].k], [rk.k])
                    cx.op("act", lambda e: e.activation(out=rkb[:], in_=rk[:], func=AF.Identity), [rk.k], [rkb.k])
                    cx.op("pe", lambda e: e.matmul(PS[5][:, 0:TM], lhsT=blk1_b[:], rhs=rkb[:], start=True, stop=True), [blk1_b.k, rkb.k], [PS[5].k])
                    tt_("dve", bon[:], PS[5][:, 0:TM], vs[g][:], ALU.mult, [PS[5].k, vs[g].k], [bon.k])
                    tt_("pool", yn2[:], yn2[:], bon[:], ALU.add, [yn2.k, bon.k], [yn2.k])
                    cx.op("pe", lambda e: e.matmul(PS[6][:, 0:TM], lhsT=gupb[:, gs], rhs=sgl[:], start=True, stop=True), [gupb.k, sgl.k], [PS[6].k])
                    yo = yob[g % 2]
                    cx.op("dve", lambda e: e.tensor_tensor(out=yo[:], in0=yn2[:], in1=PS[6][:, 0:TM], op=ALU.mult), [yn2.k, PS[6].k], [yo.k])
                    cx.dma("sp", [(ymix_d[g, :, t0:t0 + TM], yo[:])], [yo.k], cks, yo.k)

            for d in range(2):
                cx.op("dve", lambda e: e.memset(Hf[:], 0.0), [], [Hf.k])
                cx.op("pool", lambda e: e.memset(Hb[:], 0.0), [], [Hb.k])
                tiles = range(NTL) if d == 0 else range(NTL - 1, -1, -1)
                for ti in tiles:
                    prep(d, ti)
                    for j in (range(4) if d == 0 else range(3, -1, -1)):
                        if "m4scan" not in SKIP:
                            scan(d, ti, j)
                    if DBG == "rdbg" and d == 0 and ti == 0:
                        cx.op("dve", lambda e: e.tensor_copy(out=yfw[:, 0, :], in_=SC4[1][:, 0, :]), [SC4[1].k], [yfw.k])
                        cx.op("dve", lambda e: e.tensor_copy(out=yfw[:, 1, :], in_=SC4[1][:, 2, :]), [SC4[1].k], [yfw.k])
                        cx.op("dve", lambda e: e.tensor_copy(out=yfw[:, 2, :], in_=TT[1][:].rearrange("p a b -> p (a b)")), [TT[1].k], [yfw.k])
                        cx.op("dve", lambda e: e.tensor_copy(out=yfw[:, 3, :], in_=Wb[:]), [Wb.k], [yfw.k])
                        for slot in range(4):
                            cx.dma("sp", [(dbg_out[slot, :, 0:TM], yfw[:, slot, :])], [yfw.k], [], yfw.k)
                        cx.op("dve", lambda e: e.tensor_copy(out=yfw[:, 0, :], in_=Uneg[:]), [Uneg.k], [yfw.k])
                        cx.op("dve", lambda e: e.tensor_copy(out=yfw[:, 1, :], in_=v_tok[:, 3, :]), [v_tok.k], [yfw.k])
                        cx.op("dve", lambda e: e.tensor_copy(out=yfw[:, 2, :], in_=kt[2][:]), [kt[2].k], [yfw.k])
                        cx.op("dve", lambda e: e.tensor_copy(out=yfw[:, 3, 0:256], in_=AR[2][:, 3, :]), [AR[2].k], [yfw.k])
                        for slot in range(4):
                            cx.dma("sp", [(dbg_out[4 + slot, :, 0:(TM if slot < 3 else 256)], yfw[:, slot, 0:(TM if slot < 3 else 256)])], [yfw.k], [], yfw.k)
                        return
                    if d == 0:
                        cx.dma("sp", [(yracc[:, :, ti * TM:(ti + 1) * TM].rearrange("g p t -> p g t"), Ytile[:])], [Ytile.k], [yrk[ti]], Ytile.k)
                    else:
                        finalize(ti)


        def mixer_phase(l, src, dst, em):
            ms = MixState()
            mix_load_params(l, ms)
            with contextlib.ExitStack() as e1:
                cx.mem_es = e1
                fo = len(cx.owners)
                if "m1" not in SKIP:
                    m1_inproj(l, src, ms)
                cx.end_phase(fo)
            cx.mem_es = em
            with contextlib.ExitStack() as e2:
                cx.mem_es = e2
                fo = len(cx.owners)
                if "m2" not in SKIP:
                    m2_ssdprep(l, ms)
                if "m3" not in SKIP and "m2" not in SKIP:
                    m3_ssd(l, ms)
                cx.end_phase(fo)
            cx.mem_es = em
            with contextlib.ExitStack() as e4:
                cx.mem_es = e4
                fo = len(cx.owners)
                if "m4" not in SKIP:
                    m4_rwkv(l, ms)
                cx.end_phase(fo)
            cx.mem_es = em
            with contextlib.ExitStack() as e5:
                cx.mem_es = e5
                fo = len(cx.owners)
                if DBG != "rdbg":
                    m5_outproj(l, src, dst, ms)
                cx.end_phase(fo)
            cx.mem_es = em

        out_tks = []
        cur = xT
        for l in range(depth):
            with contextlib.ExitStack() as fes:
                cx.mem_es = fes
                fo = len(cx.owners)
                mod_phase(l)
                cx.end_phase(fo)
                cx.mem_es = es
            import os
            if do_mix:
                mdst = xres if (do_ffn or l < depth - 1) else yT
                with contextlib.ExitStack() as em:
                    cx.mem_es = em
                    fo = len(cx.owners)
                    mixer_phase(l, cur, mdst, em)
                    cx.end_phase(fo)
                    cx.mem_es = es
                cur = mdst
            if DBG in ("mod", "modmm", "const", "modtt"):
                dbg = Buf(cx, "dbg", [128, 512], F32)
                cx.op("dve", lambda e: e.tensor_copy(out=dbg[:, 0:96], in_=modT[:].rearrange("p c s -> p (c s)")), [modT.k], [dbg.k])
                cx.op("dve", lambda e: e.tensor_copy(out=dbg[:, 96:112], in_=A2[:].rearrange("p c s -> p (c s)")), [A2.k], [dbg.k])
                cx.dma("sp", [(yT[0, :, 0:512], dbg[:])], [dbg.k], [], dbg.k)
                break
            if do_ffn and "ffn" not in SKIP:
                last = (l == depth - 1)
                dst = yT if last else xres
                with contextlib.ExitStack() as fes:
                    cx.mem_es = fes
                    fo = len(cx.owners)
                    out_tks = ffn_phase(l, cur, dst)
                    cx.end_phase(fo)
                    cx.mem_es = es
                cur = dst
        cx.barrier()
    return nc


def _prep_core_inputs(inputs, depth=4):
    f = np.float32
    sh = {}
    wm = np.asarray(inputs["w_mod"], f)[:depth]
    sh["w_mod"] = np.ascontiguousarray(wm.reshape(depth, KD, 128, NMOD * D).transpose(0, 2, 1, 3))
    bm = np.asarray(inputs["b_mod"], f)[:depth]
    sh["b_mod"] = np.ascontiguousarray(bm.reshape(depth, NMOD * KD, 128).transpose(0, 2, 1))
    ng = np.asarray(inputs["norm_g"], f)[:depth]
    sh["norm_g"] = np.ascontiguousarray(ng.reshape(depth, 4, KD, 128).transpose(0, 3, 1, 2))
    w1 = np.asarray(inputs["w_ff1"], f)[:depth]
    sh["w_ff1"] = np.ascontiguousarray(w1.reshape(depth, KD, 128, DFF).transpose(0, 2, 1, 3)).reshape(depth, 128, KD * DFF)
    w2 = np.asarray(inputs["w_ff2"], f)[:depth]
    sh["w_ff2"] = np.ascontiguousarray(w2.reshape(depth, DFF // 128, 128, D).transpose(0, 2, 1, 3)).reshape(depth, 128, (DFF // 128) * D)
    sh["c_ones"] = np.ones((128, 128), f)
    sh["c_ident"] = np.eye(128, dtype=f)
    b1 = np.zeros((128, 128), f)
    b1[:64, :64] = 1.0
    b1[64:, 64:] = 1.0
    sh["c_blk1"] = b1
    ii = np.arange(128)
    Bk = lambda b: ((ii[:, None] // b) == (ii[None, :] // b)).astype(f)
    sh["c_bmask"] = np.ascontiguousarray(np.stack([Bk(16), Bk(32) - Bk(16), Bk(64) - Bk(32), 1.0 - Bk(64)], axis=1))
    rm = np.ones((128, 512), f)
    rm[:, ::128] = 0.0
    sh["c_rmask"] = rm
    r = np.arange(128)[:, None]
    c = np.arange(128)[None, :]
    sh["c_masks"] = np.ascontiguousarray(np.stack([(r > c), (r <= c), (r < c), (r >= c)], axis=1).astype(f))
    wi = np.asarray(inputs["w_in"], f)[:depth]
    RW = 1920
    wi2 = np.zeros((depth, D, 3584), f)
    wi2[:, :, 0:RW] = wi[:, :, 0:RW]
    wi2[:, :, RW:RW + 1024] = wi[:, :, RW + 512:RW + 1536]
    wi2[:, :, 2944:3456] = wi[:, :, RW:RW + 512]
    wi2[:, :, 3456:3472] = wi[:, :, RW + 1536:RW + 1552]
    sh["w_in"] = np.ascontiguousarray(wi2.reshape(depth, KD, 128, 3584).transpose(0, 2, 1, 3)).reshape(depth, 128, KD * 3584)
    wo = np.asarray(inputs["w_out"], f)[:depth]
    sh["w_out"] = np.ascontiguousarray(wo.reshape(depth, KD, 128, D).transpose(0, 2, 1, 3)).reshape(depth, 128, KD * D)
    pp = np.zeros((depth, 128, 114), f)
    cw = np.asarray(inputs["conv_w"], f)[:depth]
    pp[:, :, 0:40] = cw.reshape(depth, 5, 8, 128).transpose(0, 3, 2, 1).reshape(depth, 128, 40)
    pp[:, :, 40:48] = np.asarray(inputs["conv_b"], f)[:depth].reshape(depth, 8, 128).transpose(0, 2, 1)
    mu = np.asarray(inputs["shift_mu"], f)[:depth]
    pp[:, :, 48:63] = mu[:, 0].reshape(depth, 15, 128).transpose(0, 2, 1)
    pp[:, :, 63:78] = mu[:, 1].reshape(depth, 15, 128).transpose(0, 2, 1)
    pp[:, :, 78:86] = np.asarray(inputs["w0"], f)[:depth].reshape(depth, 8, 128).transpose(0, 2, 1)
    pp[:, :, 86:94] = np.asarray(inputs["a0"], f)[:depth].reshape(depth, 8, 128).transpose(0, 2, 1)
    for off, nm in ((94, "k_k"), (98, "k_a"), (102, "r_k"), (106, "gn_w"), (110, "gn_b")):
        pp[:, :, off:off + 4] = np.asarray(inputs[nm], f)[:depth].reshape(depth, 4, 128).transpose(0, 2, 1)
    sh["pp"] = pp
    bb = np.zeros((depth, 1056), f)
    bb[:, 0:16] = np.asarray(inputs["dt_bias"], f)[:depth].reshape(depth, 16)
    bb[:, 16:32] = np.asarray(inputs["A_log"], f)[:depth].reshape(depth, 16)
    bb[:, 32:544] = np.repeat(np.asarray(inputs["d_skip"], f)[:depth], 64, axis=1)
    bb[:, 544:1056] = np.asarray(inputs["ssm_norm_w"], f)[:depth]
    sh["bb"] = bb
    sh["wup"] = np.ascontiguousarray(np.asarray(inputs["w_up"], f)[:depth].reshape(depth, 128, 512))
    sh["aup"] = np.ascontiguousarray(np.asarray(inputs["a_up"], f)[:depth].reshape(depth, 128, 512))
    sh["gup"] = np.ascontiguousarray(np.asarray(inputs["g_up"], f)[:depth])
    return sh


def _core_tokens(x2seq, c2, flagval):
    f = np.float32
    NT = x2seq.shape[0]
    m = {}
    m["xT"] = np.ascontiguousarray(x2seq.T.reshape(KD, 128, NT)).astype(f)
    m["cT"] = np.ascontiguousarray(c2.reshape(2, KD, 128).transpose(2, 1, 0)).astype(f)
    m["flag"] = np.full((128, 1), flagval, f)
    return m


_NC_CACHE = {}


def kernel(**inputs):
    depth = 4
    NT = 4096
    key = (NT, depth)
    if key not in _NC_CACHE:
        _NC_CACHE[key] = build(NT, depth)
    nc = _NC_CACHE[key]
    shared = _prep_core_inputs(inputs, depth)
    xp = np.asarray(inputs["x_prompt"], np.float32)
    xs = np.asarray(inputs["x_sample"], np.float32)
    cp = np.asarray(inputs["c_prompt"], np.float32)
    cs = np.asarray(inputs["c_sample"], np.float32)
    in_maps = []
    for core in range(8):
        if core < 4:
            x2 = np.concatenate([xp[2 * core], xp[2 * core + 1]], axis=0)
            c2 = np.stack([cp[2 * core], cp[2 * core + 1]])
            m = _core_tokens(x2, c2, 0.0)
        else:
            j = core - 4
            m = _core_tokens(xs[j], np.stack([cs[j], cs[j]]), 1.0)
        m.update(shared)
        in_maps.append(m)
    res = run_bass_kernel_spmd(nc, in_maps, core_ids=list(range(8)))
    yp = np.zeros_like(xp)
    ys = np.zeros_like(xs)
    for core in range(8):
        y = res.results[core]["yT"].reshape(D, NT).T
        if core < 4:
            yp[2 * core] = y[:2048]
            yp[2 * core + 1] = y[2048:]
        else:
            ys[core - 4] = y
    return (yp, ys)
```
